# Optimizing a Trainium2 kernel written in Bass

```python
import jax, jax.numpy as jnp
from jax import lax
import numpy as np

D_MODEL = 1024
BATCH = 8
SEQ = 2048
DEPTH = 2
DEC_BATCH = 128
DEC_SEQ = 4
PAST_LEN = 16384
PAGE_SIZE = 128

N_META = 16
CHUNK = 64
BR_WIDTH = 512
N_BRANCH = 4
GLA_HEADS = 4
GLA_DK = 64
GLA_DV = BR_WIDTH // GLA_HEADS
GLA_LOWRANK = 16
GLA_GATE_NORM = 16.0
HG_HEADS = 4
HG_EXPAND = BR_WIDTH // HG_HEADS
HG_DV = BR_WIDTH // HG_HEADS
ML_HEADS = 4
ML_DH = BR_WIDTH // ML_HEADS
CONV_WIDTH = 31
CONV_CH = BR_WIDTH
EPS = 1e-6
LB_FLOOR = 1e-30

IN_SPLITS = (
    GLA_HEADS * GLA_DK, GLA_HEADS * GLA_DK, BR_WIDTH, GLA_LOWRANK, BR_WIDTH,
    HG_HEADS * HG_EXPAND, HG_HEADS * HG_EXPAND, HG_HEADS * HG_DV, BR_WIDTH,
    BR_WIDTH, BR_WIDTH, BR_WIDTH, ML_HEADS, ML_HEADS, BR_WIDTH, BR_WIDTH,
    CONV_CH, CONV_CH, BR_WIDTH,
    N_BRANCH * D_MODEL,
)
D_IN = sum(IN_SPLITS)

kernel_name = "hybrid_gla_hgrn2_mlstm_conformer_step"


def split_cols(z):
    idx = np.cumsum(np.array(IN_SPLITS))[:-1].tolist()
    return jnp.split(z, idx, axis=-1)


def rmsnorm(x, w):
    x32 = x.astype(jnp.float32)
    y = x32 * lax.rsqrt(jnp.mean(x32 * x32, axis=-1, keepdims=True) + EPS)
    return y.astype(x.dtype) * w


def heads(x, n):
    return x.reshape(x.shape[:-1] + (n, x.shape[-1] // n))


def head_rmsnorm(x, w):
    y = x * lax.rsqrt(jnp.mean(x * x, axis=-1, keepdims=True) + EPS) * w.astype(jnp.float32)
    return y.reshape(x.shape[:-2] + (-1,))


def head_layernorm(x, w):
    xc = x - jnp.mean(x, axis=-1, keepdims=True)
    y = xc * lax.rsqrt(jnp.mean(xc * xc, axis=-1, keepdims=True) + EPS)
    return y.reshape(x.shape[:-2] + (-1,)) * w.astype(jnp.float32)


def causal_mask(L):
    return jnp.tril(jnp.ones((L, L), dtype=bool))


def gla_chunk(S, inp):
    q, k, v, g = inp
    L = q.shape[1]
    b = jnp.cumsum(g, axis=1)
    diff = b[:, :, None] - b[:, None, :]
    decay = jnp.where(causal_mask(L)[None, :, :, None, None], jnp.exp(jnp.minimum(diff, 0.0)), 0.0)
    att = jnp.einsum('bthd,bshd,btshd->bths', q, k, decay)
    o = jnp.einsum('bths,bshv->bthv', att, v) + jnp.einsum('bthd,bhdv->bthv', q * jnp.exp(b), S)
    b_end = b[:, -1]
    S_new = jnp.exp(b_end)[..., None] * S + jnp.einsum('bshd,bshv->bhdv', k * jnp.exp(b_end[:, None] - b), v)
    return S_new, o


def mlstm_chunk(state, inp):
    C, n, m = state
    q, k, v, ig, lf = inp
    L = q.shape[1]
    b = jnp.cumsum(lf, axis=1)
    m_t = jnp.maximum(b + m[:, None], lax.cummax(ig - b, axis=1) + b)
    diff = b[:, :, None] - b[:, None, :] + ig[:, None, :] - m_t[:, :, None]
    dmat = jnp.where(causal_mask(L)[None, :, :, None], jnp.exp(jnp.minimum(diff, 0.0)), 0.0)
    inter = jnp.exp(b + m[:, None] - m_t)
    s = jnp.einsum('bthd,bshd->btsh', q, k) * dmat
    num = jnp.einsum('btsh,bshv->bthv', s, v) + inter[..., None] * jnp.einsum('bthd,bhdv->bthv', q, C)
    den = jnp.sum(s, axis=2) + inter * jnp.einsum('bthd,bhd->bth', q, n)
    h = num / jnp.maximum(jnp.abs(den), jnp.exp(-m_t))[..., None]
    m_end = m_t[:, -1]
    w_end = jnp.exp(b[:, -1:] - b + ig - m_end[:, None])
    carry = jnp.exp(b[:, -1] + m - m_end)
    C_new = carry[..., None, None] * C + jnp.einsum('bsh,bshd,bshv->bhdv', w_end, k, v)
    n_new = carry[..., None] * n + jnp.einsum('bsh,bshd->bhd', w_end, k)
    return (C_new, n_new, m_end), h


def run_chunked(step, state, xs, lead, chunk):
    outs = []
    if lead > 0:
        state, o = step(state, tuple(a[:, :lead] for a in xs))
        outs.append(o)
        xs = tuple(a[:, lead:] for a in xs)
    T = xs[0].shape[1]
    nc = T // chunk
    xc = tuple(jnp.moveaxis(a.reshape((a.shape[0], nc, chunk) + a.shape[2:]), 1, 0) for a in xs)
    state, oc = lax.scan(step, state, xc)
    outs.append(jnp.moveaxis(oc, 0, 1).reshape((oc.shape[1], nc * chunk) + oc.shape[3:]))
    return state, jnp.concatenate(outs, axis=1)


def zero_states(n, dtype):
    f = jnp.float32
    return (jnp.zeros((n, GLA_HEADS, GLA_DK, GLA_DV), f),
            jnp.zeros((n, HG_HEADS, HG_EXPAND, HG_DV), f),
            jnp.zeros((n, ML_HEADS, ML_DH, ML_DH), f),
            jnp.zeros((n, ML_HEADS, ML_DH), f),
            jnp.zeros((n, ML_HEADS), f),
            jnp.zeros((n, CONV_WIDTH - 1, CONV_CH), dtype))


def layer(h, st, lead, chunk, lb, norm_w, w_in, gla_wgk2, gla_bgk, gla_norm, hg_norm,
          ml_bi, ml_bf, ml_norm, conv_w, conv_b, conv_ln_g, conv_ln_b, w_branch, w_out):
    f32 = jnp.float32
    gla_S, hg_S, ml_C, ml_n, ml_m, conv_buf = st
    Bsz, T, _ = h.shape
    xn = rmsnorm(h, norm_w)
    (gq, gk, gv, glr, gg, hq, hf, hi, hg_g, mq, mk, mv, mi, mf, mo, mg,
     ca, cb, cg, mgate) = split_cols(xn @ w_in)

    g_log = jax.nn.log_sigmoid(glr.astype(f32) @ gla_wgk2.astype(f32) + gla_bgk.astype(f32)) / GLA_GATE_NORM
    xs = (heads(gq.astype(f32), GLA_HEADS) * GLA_DK ** -0.5, heads(gk.astype(f32), GLA_HEADS),
          heads(gv.astype(f32), GLA_HEADS), heads(g_log, GLA_HEADS))
    gla_S, oA = run_chunked(gla_chunk, gla_S.astype(f32), xs, lead, chunk)
    zA = head_rmsnorm(oA, gla_norm) * jax.nn.silu(gg.astype(f32))

    log_lb = jnp.log(jnp.maximum(lb, LB_FLOOR))
    log_f = jnp.logaddexp(log_lb, jnp.log1p(-lb) + jax.nn.log_sigmoid(hf.astype(f32)))
    xs = (heads(jax.nn.silu(hq.astype(f32)), HG_HEADS), heads(-jnp.expm1(log_f), HG_HEADS),
          heads(hi.astype(f32), HG_HEADS), heads(log_f, HG_HEADS))
    hg_S, oB = run_chunked(gla_chunk, hg_S.astype(f32), xs, lead, chunk)
    zB = head_rmsnorm(oB, hg_norm) * jax.nn.silu(hg_g.astype(f32))

    ig = mi.astype(f32) + ml_bi.astype(f32)
    lf = jax.nn.log_sigmoid(mf.astype(f32) + ml_bf.astype(f32))
    xs = (heads(mq.astype(f32), ML_HEADS), heads(mk.astype(f32), ML_HEADS) * ML_DH ** -0.5,
          heads(mv.astype(f32), ML_HEADS), ig, lf)
    (ml_C, ml_n, ml_m), hC = run_chunked(
        mlstm_chunk, (ml_C.astype(f32), ml_n.astype(f32), ml_m.astype(f32)), xs, lead, chunk)
    hC = hC * jax.nn.sigmoid(heads(mo.astype(f32), ML_HEADS))
    zC = head_layernorm(hC, ml_norm) * jax.nn.silu(mg.astype(f32))

    u = ca * jax.nn.sigmoid(cb)
    full = jnp.concatenate([conv_buf.astype(u.dtype), u], axis=1)
    yc = lax.conv_general_dilated(full, conv_w[:, None, :].astype(u.dtype), (1,), 'VALID',
                                  dimension_numbers=('NWC', 'WIO', 'NWC'),
                                  feature_group_count=CONV_CH) + conv_b
    conv_buf = full[:, -(CONV_WIDTH - 1):]
    yc32 = yc.astype(f32)
    yc32 = yc32 - jnp.mean(yc32, axis=-1, keepdims=True)
    yc32 = yc32 * lax.rsqrt(jnp.mean(yc32 * yc32, axis=-1, keepdims=True) + EPS)
    yc32 = yc32 * conv_ln_g.astype(f32) + conv_ln_b.astype(f32)
    zD = jax.nn.silu(yc32) * jax.nn.silu(cg.astype(f32))

    gates = jax.nn.sigmoid(mgate.astype(f32)).reshape(Bsz, T, N_BRANCH, D_MODEL).astype(h.dtype)
    branches = (zA, zB, zC, zD)
    mixed = gates[:, :, 0] * (branches[0].astype(h.dtype) @ w_branch[0])
    for i in range(1, N_BRANCH):
        mixed = mixed + gates[:, :, i] * (branches[i].astype(h.dtype) @ w_branch[i])
    h = h + mixed @ w_out
    return h, (gla_S, hg_S, ml_C, ml_n, ml_m, conv_buf)


def setup_inputs(seed: int = 0) -> dict:
    key = jax.random.key(seed)
    ks = jax.random.split(key, 26)

    def nrm(k, shape, s):
        return jax.random.normal(k, shape, jnp.float32) * s

    return {
        "x_prompt": nrm(ks[0], (BATCH, SEQ, D_MODEL), 1.0),
        "x_sample": nrm(ks[1], (DEC_BATCH, DEC_SEQ, D_MODEL), 1.0),
        "state_gla": nrm(ks[2], (DEPTH, DEC_BATCH, GLA_HEADS, GLA_DK, GLA_DV), 0.5),
        "state_hgrn": nrm(ks[3], (DEPTH, DEC_BATCH, HG_HEADS, HG_EXPAND, HG_DV), 0.5),
        "state_mlstm_C": nrm(ks[4], (DEPTH, DEC_BATCH, ML_HEADS, ML_DH, ML_DH), 0.3),
        "state_mlstm_n": nrm(ks[5], (DEPTH, DEC_BATCH, ML_HEADS, ML_DH), 0.3),
        "state_mlstm_m": 2.0 + nrm(ks[6], (DEPTH, DEC_BATCH, ML_HEADS), 1.0),
        "state_conv": nrm(ks[7], (DEPTH, DEC_BATCH, CONV_WIDTH - 1, CONV_CH), 0.5),
        "meta_tokens": nrm(ks[8], (N_META, D_MODEL), 1.0),
        "norm_w": 1.0 + nrm(ks[9], (DEPTH, D_MODEL), 0.02),
        "w_in": nrm(ks[10], (DEPTH, D_MODEL, D_IN), D_MODEL ** -0.5),
        "gla_wgk2": nrm(ks[11], (DEPTH, GLA_LOWRANK, GLA_HEADS * GLA_DK), GLA_LOWRANK ** -0.5),
        "gla_bgk": nrm(ks[12], (DEPTH, GLA_HEADS * GLA_DK), 0.1),
        "gla_norm": 1.0 + nrm(ks[13], (DEPTH, GLA_DV), 0.02),
        "hg_lb_logits": nrm(ks[14], (DEPTH, HG_HEADS * HG_EXPAND), 0.5),
        "hg_norm": 1.0 + nrm(ks[15], (DEPTH, HG_DV), 0.02),
        "ml_bi": nrm(ks[16], (DEPTH, ML_HEADS), 0.1),
        "ml_bf": jnp.linspace(3.0, 6.0, ML_HEADS, dtype=jnp.float32)[None] + nrm(ks[17], (DEPTH, ML_HEADS), 0.1),
        "ml_norm": 1.0 + nrm(ks[18], (DEPTH, BR_WIDTH), 0.02),
        "conv_w": nrm(ks[19], (DEPTH, CONV_WIDTH, CONV_CH), CONV_WIDTH ** -0.5),
        "conv_b": nrm(ks[20], (DEPTH, CONV_CH), 0.02),
        "conv_ln_g": 1.0 + nrm(ks[21], (DEPTH, CONV_CH), 0.02),
        "conv_ln_b": nrm(ks[22], (DEPTH, CONV_CH), 0.02),
        "w_branch": nrm(ks[23], (DEPTH, N_BRANCH, BR_WIDTH, D_MODEL), BR_WIDTH ** -0.5),
        "w_out": nrm(ks[24], (DEPTH, D_MODEL, D_MODEL), D_MODEL ** -0.5),
        "final_norm": 1.0 + nrm(ks[25], (D_MODEL,), 0.02),
    }


def reference(x_prompt, x_sample, state_gla, state_hgrn, state_mlstm_C, state_mlstm_n, state_mlstm_m,
              state_conv, meta_tokens, norm_w, w_in, gla_wgk2, gla_bgk, gla_norm, hg_lb_logits, hg_norm,
              ml_bi, ml_bf, ml_norm, conv_w, conv_b, conv_ln_g, conv_ln_b, w_branch, w_out, final_norm):
    lb_sm = jax.nn.softmax(hg_lb_logits.astype(jnp.float32), axis=0)
    lb = jnp.cumsum(lb_sm, axis=0) - lb_sm[0]

    Bp = x_prompt.shape[0]
    meta = jnp.broadcast_to(meta_tokens.astype(x_prompt.dtype)[None], (Bp, N_META, D_MODEL))
    hp = jnp.concatenate([meta, x_prompt], axis=1)
    hs = x_sample
    dec_chunk = x_sample.shape[1]
    p_st, s_st = [], []
    for l in range(DEPTH):
        params = (lb[l], norm_w[l], w_in[l], gla_wgk2[l], gla_bgk[l], gla_norm[l], hg_norm[l],
                  ml_bi[l], ml_bf[l], ml_norm[l], conv_w[l], conv_b[l], conv_ln_g[l], conv_ln_b[l],
                  w_branch[l], w_out[l])
        hp, ps = layer(hp, zero_states(Bp, x_prompt.dtype), N_META, CHUNK, *params)
        s0 = (state_gla[l], state_hgrn[l], state_mlstm_C[l], state_mlstm_n[l], state_mlstm_m[l], state_conv[l])
        hs, ss = layer(hs, s0, 0, dec_chunk, *params)
        p_st.append(ps)
        s_st.append(ss)

    y_prompt = rmsnorm(hp[:, N_META:], final_norm)
    y_sample = rmsnorm(hs, final_norm)

    def stk(lst, i):
        return jnp.stack([s[i] for s in lst])

    return (y_prompt, y_sample,
            stk(p_st, 0), stk(p_st, 1), stk(p_st, 2), stk(p_st, 3), stk(p_st, 4), stk(p_st, 5),
            stk(s_st, 0), stk(s_st, 1), stk(s_st, 2), stk(s_st, 3), stk(s_st, 4), stk(s_st, 5))
```

```python
import numpy as np
from contextlib import ExitStack
import concourse.bass as bass
import concourse.mybir as mybir
from concourse.bass_utils import run_bass_kernel_spmd

F32 = mybir.dt.float32
BF16 = mybir.dt.bfloat16
AF = mybir.ActivationFunctionType
ALU = mybir.AluOpType
AX = mybir.AxisListType

D = 1024
DEPTH = 2
EPS = 1e-6
D_IN = 11800
C_GQ, C_GK, C_GV, C_GLR, C_GG = 0, 256, 512, 1024, 1040
C_HQ, C_HF, C_HI, C_HG = 1552, 2064, 2576, 3088
C_MQ, C_MK, C_MV, C_MI, C_MF, C_MO, C_MG = 3600, 4112, 4624, 5136, 5140, 5144, 5656
C_CA, C_CB, C_CG = 6168, 6680, 7192
C_GATE = 7704
NSEQ = 16
KD = 16

P_NORMW = 0
P_FNORM = 8
P_NBGK = 16
P_GNORM = 20
P_HNORM = 21
P_LBL = 22
P_BI = 30
P_NBF = 31
P_MLN = 32
P_CW = 36
P_CB = 160
P_CG = 164
P_CBT = 168
P_WGK = 172
PW = 428

K_ID = 0
K_R = 128
K_OH = 512
K_SEL = 528
K_ONE = 1040
CWS = 1041
K_MASK = 1041
CW = 1425


class Res:
    __slots__ = ("w", "r")

    def __init__(self):
        self.w = None
        self.r = {}


class Tl:
    def __init__(self, t, nparts=1):
        self.t = t
        self.parts = [Res() for _ in range(nparts)]

    def __getitem__(self, idx):
        return self.t[idx]

    def p(self, i):
        return self.parts[i]


def _expand(lst):
    out = []
    for x in lst:
        if isinstance(x, Tl):
            out.extend(x.parts)
        elif isinstance(x, (list, tuple)):
            out.extend(_expand(x))
        else:
            out.append(x)
    return out


class KB:
    def __init__(self, nc, es):
        self.nc = nc
        self.eng = {"pe": nc.tensor, "act": nc.scalar, "dve": nc.vector, "pool": nc.gpsimd, "sp": nc.sync}
        self.sem = {}
        for e in ("pe", "act", "dve", "pool"):
            self.sem[e] = es.enter_context(nc.semaphore("s_" + e))
        for q in ("sp", "pool"):
            for i in range(KD):
                self.sem[("d", q, i)] = es.enter_context(nc.semaphore("d%s%d" % (q, i)))
        self.cnt = {e: 0 for e in ("pe", "act", "dve", "pool")}
        self.seen = {e: {} for e in self.eng}
        self.rec = None
        self.slack = 0.0
        self.rbias = 0.0
        self.dma_i = 0
        self.dma_qi = {"sp": 0, "pool": 0}
        self.dma_val = {(q, i): 0 for q in ("sp", "pool") for i in range(KD)}
        self.nins = 0

    def _waits(self, eng, reads, writes):
        need = {}

        def add(ev):
            if ev is None:
                return
            k, v = ev
            if need.get(k, 0) < v:
                need[k] = v

        for r in reads:
            add(r.w)
        for w in writes:
            if not (eng == "pe" and w.w is not None and w.w[0] == "pe"):
                add(w.w)
            for k, v in w.r.items():
                add((k, v))
        e = self.eng[eng]
        seen = self.seen[eng]
        for k, v in need.items():
            if seen.get(k, 0) >= v:
                continue
            e.wait_ge(self.sem[k], v)
            seen[k] = v

    def op(self, eng, fn, reads=(), writes=()):
        if self.rec is not None:
            self.rec.append((0, eng, fn, reads, writes))
            return None
        reads = _expand(reads)
        writes = _expand(writes)
        self._waits(eng, reads, writes)
        ins = fn(self.eng[eng])
        self.cnt[eng] += 1
        ev = (eng, self.cnt[eng])
        ins.then_inc(self.sem[eng], 1)
        self.nins += 1
        for r in reads:
            if r.r.get(eng, 0) < ev[1]:
                r.r[eng] = ev[1]
        for w in writes:
            w.w = ev
            w.r = {}
        return ins

    def dma(self, q, fn, reads=(), writes=()):
        if self.rec is not None:
            self.rec.append((1, q, fn, reads, writes))
            return None
        reads = _expand(reads)
        writes = _expand(writes)
        self._waits(q, reads, writes)
        s = (q, self.dma_qi[q] % KD)
        self.dma_qi[q] += 1
        self.dma_i += 1
        k = ("d",) + s
        e = self.eng[q]
        if self.dma_val[s] > 0 and self.seen[q].get(k, 0) < self.dma_val[s]:
            e.wait_ge(self.sem[k], self.dma_val[s])
            self.seen[q][k] = self.dma_val[s]
        ins = fn(e)
        self.dma_val[s] += 16
        ev = (k, self.dma_val[s])
        ins.then_inc(self.sem[k], 16)
        self.nins += 1
        for r in reads:
            if r.r.get(k, 0) < ev[1]:
                r.r[k] = ev[1]
        for w in writes:
            w.w = ev
            w.r = {}
        return ins

    COST = {"pe": 0.13, "act": 0.6, "dve": 0.6, "pool": 1.0, "sp": 0.0}

    def emit(self, ops):
        assert self.rec is None
        for k, eng, fn, r, w in ops:
            (self.dma if k else self.op)(eng, fn, r, w)

    def emit2(self, a, b):
        ca = sum(self.COST[o[1]] for o in a) + 1e-9
        cb = sum(self.COST[o[1]] for o in b) + 1e-9
        i = j = 0
        xa = xb = 0.0
        while i < len(a) or j < len(b):
            if j >= len(b) or (i < len(a) and xa / ca <= xb / cb):
                o = a[i]; i += 1; xa += self.COST[o[1]]
            else:
                o = b[j]; j += 1; xb += self.COST[o[1]]
            (self.dma if o[0] else self.op)(o[1], o[2], o[3], o[4])

    def pipeline(self, Ps, Rs, Mt=None):
        self.rec = None
        n = len(Ps)
        pos = {"P": [0, 0], "R": [0, 0]}
        segs = {"P": Ps, "R": Rs}
        Mt = Mt or []
        mflat = [(b, o) for (b, ops) in Mt for o in ops]
        mi = [0]

        def head_m():
            if mi[0] >= len(mflat):
                return None
            head("P")
            head("R")
            b, o = mflat[mi[0]]
            if pos["P"][0] < n or pos["R"][0] <= b:
                return None
            return o

        if not hasattr(self, "efree"):
            self.efree = {}
            self.rt = {}

        def head(k):
            sg, ix = pos[k]
            while sg < n and ix >= len(segs[k][sg]):
                sg += 1
                ix = 0
            pos[k] = [sg, ix]
            if sg >= n:
                return None
            if k == "R" and pos["P"][0] <= sg:
                return None
            if k == "P" and sg >= 2 and pos["R"][0] <= sg - 2:
                return None
            return segs[k][sg][ix]

        def est(o):
            kind, eng, fn, r, w = o
            rr, ww = _expand(r), _expand(w)
            t = 0.0
            for x in rr:
                t = max(t, self.rt.get(id(x), (0.0, 0.0))[0])
            for x in ww:
                a = self.rt.get(id(x), (0.0, 0.0))
                t = max(t, a[0], a[1])
            return max(self.efree.get(eng, 0.0), t + self.slack), rr, ww

        while True:
            hp, hr, hm = head("P"), head("R"), head_m()
            if hp is None and hr is None and hm is None:
                if pos["P"][0] >= n and pos["R"][0] >= n and mi[0] >= len(mflat):
                    break
                raise RuntimeError("pipeline deadlock")
            cand = []
            if hr is not None:
                cand.append(("R", hr) + est(hr))
            if hp is not None:
                cand.append(("P", hp) + est(hp))
            if hm is not None:
                cand.append(("M", hm) + est(hm))
            cand.sort(key=lambda c: c[2] - (self.rbias if c[0] == "R" else 0.0))
            k, o, st, rr, ww = cand[0]
            if k == "M":
                mi[0] += 1
            else:
                pos[k][1] += 1
            dur = _cost(o)
            kind, eng = o[0], o[1]
            if kind:
                self.efree[eng] = st + 0.05
                end = st + 1.5
            else:
                end = st + dur
                self.efree[eng] = end
            for x in rr:
                a = self.rt.setdefault(id(x), [0.0, 0.0])
                a[1] = max(a[1], end)
            for x in ww:
                self.rt[id(x)] = [end, 0.0]
            (self.dma if kind else self.op)(o[1], o[2], o[3], o[4])

    def finish(self):
        e = self.eng["sp"]
        for s, v in self.dma_val.items():
            if v > 0:
                e.wait_ge(self.sem[("d",) + s], v)
        for en in ("pe", "act", "dve", "pool"):
            if self.cnt[en] > 0:
                e.wait_ge(self.sem[en], self.cnt[en])


class _Dummy:
    def then_inc(self, *a, **k):
        return self


class _Probe:
    def __init__(self):
        self.out = None

    def __getattr__(self, name):
        def f(*a, **k):
            o = k.get("out", a[0] if a else None)
            if o is None:
                o = k.get("ap", None)
            self.out = o
            return _Dummy()
        return f


def _cost(o):
    kind, eng, fn = o[0], o[1], o[2]
    if kind:
        return 0.05
    try:
        c = _LINE_COST.get(fn.__code__.co_firstlineno)
        if c is not None:
            return c
    except Exception:
        pass
    p = _Probe()
    try:
        fn(p)
        shp = list(p.out.shape)
        n = 1
        for d in shp[1:]:
            n *= int(d)
    except Exception:
        n = 256
    if eng == "pe":
        return 0.03 + n * 0.00047
    if eng == "act":
        return 0.22 + n * 0.00095
    if eng == "dve":
        return 0.12 + n * 0.0011
    return 0.3 + n * 0.0022


class Pool:
    def __init__(self, tiles):
        self.tiles = tiles
        self.i = 0

    def get(self):
        t = self.tiles[self.i % len(self.tiles)]
        self.i += 1
        return t


def build(NT):
    NCOL = NT * 128
    NPR = 64 * (NT * 2 - 2) + 16
    SEQ = NPR - 16
    nc = bass.Bass("TRN2", target_bir_lowering=False)
    es = ExitStack()

    def din(name, shape):
        return nc.dram_tensor(name, shape, F32, kind="ExternalInput").ap()

    def dout(name, shape):
        return nc.dram_tensor(name, shape, F32, kind="ExternalOutput").ap()

    hin = din("hin", [NCOL, D])
    w_in = din("w_in", [DEPTH, D, D_IN])
    w_br = din("w_branch", [DEPTH, 4, 512, D])
    w_out = din("w_out", [DEPTH, D, D])
    par = din("par", [DEPTH, 128, PW])
    cst = din("cst", [128, CW])
    fnb = din("fnb", [128, D])
    st_gla = din("st_gla", [DEPTH, NSEQ, 4, 64, 128])
    st_hg = din("st_hg", [DEPTH, NSEQ, 4, 128, 128])
    st_C = din("st_C", [DEPTH, NSEQ, 4, 128, 128])
    st_n = din("st_n", [DEPTH, NSEQ * 4, 128])
    st_m = din("st_m", [DEPTH, NSEQ * 4])
    st_cv = din("st_cv", [DEPTH, NSEQ, 30, 512])

    yp = dout("yp", [SEQ, D])
    ys = dout("ys", [64, D])
    o_pg = dout("p_gla", [DEPTH, 4, 64, 128])
    o_ph = dout("p_hg", [DEPTH, 4, 128, 128])
    o_pC = dout("p_C", [DEPTH, 4, 128, 128])
    o_pn = dout("p_n", [DEPTH, 4, 128])
    o_pm = dout("p_m", [DEPTH, 4])
    o_pcv = dout("p_cv", [DEPTH, 30, 512])
    o_sg = dout("s_gla", [DEPTH, NSEQ, 4, 64, 128])
    o_sh = dout("s_hg", [DEPTH, NSEQ, 4, 128, 128])
    o_sC = dout("s_C", [DEPTH, NSEQ, 4, 128, 128])
    o_sn = dout("s_n", [DEPTH, NSEQ * 4, 128])
    o_sm = dout("s_m", [DEPTH, NSEQ, 4])
    o_scv = dout("s_cv", [DEPTH, NSEQ, 30, 512])
    hscr = nc.dram_tensor("hscr", [NCOL, D], F32, kind="Internal").ap()
    hscr_res = Res()

    kb = KB(nc, es)
    cnt = [0]

    def sb(shape, dt, nparts=1, name=None):
        cnt[0] += 1
        t = es.enter_context(nc.sbuf_tensor("%s%d" % (name or "t", cnt[0]), list(shape), dt))
        return Tl(t, nparts)

    def ps(shape, dt, name=None):
        cnt[0] += 1
        t = es.enter_context(nc.psum_tensor("%s%d" % (name or "p", cnt[0]), list(shape), dt))
        return Tl(t)

    def pool(shape, dt, bufs, nparts=1, name=None):
        return Pool([sb(shape, dt, nparts, name) for _ in range(bufs)])

    xnT = sb([128, 8, NCOL], BF16, NT, "xnT")
    mixT = sb([128, 8, NCOL], BF16, NT, "mixT")
    zT = sb([128, 4, NCOL], BF16, NT, "zT")
    NSLOT = 5
    wslots = pool([128, 8, 512], BF16, NSLOT, name="w")
    wsm = pool([128, 8, 16], BF16, 2, name="wsm")
    cf = sb([128, CWS], F32, name="cf")
    identb = sb([128, 128], BF16)
    maskb = sb([128, 3, 128], BF16)
    onesb = sb([128, 128], BF16)
    onesD = sb([128, 128], BF16)
    onesC = sb([128, 128], BF16)
    pf = [sb([128, P_WGK], F32, name="pf") for _ in range(DEPTH)]
    wgkb = [sb([16, 256], BF16) for _ in range(DEPTH)]
    lbt = sb([128, 4], F32)
    omlt = sb([128, 4], F32)
    zero4 = sb([128, 4], F32)
    epsc = sb([128, 1], F32)

    pjb = [ps([128, 512], F32, "pj") for _ in range(2)]
    pat = ps([128, 512], F32, "at")
    potb = [ps([128, 512], F32, "ot") for _ in range(2)]
    pkv_t = ps([128, 1024], F32, "kv")
    ptr = ps([128, 1024], BF16, "tr")
    pj = Pool(list(pjb))
    pot = Pool(list(potb))

    def view(ap, parts):
        t = Tl(ap, 1)
        t.parts = parts
        return t

    sm1 = pool([128, 4], F32, 4, name="sm")
    NBM = 256
    gA_t = sb([128, 4, NBM], F32, name="gA")
    gB_t = sb([128, 4, NBM], F32, name="gB")
    gAp, gBp = Pool([gA_t]), Pool([gB_t])
    hflat = [view(t[:, :, :].rearrange("p a b -> p (a b)"), t.parts) for t in (gA_t, gB_t)]
    hst = Pool(hflat)
    ar1 = sb([128, 2240], F32, name="ar1")
    r_eb, r_enb = Res(), Res()
    eb_t = view(ar1[:, 0:1024].rearrange("p (a b) -> p a b", b=NBM), [r_eb])
    enb_t = view(ar1[:, 1024:2048].rearrange("p (a b) -> p a b", b=NBM), [r_enb])
    ebp, enbp = Pool([eb_t]), Pool([enb_t])
    stg = Pool([gA_t, gB_t, eb_t, enb_t])
    hst.tiles += [view(ar1[:, 0:1024], [r_eb]), view(ar1[:, 1024:2048], [r_enb])]
    ucs = view(ar1[:, 0:2176].rearrange("p (a s r) -> p a s r", s=NSEQ, r=34), [r_eb, r_enb])
    fnbt = view(ar1[:, 0:1024], [r_eb, r_enb])
    ar2 = sb([128, 2560], F32, name="ar2")
    ar2b = sb([128, 2560], F32, name="ar2b")
    r2 = [Res() for _ in range(5)]
    r2b = [Res() for _ in range(5)]

    def bfv(i, b, ar=None):
        ar = ar2 if ar is None else ar
        return ar[:, 512 * i:512 * (i + 1)].bitcast(BF16).rearrange("p (a b) -> p a b", b=b)

    qTp = Pool([view(bfv(0, NBM), [r2[0]]), view(bfv(0, NBM, ar2b), [r2b[0]])])
    kTp = Pool([view(bfv(1, NBM), [r2[1]]), view(bfv(1, NBM, ar2b), [r2b[1]])])
    khTp = Pool([view(bfv(2, NBM), [r2[2]]), view(bfv(2, NBM, ar2b), [r2b[2]])])
    vSp = Pool([view(bfv(3, 512), [r2[3]]), view(bfv(3, 512, ar2b), [r2b[3]])])
    sgop = Pool([view(bfv(4, NBM), [r2[4]]), view(bfv(4, NBM, ar2b), [r2b[4]])])
    ucb = view(ar2[:, 0:376].rearrange("p (a b) -> p a b", b=94), [r2[0]])
    ycp = Pool([view(ar2[:, 1144:2168].rearrange("p (a b) -> p a b", b=NBM), r2[2:5])])
    sgp = pool([128, 4, NBM], BF16, 2)
    hst.tiles += [view(ar2[:, 0:1024], r2[0:2]), view(ar2[:, 1024:2048], r2[2:4]),
                  view(ar2b[:, 0:1024], r2b[0:2]), view(ar2b[:, 1024:2048], r2b[2:4])]
    dgA = view(ar2b[:, 576:2560].bitcast(BF16).rearrange("p (a b) -> p a b", b=128), r2b[1:5])
    dgB = view(ar1[:, 0:1984].bitcast(BF16).rearrange("p (a b) -> p a b", b=128), [r_eb, r_enb])
    ubf = view(ar2b[:, 0:572].bitcast(BF16).rearrange("p (a b) -> p a b", b=30 + NBM), r2b[0:2])
    ubf2 = view(ar2[:, 384:956].bitcast(BF16).rearrange("p (a b) -> p a b", b=30 + NBM), r2[0:2])
    wcb = sb([128, 4, 31], BF16)
    EBEp = pool([128, 4, 20], F32, 2)
    gtx_t = sb([128, 1024], BF16, name="gtx")
    khp = pool([128, 512], BF16, 2)
    khmp = pool([128, 512], BF16, 1)
    attp = pool([128, 4, 128], BF16, 2)
    sqp = pool([128, 512], BF16, 2)
    f1p = pool([128, 4, 128], F32, 1)
    f2p = pool([128, 4, 128], F32, 1)
    stgb = pool([128, 4, 256], BF16, 1)
    row4 = pool([4, NBM], F32, 6, name="row4")
    ml_tmp = pool([4, NSEQ], F32, 2)
    glrp = pool([16, NBM], BF16, 1)
    ycb = pool([128, 4, NBM], BF16, 1)
    gtp = Pool([view(ycb.tiles[0][:, :, :].rearrange("p a b -> p (a b)"), ycb.tiles[0].parts),
                view(f1p.tiles[0][:, :, :].rearrange("p a b -> p (a b)").bitcast(BF16), f1p.tiles[0].parts),
                view(f2p.tiles[0][:, :, :].rearrange("p a b -> p (a b)").bitcast(BF16), f2p.tiles[0].parts)])
    r_kv = [Res(), Res()]
    kvA = view(pkv_t[:, 0:512], [r_kv[0]])
    kvB = view(pkv_t[:, 512:1024], [r_kv[1]])
    pkv = view(pkv_t[:, :], r_kv)
    pjm = Pool(pjb + [pat] + potb + [kvA, kvB])

    def psum_plan(kind):
        if kind == "rec1":
            pj.tiles = pjb + [kvB]
            pot.tiles = list(potb)
        elif kind == "rec2":
            pj.tiles = list(pjb)
            pot.tiles = list(potb)
        else:
            pj.tiles = pjb + potb + [kvA, kvB]
            pot.tiles = list(potb)
        pj.i = 0
        pot.i = 0

    hbf = Pool(list(gtp.tiles))
    _rg = [Res(), Res()]
    gtx = Pool([view(gtx_t[:, 0:512], [_rg[0]]), view(gtx_t[:, 512:1024], [_rg[1]])])
    tokp = Pool([view(gtx_t[:, :].bitcast(F32), _rg)])
    cvin = tokp
    S_all = sb([128, 4, 256], F32, name="S")
    Sb_all = sb([128, 4, 256], BF16, name="Sb")
    ml_m0T = sb([4, NSEQ], F32)
    ml_em0 = sb([128, NSEQ * 4], F32)
    ml_n0T = sb([128, NSEQ * 4], F32)
    ml_nTo = sb([128, NSEQ * 4], F32)
    ml_emn = sb([128, 4, NSEQ], F32)
    ml_emnp = sb([128, 4], F32)
    ml_mcar = sb([4, 1], F32)
    ml_msE = sb([4, NSEQ], F32)
    ml_npo = sb([128, 4], F32)

    def V(fn, r, w):
        return kb.op("dve", fn, r, w)

    def A(fn, r, w):
        return kb.op("act", fn, r, w)

    def G(fn, r, w):
        return kb.op("pool", fn, r, w)

    def PE(fn, r, w):
        return kb.op("pe", fn, r, w)

    def DMA(q, fn, r, w):
        return kb.dma(q, fn, r, w)

    def ncdma(e, out, in_):
        with nc.allow_non_contiguous_dma(reason="tiny"):
            return e.dma_start(out=out, in_=in_)

    def rsqrt(out_ap, in_ap, np_, r, w, scale=1.0):
        A(lambda e: e.activation(out=out_ap, in_=in_ap, func=AF.Ln, bias=epsc[0:np_, 0:1], scale=scale), list(r) + [epsc], w)
        A(lambda e: e.activation(out=out_ap, in_=out_ap, func=AF.Exp, scale=-0.5), w, w)

    DMA("sp", lambda e: e.dma_start(out=cf[:, :], in_=cst[:, 0:CWS]), [], [cf])
    for l in range(DEPTH):
        DMA("sp", lambda e, l=l: e.dma_start(out=pf[l][:, :], in_=par[l, :, 0:P_WGK]), [], [pf[l]])
    V(lambda e: e.tensor_copy(identb[:, :], cf[:, K_ID:K_ID + 128]), [cf], [identb])
    tm = tokp.get()
    DMA("sp", lambda e: e.dma_start(out=tm[:, 0:384], in_=cst[:, K_MASK:K_MASK + 384]), [], [tm])
    V(lambda e: e.tensor_copy(maskb[:, :, :], tm[:, 0:384].rearrange("p (a b) -> p a b", b=128)), [tm], [maskb])
    V(lambda e: e.memset(onesb[:, :], 1.0), [], [onesb])
    V(lambda e: e.memset(onesD[:, :], 1.0 / 128), [], [onesD])
    V(lambda e: e.memset(onesC[:, :], 1.0 / 512), [], [onesC])
    V(lambda e: e.memset(zero4[:, :], 0.0), [], [zero4])
    V(lambda e: e.memset(epsc[:, :], EPS), [], [epsc])
    for l in range(DEPTH):
        tw = tokp.get()
        DMA("sp", lambda e, l=l, tw=tw: e.dma_start(out=tw[0:16, 0:256], in_=par[l, 0:16, P_WGK:P_WGK + 256]), [], [tw])
        V(lambda e, l=l, tw=tw: e.tensor_copy(wgkb[l][:, :], tw[0:16, 0:256]), [tw], [wgkb[l]])
    one_c = cf[:, K_ONE:K_ONE + 1]

    def sel(h):
        return cf[0:4, K_SEL + h * 128:K_SEL + (h + 1) * 128]

    def tile_chunks(i):
        if i == NT - 1:
            return [(0, 64, "p", 0)] + [(64 + 4 * j, 4, "s", j) for j in range(NSEQ)]
        if i == 0:
            return [(0, 16, "p", 0), (64, 64, "p", 0)]
        return [(0, 64, "p", 0), (64, 64, "p", 0)]

    def mask_of(i):
        return 2 if i == NT - 1 else (0 if i == 0 else 1)

    blocks = []
    i = 0
    while i < NT - 1:
        n = min(2, NT - 1 - i)
        blocks.append(list(range(i, i + n)))
        i += n
    blocks.append([NT - 1])

    mblocks = []
    i = 0
    while i < NT - 1:
        n = min(4, NT - 1 - i)
        mblocks.append(list(range(i, i + n)))
        i += n
    mblocks.append([NT - 1])

    def rpat(blk, nb):
        if blk[0] == NT - 1:
            return cf[:, K_R + 256:K_R + 384]
        return cf[:, K_R:K_R + nb]

    def load_w(l, col0, n, small=False):
        t = (wsm if small else wslots).get()
        src = w_in[l, :, col0:col0 + n].rearrange("(kc p) n -> p kc n", p=128)
        DMA("pool", lambda e: e.dma_start(out=t[:, :, 0:n], in_=src), [], [t])
        return t

    def load_wb(l, i):
        ts = []
        for hh in range(2):
            t = wslots.get()
            src = w_br[l, i, :, hh * 512:(hh + 1) * 512].rearrange("(kc p) n -> p kc n", p=128)
            DMA("pool", lambda e, t=t, src=src: e.dma_start(out=t[:, 0:4, :], in_=src), [], [t])
            ts.append(t)
        return ts

    def load_wo(l):
        ts = []
        for hh in range(2):
            t = wslots.get()
            src = w_out[l, :, hh * 512:(hh + 1) * 512].rearrange("(kc p) n -> p kc n", p=128)
            DMA("pool", lambda e, t=t, src=src: e.dma_start(out=t[:, :, :], in_=src), [], [t])
            ts.append(t)
        return ts

    def xparts(c0, nb):
        return [xnT.p(t) for t in range(c0 // 128, (c0 + nb + 127) // 128)]

    def proj_fm(w, wc0, M, c0, nb, pst):
        for kc in range(8):
            PE(lambda e, kc=kc: e.matmul(pst[0:M, 0:nb], w[:, kc, wc0:wc0 + M], xnT[:, kc, c0:c0 + nb],
                                        start=(kc == 0), stop=(kc == 7)), [w] + xparts(c0, nb), [pst])

    def proj_tm(w, n, ti, pst):
        for kc in range(8):
            PE(lambda e, kc=kc: e.matmul(pst[:, 0:n], xnT[:, kc, ti * 128:(ti + 1) * 128], w[:, kc, 0:n],
                                        start=(kc == 0), stop=(kc == 7)), [w, xnT.p(ti)], [pst])

    def norm_tile(l, ht, ti, sqpool=None):
        sq = (sqpool or hst).get()
        A(lambda e: e.activation(out=sq[:, :], in_=ht[:, :], func=AF.Square), [ht], [sq])
        s1 = sm1.get()
        V(lambda e: e.reduce_sum(out=s1[:, 0:1], in_=sq[:, :], axis=AX.X), [sq], [s1])
        rsqrt(s1[:, 2:3], s1[:, 0:1], 128, [s1], [s1], scale=1.0 / D)
        xb = hbf.get()
        V(lambda e: e.tensor_scalar_mul(out=xb[:, :], in0=ht[:, :], scalar1=s1[:, 2:3]), [ht, s1], [xb])
        for kc in range(8):
            PE(lambda e, kc=kc: e.transpose(ptr[:, kc * 128:(kc + 1) * 128], xb[:, kc * 128:(kc + 1) * 128], identb[:, :]), [xb, identb], [ptr])
        nw = pf[l][:, P_NORMW:P_NORMW + 8].unsqueeze(2).broadcast_to([128, 8, 128])
        V(lambda e: e.tensor_tensor(out=xnT[:, :, ti * 128:(ti + 1) * 128], in0=ptr[:, :].rearrange("p (a b) -> p a b", b=128), in1=nw, op=ALU.mult),
          [ptr, pf[l]], [xnT.p(ti)])
        return s1

    def rec_tile(cfg, l, ti, bi, c0, qT, kT, khT, vS, EBE, ebi0, post):
        dk, nvc = cfg["dk"], cfg["nvc"]
        dvx = 128 * nvc
        S, Sb = cfg["S"], cfg["Sb"]
        pkv = cfg["kv"]
        tc0 = c0 + bi * 128
        chunks = tile_chunks(ti)
        for h in range(4):
            PE(lambda e, h=h: e.transpose(ptr[:, h * dk:(h + 1) * dk], khT[0:dk, h, tc0:tc0 + 128], identb[0:dk, 0:dk]), [khT, identb], [ptr])
        kh = khp.get()
        A(lambda e: e.activation(out=kh[:, 0:4 * dk], in_=ptr[:, 0:4 * dk], func=AF.Copy), [ptr], [kh])
        for h in range(4):
            PE(lambda e, h=h: e.matmul(pat[:, h * 128:(h + 1) * 128], kT[0:dk, h, tc0:tc0 + 128], qT[0:dk, h, tc0:tc0 + 128], start=True, stop=True),
               [kT, qT], [pat])
        attm = attp.get()
        mk = maskb[:, mask_of(ti):mask_of(ti) + 1, :].broadcast_to([128, 4, 128])
        V(lambda e: e.tensor_tensor(out=attm[:, :, :], in0=pat[:, :].rearrange("p (a b) -> p a b", b=128), in1=mk, op=ALU.mult), [pat, maskb], [attm])
        ots = [pot.get() for _ in range(nvc)]
        for vc in range(nvc):
            for h in range(4):
                lhs = vS[:, bi, h * 128:(h + 1) * 128] if vc == 0 else onesb[:, :]
                PE(lambda e, vc=vc, h=h, lhs=lhs: e.matmul(ots[vc][:, h * 128:(h + 1) * 128], lhs, attm[:, h, :], start=(h == 0), stop=False, skip_group_check=True),
                   [vS, onesb, attm], [ots[vc]])
        nch = len(chunks)
        def do_chunk(ci, off, L, kind, j):
            last = ci == nch - 1
            ei = ebi0 + ci
            if kind == "p":
                src_b = Sb
            else:
                st = stg.get()
                cfg["load_state"](l, j, st)
                src_b = cfg["stgb"].get()
                A(lambda e, st=st, src_b=src_b: e.activation(out=src_b[0:dk, :, 0:dvx], in_=st[0:dk, :, 0:dvx], func=AF.Copy), [st], [src_b])
            for vc in range(nvc):
                for h in range(4):
                    PE(lambda e, vc=vc, h=h, src_b=src_b: e.matmul(ots[vc][:, h * 128 + off:h * 128 + off + L], src_b[0:dk, h, vc * 128:(vc + 1) * 128],
                                                                qT[0:dk, h, tc0 + off:tc0 + off + L], start=False, stop=last, skip_group_check=True), [src_b, qT], [ots[vc]])
            if kind == "p":
                r0, r1 = off, off + L
                lk = kh
            else:
                r0, r1 = 64, 128
                lk = cfg["khm"].get()
                V(lambda e, lk=lk, j=j: e.tensor_scalar_mul(out=lk[64:128, 0:4 * dk], in0=kh[64:128, 0:4 * dk], scalar1=cf[64:128, K_OH + j:K_OH + j + 1]),
                  [kh, cf], [lk])
            for h in range(4):
                PE(lambda e, h=h, lk=lk: e.matmul(pkv[0:dk, h * dvx:h * dvx + 128], lk[r0:r1, h * dk:(h + 1) * dk], vS[r0:r1, bi, h * 128:(h + 1) * 128], start=True, stop=True),
                   [lk, vS], [pkv])
                if nvc == 2:
                    PE(lambda e, h=h, lk=lk: e.matmul(pkv[0:dk, h * dvx + 128:h * dvx + 256], lk[r0:r1, h * dk:(h + 1) * dk], onesb[r0:r1, :], start=True, stop=True),
                       [lk, onesb], [pkv])
            ebb = EBE[0:dk, :, ei:ei + 1].broadcast_to([dk, 4, dvx])
            kvv = pkv[0:dk, 0:4 * dvx].rearrange("p (a b) -> p a b", b=dvx)
            if kind == "p":
                for h in range(4):
                    V(lambda e, h=h: e.scalar_tensor_tensor(out=S[0:dk, h, 0:dvx], in0=S[0:dk, h, 0:dvx], scalar=EBE[0:dk, h, ei:ei + 1],
                                                            in1=pkv[0:dk, h * dvx:(h + 1) * dvx], op0=ALU.mult, op1=ALU.add), [S, EBE, pkv], [S])
                A(lambda e: e.activation(out=Sb[0:dk, :, 0:dvx], in_=S[0:dk, :, 0:dvx], func=AF.Copy), [S], [Sb])
            else:
                so = st
                for h in range(4):
                    V(lambda e, h=h, so=so, st=st: e.scalar_tensor_tensor(out=so[0:dk, h, 0:dvx], in0=st[0:dk, h, 0:dvx], scalar=EBE[0:dk, h, ei:ei + 1],
                                                                          in1=pkv[0:dk, h * dvx:(h + 1) * dvx], op0=ALU.mult, op1=ALU.add), [st, EBE, pkv], [so])
                cfg["store_state"](l, j, so)

        for ci, (off, L, kind, j) in enumerate(chunks):
            do_chunk(ci, off, L, kind, j)
        return lambda: post(ots, ti, bi, tc0)

    def ebe_fill(EBE, eb, dk, blk, c0):
        base = []
        idx = 0
        for bi, ti in enumerate(blk):
            base.append(idx)
            tcol = bi * 128
            if ti == NT - 1:
                V(lambda e, idx=idx, tcol=tcol: e.tensor_copy(EBE[0:dk, :, idx:idx + 1], eb[0:dk, :, tcol + 63:tcol + 64]), [eb], [EBE])
                V(lambda e, idx=idx, tcol=tcol: e.tensor_copy(EBE[0:dk, :, idx + 1:idx + 17], eb[0:dk, :, tcol + 67:tcol + 128:4]), [eb], [EBE])
                idx += 17
            else:
                cA = 15 if ti == 0 else 63
                V(lambda e, idx=idx, tcol=tcol, cA=cA: e.tensor_copy(EBE[0:dk, :, idx:idx + 1], eb[0:dk, :, tcol + cA:tcol + cA + 1]), [eb], [EBE])
                V(lambda e, idx=idx, tcol=tcol: e.tensor_copy(EBE[0:dk, :, idx + 1:idx + 2], eb[0:dk, :, tcol + 127:tcol + 128]), [eb], [EBE])
                idx += 2
        return base

    def khat(khT, kT, EBE, base, dk, blk):
        for bi, ti in enumerate(blk):
            tcol = bi * 128
            b0 = base[bi]
            for h in range(4):
                if ti == NT - 1:
                    V(lambda e, h=h, b0=b0, tcol=tcol: e.tensor_scalar_mul(out=khT[0:dk, h, tcol:tcol + 64], in0=kT[0:dk, h, tcol:tcol + 64], scalar1=EBE[0:dk, h, b0:b0 + 1]),
                      [kT, EBE], [khT])
                    V(lambda e, h=h, b0=b0, tcol=tcol: e.tensor_tensor(out=khT[0:dk, h, tcol + 64:tcol + 128].rearrange("p (a b) -> p a b", b=4),
                                                                       in0=kT[0:dk, h, tcol + 64:tcol + 128].rearrange("p (a b) -> p a b", b=4),
                                                                       in1=EBE[0:dk, h, b0 + 1:b0 + 17].unsqueeze(2).broadcast_to([dk, 16, 4]), op=ALU.mult), [kT, EBE], [khT])
                elif h == 0:
                    V(lambda e, b0=b0, tcol=tcol: e.tensor_tensor(out=khT[0:dk, :, tcol:tcol + 128].rearrange("p h (a b) -> p h a b", b=64),
                                                                  in0=kT[0:dk, :, tcol:tcol + 128].rearrange("p h (a b) -> p h a b", b=64),
                                                                  in1=EBE[0:dk, :, b0:b0 + 2].unsqueeze(3).broadcast_to([dk, 4, 2, 64]), op=ALU.mult), [kT, EBE], [khT])

    def v_proj(wv, blk, vS):
        for bi, ti in enumerate(blk):
            pst = pj.get()
            proj_tm(wv, 512, ti, pst)
            A(lambda e, bi=bi, pst=pst: e.activation(out=vS[:, bi, :], in_=pst[:, :], func=AF.Copy), [pst], [vS])

    def act_proj(w, c0, nb, dst, func, scale=1.0):
        for h in range(4):
            pst = pj.get()
            proj_fm(w, h * 128, 128, c0, nb, pst)
            A(lambda e, h=h, pst=pst: e.activation(out=dst[:, h, 0:nb], in_=pst[:, 0:nb], func=func, scale=scale), [pst], [dst])

    def post_rms(cfg, l, sg, normcol):
        def post(ots, ti, bi, tc0):
            ot = ots[0]
            sq = sqp.get()
            A(lambda e: e.activation(out=sq[:, :], in_=ot[:, :], func=AF.Square), [ot], [sq])
            PE(lambda e: e.matmul(pat[:, :], onesD[:, :], sq[:, :], start=True, stop=True), [onesD, sq], [pat])
            rs = f1p.get()
            rsqrt(rs[:, :, :].rearrange("p a b -> p (a b)"), pat[:, :], 128, [pat], [rs])
            t1 = f2p.get()
            V(lambda e: e.tensor_tensor(out=t1[:, :, :].rearrange("p a b -> p (a b)"), in0=ot[:, :], in1=rs[:, :, :].rearrange("p a b -> p (a b)"), op=ALU.mult), [ot, rs], [t1])
            V(lambda e: e.scalar_tensor_tensor(out=zT[:, :, ti * 128:(ti + 1) * 128], in0=t1[:, :, :], scalar=normcol, in1=sg[:, :, tc0:tc0 + 128], op0=ALU.mult, op1=ALU.mult),
              [t1, sg, pf[l]], [zT.p(ti)])
        return post

    def phase_gla(l):
        S, Sb = S_all, Sb_all
        V(lambda e: e.memset(S[:, :, :], 0.0), [], [S])
        V(lambda e: e.memset(Sb[:, :, :], 0.0), [], [Sb])
        wqk = load_w(l, C_GQ, 512)
        wlr = load_w(l, C_GLR, 16, small=True)
        wv = load_w(l, C_GV, 512)
        wg = load_w(l, C_GG, 512)

        def load_state(l, j, st):
            DMA("sp", lambda e: e.dma_start(out=st[0:64, :, 0:128], in_=st_gla[l, j].rearrange("h d v -> d h v")), [], [st])

        def store_state(l, j, so):
            DMA("sp", lambda e: e.dma_start(out=o_sg[l, j].rearrange("h d v -> d h v"), in_=so[0:64, :, 0:128]), [so], [])

        cfg = dict(dk=64, nvc=1, S=S, Sb=Sb, load_state=load_state, store_state=store_state, kv=kvA)
        psum_plan("rec1")
        Ps, Rs = [], []

        def body(blk, par):
            Ps.append([])
            Rs.append([])
            kb.rec = Ps[-1]
            c0 = blk[0] * 128
            nb = len(blk) * 128
            pst = pj.get()
            proj_fm(wlr, 0, 16, c0, nb, pst)
            glr = glrp.get()
            A(lambda e: e.activation(out=glr[0:16, 0:nb], in_=pst[0:16, 0:nb], func=AF.Copy), [pst], [glr])
            gA = gAp.get()
            for h in range(4):
                p2 = pj.get()
                PE(lambda e, h=h, p2=p2: e.matmul(p2[0:64, 0:nb], wgkb[l][0:16, h * 64:(h + 1) * 64], glr[0:16, 0:nb], start=True, stop=True), [wgkb[l], glr], [p2])
                A(lambda e, h=h, p2=p2: e.activation(out=gA[0:64, h, 0:nb], in_=p2[0:64, 0:nb], func=AF.Exp, bias=pf[l][0:64, P_NBGK + h:P_NBGK + h + 1], scale=-1.0), [p2, pf[l]], [gA])
            A(lambda e: e.activation(out=gA[0:64, :, 0:nb], in_=gA[0:64, :, 0:nb], func=AF.Ln, bias=one_c[0:64, :], scale=1.0), [gA, cf], [gA])
            gB = gBp.get()
            for h in range(4):
                V(lambda e, h=h: e.tensor_tensor_scan(out=gB[0:64, h, 0:nb], data0=rpat(blk, nb)[0:64, :], data1=gA[0:64, h, 0:nb], initial=0.0, op0=ALU.mult, op1=ALU.add), [gA, cf], [gB])
            eb = ebp.get()
            enb = enbp.get()
            A(lambda e: e.activation(out=eb[0:64, :, 0:nb], in_=gB[0:64, :, 0:nb], func=AF.Exp, scale=-1.0 / 16), [gB], [eb])
            A(lambda e: e.activation(out=enb[0:64, :, 0:nb], in_=gB[0:64, :, 0:nb], func=AF.Exp, scale=1.0 / 16), [gB], [enb])
            qT, kT, khT = qTp.tiles[par], kTp.tiles[par], khTp.tiles[par]
            for h in range(4):
                pq = pj.get()
                proj_fm(wqk, h * 64, 64, c0, nb, pq)
                V(lambda e, h=h, pq=pq: e.scalar_tensor_tensor(out=qT[0:64, h, 0:nb], in0=pq[0:64, 0:nb], scalar=0.125, in1=eb[0:64, h, 0:nb], op0=ALU.mult, op1=ALU.mult), [pq, eb], [qT])
                pk = pj.get()
                proj_fm(wqk, 256 + h * 64, 64, c0, nb, pk)
                V(lambda e, h=h, pk=pk: e.tensor_tensor(out=kT[0:64, h, 0:nb], in0=pk[0:64, 0:nb], in1=enb[0:64, h, 0:nb], op=ALU.mult), [pk, enb], [kT])
            EBE = EBEp.tiles[par]
            base = ebe_fill(EBE, eb, 64, blk, c0)
            khat(khT, kT, EBE, base, 64, blk)
            vS = vSp.tiles[par]
            v_proj(wv, blk, vS)
            sg = sgp.tiles[par]
            act_proj(wg, c0, nb, sg, AF.Silu)
            post = post_rms(cfg, l, sg, pf[l][:, P_GNORM:P_GNORM + 1])
            kb.rec = Rs[-1]
            op_ = 1 - par
            cfg["stgb"] = Pool([stgb.tiles[0], view(ycb.tiles[0][:, :, :], ycb.tiles[0].parts), qTp.tiles[op_], kTp.tiles[op_], khTp.tiles[op_], sgop.tiles[op_]])
            sgo_ = sgp.tiles[op_]
            cfg["khm"] = Pool([khmp.tiles[0], view(sgo_[:, 0:2, :].rearrange("p a b -> p (a b)"), sgo_.parts), view(sgo_[:, 2:4, :].rearrange("p a b -> p (a b)"), sgo_.parts)])
            pend = None
            for bi, ti in enumerate(blk):
                th = rec_tile(cfg, l, ti, bi, 0, qT, kT, khT, vS, EBE, base[bi], post)
                if cfg["nvc"] == 2:
                    th()
                else:
                    if pend is not None:
                        pend()
                    pend = th
            if pend is not None:
                pend()

        for bidx, blk in enumerate(blocks):
            body(blk, bidx % 2)
        merge_overlapped(l, 0, Ps, Rs)
        DMA("sp", lambda e: e.dma_start(out=o_pg[l].rearrange("h d v -> d h v"), in_=S[0:64, :, 0:128]), [S], [])

    def phase_hgrn(l):
        S, Sb = S_all, Sb_all
        V(lambda e: e.memset(S[:, :, :], 0.0), [], [S])
        V(lambda e: e.memset(Sb[:, :, :], 0.0), [], [Sb])
        wq = load_w(l, C_HQ, 512)
        wf = load_w(l, C_HF, 512)
        wv = load_w(l, C_HI, 512)
        wg = load_w(l, C_HG, 512)
        lb = zero4 if l == 0 else lbt
        oml = omlt

        def load_state(l, j, st):
            DMA("sp", lambda e: e.dma_start(out=st[:, :, 0:128], in_=st_hg[l, j].rearrange("h d v -> d h v")), [], [st])

        def store_state(l, j, so):
            DMA("sp", lambda e: e.dma_start(out=o_sh[l, j].rearrange("h d v -> d h v"), in_=so[:, :, 0:128]), [so], [])

        cfg = dict(dk=128, nvc=1, S=S, Sb=Sb, load_state=load_state, store_state=store_state, kv=kvA)
        psum_plan("rec1")
        Ps, Rs = [], []

        def body(blk, par):
            Ps.append([])
            Rs.append([])
            kb.rec = Ps[-1]
            c0 = blk[0] * 128
            nb = len(blk) * 128
            gA, gB = gAp.get(), gBp.get()
            kT = kTp.tiles[par]
            eb, enb = ebp.get(), enbp.get()
            for h in range(4):
                pst = pj.get()
                proj_fm(wf, h * 128, 128, c0, nb, pst)
                A(lambda e, h=h, pst=pst: e.activation(out=gA[:, h, 0:nb], in_=pst[:, 0:nb], func=AF.Sigmoid), [pst], [gA])
                A(lambda e, h=h, pst=pst: e.activation(out=kT[:, h, 0:nb], in_=pst[:, 0:nb], func=AF.Sigmoid, scale=-1.0), [pst], [kT])
                V(lambda e, h=h: e.tensor_scalar(out=gA[:, h, 0:nb], in0=gA[:, h, 0:nb], scalar1=oml[:, h:h + 1], scalar2=lb[:, h:h + 1], op0=ALU.mult, op1=ALU.add), [gA, oml, lb], [gA])
            A(lambda e: e.activation(out=gA[:, :, 0:nb], in_=gA[:, :, 0:nb], func=AF.Ln), [gA], [gA])
            for h in range(4):
                V(lambda e, h=h: e.tensor_tensor_scan(out=gB[:, h, 0:nb], data0=rpat(blk, nb), data1=gA[:, h, 0:nb], initial=0.0, op0=ALU.mult, op1=ALU.add), [gA, cf], [gB])
            A(lambda e: e.activation(out=eb[:, :, 0:nb], in_=gB[:, :, 0:nb], func=AF.Exp), [gB], [eb])
            A(lambda e: e.activation(out=enb[:, :, 0:nb], in_=gB[:, :, 0:nb], func=AF.Exp, scale=-1.0), [gB], [enb])
            for h in range(4):
                V(lambda e, h=h: e.scalar_tensor_tensor(out=kT[:, h, 0:nb], in0=kT[:, h, 0:nb], scalar=oml[:, h:h + 1], in1=enb[:, h, 0:nb], op0=ALU.mult, op1=ALU.mult), [kT, oml, enb], [kT])
            qT, khT = qTp.tiles[par], khTp.tiles[par]
            for h in range(4):
                pq = pj.get()
                proj_fm(wq, h * 128, 128, c0, nb, pq)
                A(lambda e, h=h, pq=pq: e.activation(out=gA[:, h, 0:nb], in_=pq[:, 0:nb], func=AF.Silu), [pq], [gA])
            V(lambda e: e.tensor_tensor(out=qT[:, :, 0:nb], in0=gA[:, :, 0:nb], in1=eb[:, :, 0:nb], op=ALU.mult), [gA, eb], [qT])
            EBE = EBEp.tiles[par]
            base = ebe_fill(EBE, eb, 128, blk, c0)
            khat(khT, kT, EBE, base, 128, blk)
            vS = vSp.tiles[par]
            v_proj(wv, blk, vS)
            sg = sgp.tiles[par]
            act_proj(wg, c0, nb, sg, AF.Silu)
            post = post_rms(cfg, l, sg, pf[l][:, P_HNORM:P_HNORM + 1])
            kb.rec = Rs[-1]
            op_ = 1 - par
            cfg["stgb"] = Pool([stgb.tiles[0], view(ycb.tiles[0][:, :, :], ycb.tiles[0].parts), qTp.tiles[op_], kTp.tiles[op_], khTp.tiles[op_], sgop.tiles[op_]])
            sgo_ = sgp.tiles[op_]
            cfg["khm"] = Pool([khmp.tiles[0], view(sgo_[:, 0:2, :].rearrange("p a b -> p (a b)"), sgo_.parts), view(sgo_[:, 2:4, :].rearrange("p a b -> p (a b)"), sgo_.parts)])
            pend = None
            for bi, ti in enumerate(blk):
                th = rec_tile(cfg, l, ti, bi, 0, qT, kT, khT, vS, EBE, base[bi], post)
                if cfg["nvc"] == 2:
                    th()
                else:
                    if pend is not None:
                        pend()
                    pend = th
            if pend is not None:
                pend()

        for bidx, blk in enumerate(blocks):
            body(blk, bidx % 2)
        merge_overlapped(l, 1, Ps, Rs)
        DMA("sp", lambda e: e.dma_start(out=o_ph[l].rearrange("h d v -> d h v"), in_=S[:, :, 0:128]), [S], [])

    def phase_mlstm(l):
        S, Sb = S_all, Sb_all
        V(lambda e: e.memset(S[:, :, :], 0.0), [], [S])
        V(lambda e: e.memset(Sb[:, :, :], 0.0), [], [Sb])
        wq = load_w(l, C_MQ, 512)
        wk = load_w(l, C_MK, 512)
        wif = load_w(l, C_MI, 8, small=True)
        wv = load_w(l, C_MV, 512)
        wo = load_w(l, C_MO, 512)
        wg = load_w(l, C_MG, 512)
        m0T = ml_m0T
        DMA("sp", lambda e: ncdma(e, m0T[:, :], st_m[l].rearrange("(s h) -> h s", h=4)), [], [m0T])
        em0 = ml_em0
        DMA("sp", lambda e: e.dma_start(out=em0[:, :], in_=st_m[l].partition_broadcast(128)), [], [em0])
        A(lambda e: e.activation(out=em0[:, :], in_=em0[:, :], func=AF.Exp), [em0], [em0])
        n0 = tokp.get()
        DMA("sp", lambda e: e.dma_start(out=n0[0:64, 0:128], in_=st_n[l]), [], [n0])
        PE(lambda e: e.transpose(pat[:, 0:64], n0[0:64, 0:128], cf[0:64, K_ID:K_ID + 64]), [n0, cf], [pat])
        n0T = ml_n0T
        V(lambda e: e.tensor_tensor(out=n0T[:, :], in0=pat[:, 0:64], in1=em0[:, :], op=ALU.mult), [pat, em0], [n0T])
        nTo = ml_nTo
        emn = ml_emn
        emnp = ml_emnp
        mcar = ml_mcar
        V(lambda e: e.memset(mcar[:, :], 0.0), [], [mcar])
        msE = ml_msE
        Cout = {}

        def load_state(l, j, st):
            DMA("sp", lambda e: e.dma_start(out=st[:, :, 0:128], in_=st_C[l, j].rearrange("h d v -> d h v")), [], [st])
            V(lambda e: e.tensor_tensor(out=st[:, :, 0:128], in0=st[:, :, 0:128], in1=em0[:, 4 * j:4 * j + 4].unsqueeze(2).broadcast_to([128, 4, 128]), op=ALU.mult), [st, em0], [st])
            V(lambda e: e.tensor_copy(st[:, :, 128:256], n0T[:, 4 * j:4 * j + 4].unsqueeze(2).broadcast_to([128, 4, 128])), [n0T], [st])

        def store_state(l, j, so):
            V(lambda e: e.tensor_tensor(out=so[:, :, 0:128], in0=so[:, :, 0:128], in1=emn[:, :, j:j + 1].broadcast_to([128, 4, 128]), op=ALU.mult), [so, emn], [so])
            V(lambda e: e.tensor_tensor(out=nTo[:, 4 * j:4 * j + 4], in0=so[:, :, 128], in1=emn[:, :, j], op=ALU.mult), [so, emn], [nTo])
            DMA("sp", lambda e: e.dma_start(out=o_sC[l, j].rearrange("h d v -> d h v"), in_=so[:, :, 0:128]), [so], [])

        cfg = dict(dk=128, nvc=2, S=S, Sb=Sb, load_state=load_state, store_state=store_state, kv=pkv)
        psum_plan("rec2")
        Ps, Rs = [], []

        def body(blk, par):
            Ps.append([])
            Rs.append([])
            kb.rec = Ps[-1]
            c0 = blk[0] * 128
            nb = len(blk) * 128
            islast = blk[0] == NT - 1
            IG, LFn, LF, BP, A2, MR = [row4.get() for _ in range(6)]
            pst = pj.get()
            proj_fm(wif, 0, 4, c0, nb, pst)
            A(lambda e: e.activation(out=IG[0:4, 0:nb], in_=pst[0:4, 0:nb], func=AF.Identity, bias=pf[l][0:4, P_BI:P_BI + 1], scale=1.0), [pst, pf[l]], [IG])
            pst2 = pj.get()
            proj_fm(wif, 4, 4, c0, nb, pst2)
            A(lambda e: e.activation(out=LFn[0:4, 0:nb], in_=pst2[0:4, 0:nb], func=AF.Exp, bias=pf[l][0:4, P_NBF:P_NBF + 1], scale=-1.0), [pst2, pf[l]], [LFn])
            A(lambda e: e.activation(out=LFn[0:4, 0:nb], in_=LFn[0:4, 0:nb], func=AF.Ln, bias=one_c[0:4, :], scale=1.0), [LFn, cf], [LFn])
            V(lambda e: e.tensor_scalar(out=LF[0:4, 0:nb], in0=LFn[0:4, 0:nb], scalar1=-1.0, scalar2=None, op0=ALU.mult), [LFn], [LF])
            V(lambda e: e.tensor_tensor_scan(out=BP[0:4, 0:nb], data0=rpat(blk, nb)[0:4, :], data1=LFn[0:4, 0:nb], initial=0.0, op0=ALU.mult, op1=ALU.add), [LFn, cf], [BP])
            V(lambda e: e.tensor_tensor(out=A2[0:4, 0:nb], in0=IG[0:4, 0:nb], in1=BP[0:4, 0:nb], op=ALU.add), [IG, BP], [A2])
            segs = []
            if blk[0] == 0:
                segs = [(0, 16), (64, nb)]
            elif islast:
                segs = [(0, 64)]
            else:
                segs = [(0, nb)]
            for (a0, a1) in segs:
                V(lambda e, a0=a0, a1=a1: e.tensor_tensor_scan(out=MR[0:4, a0:a1], data0=LF[0:4, a0:a1], data1=IG[0:4, a0:a1], initial=mcar[0:4, 0:1], op0=ALU.add, op1=ALU.max), [LF, IG, mcar], [MR])
                V(lambda e, a1=a1: e.tensor_copy(mcar[0:4, 0:1], MR[0:4, a1 - 1:a1]), [MR], [mcar])
            if islast:
                DMA("sp", lambda e: e.dma_start(out=o_pm[l].rearrange("(h o) -> h o", o=1), in_=mcar[0:4, 0:1]), [mcar], [])
                pq1 = pj.get()
                for h in range(4):
                    PE(lambda e, h=h: e.matmul(pq1[:, h:h + 1], sel(h), mcar[0:4, 0:1], start=True, stop=True), [cf, mcar], [pq1])
                A(lambda e: e.activation(out=emnp[:, :], in_=pq1[:, 0:4], func=AF.Exp, scale=-1.0), [pq1], [emnp])
                cur = m0T
                for p in range(4):
                    tmp = ml_tmp.get()
                    V(lambda e, p=p, cur=cur, tmp=tmp: e.tensor_tensor(out=tmp[0:4, 0:NSEQ], in0=LF[0:4, 64 + p:128:4], in1=cur[0:4, 0:NSEQ], op=ALU.add), [LF, cur], [tmp])
                    V(lambda e, p=p, tmp=tmp: e.tensor_tensor(out=msE[0:4, 0:NSEQ], in0=tmp[0:4, 0:NSEQ], in1=IG[0:4, 64 + p:128:4], op=ALU.max), [tmp, IG], [msE])
                    cur = msE
                DMA("sp", lambda e: ncdma(e, o_sm[l].rearrange("s h -> h s"), msE[0:4, 0:NSEQ]), [msE], [])
                pq2 = pj.get()
                for h in range(4):
                    PE(lambda e, h=h: e.matmul(pq2[:, 64 + h * NSEQ:64 + (h + 1) * NSEQ], sel(h), msE[0:4, 0:NSEQ], start=True, stop=True), [cf, msE], [pq2])
                A(lambda e: e.activation(out=emn[:, :, :].rearrange("p a b -> p (a b)"), in_=pq2[:, 64:64 + 4 * NSEQ], func=AF.Exp, scale=-1.0), [pq2], [emn])
            eb, enb = ebp.get(), enbp.get()
            for h in range(4):
                pb = pj.get()
                PE(lambda e, h=h, pb=pb: e.matmul(pb[:, 0:nb], sel(h), BP[0:4, 0:nb], start=True, stop=True), [cf, BP], [pb])
                A(lambda e, h=h, pb=pb: e.activation(out=eb[:, h, 0:nb], in_=pb[:, 0:nb], func=AF.Exp, scale=-1.0), [pb], [eb])
                pb2 = pj.get()
                PE(lambda e, h=h, pb2=pb2: e.matmul(pb2[:, 0:nb], sel(h), A2[0:4, 0:nb], start=True, stop=True), [cf, A2], [pb2])
                A(lambda e, h=h, pb2=pb2: e.activation(out=enb[:, h, 0:nb], in_=pb2[:, 0:nb], func=AF.Exp), [pb2], [enb])
            qT, kT, khT = qTp.tiles[par], kTp.tiles[par], khTp.tiles[par]
            for h in range(4):
                pq = pj.get()
                proj_fm(wq, h * 128, 128, c0, nb, pq)
                V(lambda e, h=h, pq=pq: e.tensor_tensor(out=qT[:, h, 0:nb], in0=pq[:, 0:nb], in1=eb[:, h, 0:nb], op=ALU.mult), [pq, eb], [qT])
                pk = pj.get()
                proj_fm(wk, h * 128, 128, c0, nb, pk)
                V(lambda e, h=h, pk=pk: e.scalar_tensor_tensor(out=kT[:, h, 0:nb], in0=pk[:, 0:nb], scalar=128.0 ** -0.5, in1=enb[:, h, 0:nb], op0=ALU.mult, op1=ALU.mult), [pk, enb], [kT])
            EBE = EBEp.tiles[par]
            base = ebe_fill(EBE, eb, 128, blk, c0)
            khat(khT, kT, EBE, base, 128, blk)
            vS = vSp.tiles[par]
            v_proj(wv, blk, vS)
            sg, sgo = sgp.tiles[par], sgop.tiles[par]
            act_proj(wg, c0, nb, sg, AF.Silu)
            act_proj(wo, c0, nb, sgo, AF.Sigmoid)

            def post(ots, ti, bi, tc0):
                num, den = ots
                dd = f1p.get()
                ddf = dd[:, :, :].rearrange("p a b -> p (a b)")
                A(lambda e: e.activation(out=ddf, in_=den[:, :], func=AF.Abs), [den], [dd])
                V(lambda e: e.tensor_scalar_max(out=ddf, in0=ddf, scalar1=1.0), [dd], [dd])
                A(lambda e: e.activation(out=ddf, in_=ddf, func=AF.Ln), [dd], [dd])
                A(lambda e: e.activation(out=ddf, in_=ddf, func=AF.Exp, scale=-1.0), [dd], [dd])
                x = f2p.get()
                xf = x[:, :, :].rearrange("p a b -> p (a b)")
                V(lambda e: e.tensor_tensor(out=xf, in0=num[:, :], in1=ddf, op=ALU.mult), [num, dd], [x])
                V(lambda e: e.tensor_tensor(out=x[:, :, :], in0=x[:, :, :], in1=sgo[:, :, tc0:tc0 + 128], op=ALU.mult), [x, sgo], [x])
                xb = sqp.get()
                G(lambda e: e.tensor_copy(xb[:, :], xf), [x], [xb])
                PE(lambda e: e.matmul(pat[:, :], onesD[:, :], xb[:, :], start=True, stop=True), [onesD, xb], [pat])
                V(lambda e: e.tensor_tensor(out=xf, in0=xf, in1=pat[:, :], op=ALU.subtract), [x, pat], [x])
                sq = sqp.get()
                A(lambda e: e.activation(out=sq[:, :], in_=xf, func=AF.Square), [x], [sq])
                PE(lambda e: e.matmul(pat[:, :], onesD[:, :], sq[:, :], start=True, stop=True), [onesD, sq], [pat])
                rs = f1p.get()
                rsf = rs[:, :, :].rearrange("p a b -> p (a b)")
                rsqrt(rsf, pat[:, :], 128, [pat], [rs])
                V(lambda e: e.tensor_tensor(out=xf, in0=xf, in1=rsf, op=ALU.mult), [x, rs], [x])
                V(lambda e: e.tensor_tensor(out=x[:, :, :], in0=x[:, :, :], in1=pf[l][:, P_MLN:P_MLN + 4].unsqueeze(2).broadcast_to([128, 4, 128]), op=ALU.mult), [x, pf[l]], [x])
                V(lambda e: e.tensor_tensor(out=zT[:, :, ti * 128:(ti + 1) * 128], in0=x[:, :, :], in1=sg[:, :, tc0:tc0 + 128], op=ALU.mult), [x, sg], [zT.p(ti)])

            kb.rec = Rs[-1]
            op_ = 1 - par
            cfg["stgb"] = Pool([stgb.tiles[0], view(ycb.tiles[0][:, :, :], ycb.tiles[0].parts), qTp.tiles[op_], kTp.tiles[op_], khTp.tiles[op_], sgop.tiles[op_]])
            sgo_ = sgp.tiles[op_]
            cfg["khm"] = Pool([khmp.tiles[0], view(sgo_[:, 0:2, :].rearrange("p a b -> p (a b)"), sgo_.parts), view(sgo_[:, 2:4, :].rearrange("p a b -> p (a b)"), sgo_.parts)])
            pend = None
            for bi, ti in enumerate(blk):
                th = rec_tile(cfg, l, ti, bi, 0, qT, kT, khT, vS, EBE, base[bi], post)
                if cfg["nvc"] == 2:
                    th()
                else:
                    if pend is not None:
                        pend()
                    pend = th
            if pend is not None:
                pend()

        for bidx, blk in enumerate(blocks):
            body(blk, bidx % 2)
        merge_overlapped(l, 2, Ps, Rs)
        so = stg.get()
        V(lambda e: e.tensor_tensor(out=so[:, :, 0:128], in0=S[:, :, 0:128], in1=emnp[:, :].unsqueeze(2).broadcast_to([128, 4, 128]), op=ALU.mult), [S, emnp], [so])
        DMA("sp", lambda e: e.dma_start(out=o_pC[l].rearrange("h d v -> d h v"), in_=so[:, :, 0:128]), [so], [])
        npo = ml_npo
        V(lambda e: e.tensor_tensor(out=npo[:, :], in0=S[:, :, 128], in1=emnp[:, :], op=ALU.mult), [S, emnp], [npo])
        PE(lambda e: e.transpose(pat[0:4, 0:128], npo[:, 0:4], cf[:, K_ID:K_ID + 128]), [npo, cf], [pat])
        t4 = tokp.get()
        A(lambda e: e.activation(out=t4[0:4, 0:128], in_=pat[0:4, 0:128], func=AF.Copy), [pat], [t4])
        DMA("sp", lambda e: e.dma_start(out=o_pn[l], in_=t4[0:4, 0:128]), [t4], [])
        PE(lambda e: e.transpose(pat[0:64, 128:256], nTo[:, 0:64], cf[:, K_ID:K_ID + 128]), [nTo, cf], [pat])
        t5 = tokp.get()
        A(lambda e: e.activation(out=t5[0:64, 0:128], in_=pat[0:64, 128:256], func=AF.Copy), [pat], [t5])
        DMA("sp", lambda e: e.dma_start(out=o_sn[l], in_=t5[0:64, 0:128]), [t5], [])

    def phase_conv(l):
        psum_plan("stream")
        wa = load_w(l, C_CA, 512)
        wb = load_w(l, C_CB, 512)
        wg = load_w(l, C_CG, 512)
        V(lambda e: e.memset(ubf[:, :, 0:30], 0.0), [], [ubf])
        V(lambda e: e.tensor_copy(wcb[:, :, :], pf[l][:, P_CW:P_CW + 124].rearrange("p (a b) -> p a b", b=31)), [pf[l]], [wcb])
        for g4 in range(4):
            cv = cvin.get()
            DMA("sp", lambda e, cv=cv, g4=g4: e.dma_start(out=cv[0:120, :], in_=st_cv[l, 4 * g4:4 * g4 + 4].rearrange("s r c -> (s r) c")), [], [cv])
            for cc in range(4):
                PE(lambda e, cc=cc, cv=cv: e.transpose(pat[:, cc * 128:cc * 128 + 120], cv[0:120, cc * 128:(cc + 1) * 128], cf[0:120, K_ID:K_ID + 120]), [cv, cf], [pat])
            A(lambda e, g4=g4: e.activation(out=ucs[:, :, 4 * g4:4 * g4 + 4, 0:30], in_=pat[:, :].rearrange("p (a b) -> p a b", b=128)[:, :, 0:120].rearrange("p a (s r) -> p a s r", r=30),
                                            func=AF.Copy), [pat], [ucs])
        DMA("sp", lambda e: e.dma_start(out=o_scv[l, :, 0:26, :], in_=st_cv[l, :, 4:30, :]), [], [])
        ys = f2p.get()

        ubufs = [ubf, ubf2]
        V(lambda e: e.memset(ubf2[:, :, 0:30], 0.0), [], [ubf2])

        def stage1(cc, bidx, blk):
            ub = ubufs[bidx % 2]
            c0 = blk[0] * 128
            nb = len(blk) * 128
            islast = blk[0] == NT - 1
            if blk[0] == 0:
                segs = [(0, 16, 30), (64, nb, 46)]
                nreal = nb - 48
            elif islast:
                segs = [(0, 64, 30)]
                nreal = 64
            else:
                segs = [(0, nb, 30)]
                nreal = nb
            pa = pj.get()
            proj_fm(wa, cc * 128, 128, c0, nb, pa)
            pb = pj.get()
            proj_fm(wb, cc * 128, 128, c0, nb, pb)
            sgm = sgp.get()
            A(lambda e: e.activation(out=sgm[:, 0, 0:nb], in_=pb[:, 0:nb], func=AF.Sigmoid), [pb], [sgm])
            for (a0, a1, d0) in segs:
                V(lambda e, a0=a0, a1=a1, d0=d0: e.tensor_tensor(out=ub[:, cc, d0:d0 + a1 - a0], in0=pa[:, a0:a1], in1=sgm[:, 0, a0:a1], op=ALU.mult), [pa, sgm], [ub])
            if islast:
                V(lambda e: e.tensor_tensor(out=ucb[:, cc, 30:94], in0=pa[:, 0:64], in1=sgm[:, 0, 0:64], op=ALU.mult), [pa, sgm], [ucb])
                V(lambda e: e.tensor_tensor(out=ucs[:, cc, :, 30:34], in0=pa[:, 64:128].rearrange("p (s r) -> p s r", r=4),
                                            in1=sgm[:, 0, 64:128].rearrange("p (s r) -> p s r", r=4), op=ALU.mult), [pa, sgm], [ucs])
            return (cc, blk, ub, nreal, islast)

        def halo(cc, bidx, st):
            ub, nreal, islast = st[2], st[3], st[4]
            if not islast:
                nx = ubufs[(bidx + 1) % 2]
                V(lambda e: e.tensor_copy(nx[:, cc, 0:30], ub[:, cc, nreal:nreal + 30]), [ub], [nx])

        def stage2(st):
            cc, blk, ub, nreal, islast = st
            c0 = blk[0] * 128
            nb = len(blk) * 128
            zparts = [zT.p(t) for t in blk]
            pc = pj.get()
            for jt in range(31):
                PE(lambda e, jt=jt: e.matmul(pc[:, 0:nreal], dg[:, jt, :], ub[:, cc, jt:jt + nreal], start=(jt == 0), stop=(jt == 30)), [dg, ub], [pc])
            bcol = pf[l][:, P_CB + cc:P_CB + cc + 1]
            if blk[0] == 0:
                A(lambda e: e.activation(out=zT[:, cc, c0:c0 + 16], in_=pc[:, 0:16], func=AF.Identity, bias=bcol, scale=1.0), [pc, pf[l]], zparts)
                A(lambda e: e.activation(out=zT[:, cc, c0 + 64:c0 + nb], in_=pc[:, 16:nreal], func=AF.Identity, bias=bcol, scale=1.0), [pc, pf[l]], zparts)
            else:
                A(lambda e: e.activation(out=zT[:, cc, c0:c0 + nreal], in_=pc[:, 0:nreal], func=AF.Identity, bias=bcol, scale=1.0), [pc, pf[l]], zparts)

            def sample_taps():
                o3 = ys[:, cc, 0:64].rearrange("p (s r) -> p s r", r=4)
                for jt in range(31):
                    wcol = pf[l][:, P_CW + cc * 31 + jt:P_CW + cc * 31 + jt + 1]
                    if jt == 0:
                        V(lambda e, wcol=wcol: e.tensor_scalar(out=o3, in0=ucs[:, cc, :, 0:4], scalar1=wcol, scalar2=bcol, op0=ALU.mult, op1=ALU.add), [ucs, pf[l]], [ys])
                    else:
                        V(lambda e, wcol=wcol, jt=jt: e.scalar_tensor_tensor(out=o3, in0=ucs[:, cc, :, jt:jt + 4], scalar=wcol, in1=o3, op0=ALU.mult, op1=ALU.add), [ucs, pf[l], ys], [ys])
                V(lambda e: e.tensor_copy(zT[:, cc, c0 + 64:c0 + 128], ys[:, cc, 0:64]), [ys], zparts)

            return sample_taps if islast else None

        dg = dgA

        def build_dg(cc):
            V(lambda e: e.tensor_tensor(out=dg[:, :, :], in0=identb[:, :].unsqueeze(1).broadcast_to([128, 31, 128]),
                                        in1=wcb[:, cc, :].unsqueeze(2).broadcast_to([128, 31, 128]), op=ALU.mult), [identb, wcb], [dg])

        build_dg(0)
        for cc in range(4):
            taps = None
            pend = None
            for bidx, blk in enumerate(blocks):
                st = stage1(cc, bidx, blk)
                if pend is not None:
                    taps = stage2(pend) or taps
                halo(cc, bidx, st)
                pend = st
            taps = stage2(pend) or taps
            if cc < 3:
                build_dg(cc + 1)
            if taps is not None:
                taps()
        for cc in range(4):
            PE(lambda e, cc=cc: e.transpose(pat[0:64, cc * 128:(cc + 1) * 128], ucb[:, cc, 30:94], cf[:, K_ID:K_ID + 128]), [ucb, cf], [pat])
        tk = tokp.get()
        A(lambda e: e.activation(out=tk[0:64, :], in_=pat[0:64, :], func=AF.Copy), [pat], [tk])
        DMA("sp", lambda e: e.dma_start(out=o_pcv[l, :, :], in_=tk[34:64, :]), [tk], [])
        us = ys
        V(lambda e: e.tensor_copy(us[:, :, 0:64].rearrange("p a (s r) -> p a s r", r=4), ucs[:, :, :, 30:34]), [ucs], [us])
        for cc in range(4):
            PE(lambda e, cc=cc: e.transpose(pat[0:64, cc * 128:(cc + 1) * 128], us[:, cc, 0:64], cf[:, K_ID:K_ID + 128]), [us, cf], [pat])
        tk2 = tokp.get()
        A(lambda e: e.activation(out=tk2[0:64, :], in_=pat[0:64, :], func=AF.Copy), [pat], [tk2])
        DMA("sp", lambda e: e.dma_start(out=o_scv[l, :, 26:30, :], in_=tk2[0:64, :]), [tk2], [])

        def ln_block(blk):
            c0 = blk[0] * 128
            nb = len(blk) * 128
            zparts = [zT.p(t) for t in blk]
            yc = ycp.get()
            for cc in range(4):
                PE(lambda e, cc=cc: e.matmul(pat[:, 0:nb], onesC[:, :], zT[:, cc, c0:c0 + nb], start=(cc == 0), stop=(cc == 3)), [onesC] + zparts, [pat])
            V(lambda e: e.tensor_tensor(out=yc[:, :, 0:nb], in0=zT[:, :, c0:c0 + nb], in1=pat[:, 0:nb].unsqueeze(1).broadcast_to([128, 4, nb]), op=ALU.subtract), zparts + [pat], [yc])
            sg = sgp.get()
            act_proj(wg, c0, nb, sg, AF.Silu)
            yb = ycb.get()
            A(lambda e: e.activation(out=yb[:, :, 0:nb], in_=yc[:, :, 0:nb], func=AF.Square), [yc], [yb])
            for cc in range(4):
                PE(lambda e, cc=cc: e.matmul(pat[:, 0:nb], onesC[:, :], yb[:, cc, 0:nb], start=(cc == 0), stop=(cc == 3)), [onesC, yb], [pat])
            rs = f2p.get()
            rsf = rs[:, :, :].rearrange("p a b -> p (a b)")
            rsqrt(rsf[:, 0:nb], pat[:, 0:nb], 128, [pat], [rs])
            V(lambda e: e.tensor_tensor(out=yc[:, :, 0:nb], in0=yc[:, :, 0:nb], in1=rsf[:, 0:nb].unsqueeze(1).broadcast_to([128, 4, nb]), op=ALU.mult), [yc, rs], [yc])
            for cc in range(4):
                V(lambda e, cc=cc: e.tensor_scalar(out=yc[:, cc, 0:nb], in0=yc[:, cc, 0:nb], scalar1=pf[l][:, P_CG + cc:P_CG + cc + 1], scalar2=pf[l][:, P_CBT + cc:P_CBT + cc + 1], op0=ALU.mult, op1=ALU.add),
                  [yc, pf[l]], [yc])
            A(lambda e: e.activation(out=yc[:, :, 0:nb], in_=yc[:, :, 0:nb], func=AF.Silu), [yc], [yc])
            V(lambda e: e.tensor_tensor(out=zT[:, :, c0:c0 + nb], in0=yc[:, :, 0:nb], in1=sg[:, :, 0:nb], op=ALU.mult), [yc, sg], zparts)

        for blk in blocks:
            ln_block(blk)

    def merge_blocks(i, wts, blks, pgp, gtl, W=512):
        wg0, wg1, wb = wts

        def one(blk, oc):
            c0 = blk[0] * 128
            nb = len(blk) * 128
            wgt = wg0 if oc < 4 else wg1
            pg = pgp.get()
            proj_fm(wgt, (oc % 4) * 128, 128, c0, nb, pg)
            gt = gtl.get()
            A(lambda e: e.activation(out=gt[:, 0:nb], in_=pg[:, 0:nb], func=AF.Sigmoid), [pg], [gt])
            py = pgp.get()
            wbt = wb[oc // 4]
            for kc in range(4):
                PE(lambda e, kc=kc: e.matmul(py[:, 0:nb], wbt[:, kc, (oc % 4) * 128:(oc % 4 + 1) * 128], zT[:, kc, c0:c0 + nb], start=(kc == 0), stop=(kc == 3)),
                   [wbt] + [zT.p(t) for t in blk], [py])
            mparts = [mixT.p(t) for t in blk]
            if i == 0:
                V(lambda e: e.tensor_tensor(out=mixT[:, oc, c0:c0 + nb], in0=py[:, 0:nb], in1=gt[:, 0:nb], op=ALU.mult), [py, gt], mparts)
            else:
                V(lambda e: e.tensor_tensor(out=gt[:, W:W + nb], in0=py[:, 0:nb], in1=gt[:, 0:nb], op=ALU.mult), [py, gt], [gt])
                G(lambda e: e.tensor_tensor(out=mixT[:, oc, c0:c0 + nb], in0=mixT[:, oc, c0:c0 + nb], in1=gt[:, W:W + nb], op=ALU.add), [gt] + mparts, mparts)

        for blk in blks:
            for oc in range(8):
                one(blk, oc)

    def load_merge_w(l, i):
        return (load_w(l, C_GATE + i * 1024, 512), load_w(l, C_GATE + i * 1024 + 512, 512), load_wb(l, i))

    def phase_merge(l, i):
        wts = load_merge_w(l, i)
        psum_plan("stream")
        merge_blocks(i, wts, mblocks, pjm, gtp)

    def merge_overlapped(l, i, Ps, Rs):
        Mt = []
        kb.rec = []
        wts = load_merge_w(l, i)
        mpool = Pool(list(pj.tiles))
        for bi_, blk in enumerate(blocks[:-1]):
            if bi_ > 0:
                kb.rec = []
            merge_blocks(i, wts, [blk], mpool, gtx, 256)
            Mt.append((bi_, kb.rec))
        kb.pipeline(Ps, Rs, Mt)
        psum_plan("stream")
        merge_blocks(i, wts, blocks[-1:], pjm, gtp)

    def phase_out(l):
        psum_plan("stream")
        wo = load_wo(l)
        if l == DEPTH - 1:
            hst.tiles = [t for t in hst.tiles if t.parts != [r_eb]]
            DMA("sp", lambda e: e.dma_start(out=fnbt[:, :], in_=fnb[:, :]), [], [fnbt])
        tl = list(hst.tiles)
        hA, hB = Pool(tl[:4]), Pool(tl[4:])
        LA = 3
        hts = {}

        def issue(tj):
            t = hA.get()
            if l == 0:
                DMA("sp", lambda e: e.dma_start(out=t[:, :], in_=hin[tj * 128:(tj + 1) * 128, :]), [], [t])
            else:
                DMA("sp", lambda e: e.dma_start(out=t[:, :], in_=hscr[tj * 128:(tj + 1) * 128, :]), [hscr_res], [t])
            hts[tj] = t

        for tj in range(min(LA, NT)):
            issue(tj)
        for ti in range(NT):
            ht = hts.pop(ti)
            for hh in range(2):
                po = pj.get()
                for kc in range(8):
                    PE(lambda e, kc=kc, po=po: e.matmul(po[:, :], mixT[:, kc, ti * 128:(ti + 1) * 128], wo[hh][:, kc, :], start=(kc == 0), stop=(kc == 7)), [mixT.p(ti), wo[hh]], [po])
                V(lambda e, po=po: e.tensor_tensor(out=ht[:, hh * 512:(hh + 1) * 512], in0=ht[:, hh * 512:(hh + 1) * 512], in1=po[:, :], op=ALU.add), [ht, po], [ht])
            if l == 0:
                DMA("sp", lambda e: e.dma_start(out=hscr[ti * 128:(ti + 1) * 128, :], in_=ht[:, :]), [ht], [hscr_res])
                norm_tile(1, ht, ti, hB)
            else:
                sq = hB.get()
                A(lambda e: e.activation(out=sq[:, :], in_=ht[:, :], func=AF.Square), [ht], [sq])
                s1 = sm1.get()
                V(lambda e: e.reduce_sum(out=s1[:, 0:1], in_=sq[:, :], axis=AX.X), [sq], [s1])
                rsqrt(s1[:, 2:3], s1[:, 0:1], 128, [s1], [s1], scale=1.0 / D)
                V(lambda e: e.scalar_tensor_tensor(out=sq[:, :], in0=ht[:, :], scalar=s1[:, 2:3], in1=fnbt[:, :], op0=ALU.mult, op1=ALU.mult), [ht, s1, fnbt], [sq])
                if ti == 0:
                    DMA("sp", lambda e: e.dma_start(out=yp[0:64, :], in_=sq[64:128, :]), [sq], [])
                elif ti == NT - 1:
                    DMA("sp", lambda e: e.dma_start(out=yp[SEQ - 64:SEQ, :], in_=sq[0:64, :]), [sq], [])
                    DMA("sp", lambda e: e.dma_start(out=ys[:, :], in_=sq[64:128, :]), [sq], [])
                else:
                    DMA("sp", lambda e: e.dma_start(out=yp[ti * 128 - 64:ti * 128 + 64, :], in_=sq[:, :]), [sq], [])
            if ti + LA < NT:
                issue(ti + LA)

    lbl = pf[1][:, P_LBL:P_LBL + 8].rearrange("p (h l) -> p h l", l=2)
    V(lambda e: e.tensor_tensor(out=lbt[:, :], in0=lbl[:, :, 1], in1=lbl[:, :, 0], op=ALU.subtract), [pf[1]], [lbt])
    A(lambda e: e.activation(out=lbt[:, :], in_=lbt[:, :], func=AF.Sigmoid), [lbt], [lbt])

    tl0 = list(hst.tiles)
    hA0, hB0 = Pool(tl0[:4]), Pool(tl0[4:])
    hts0 = {}

    def issue0(tj):
        t = hA0.get()
        DMA("sp", lambda e: e.dma_start(out=t[:, :], in_=hin[tj * 128:(tj + 1) * 128, :]), [], [t])
        hts0[tj] = t

    for tj in range(min(3, NT)):
        issue0(tj)
    for ti in range(NT):
        norm_tile(0, hts0.pop(ti), ti, hB0)
        if ti + 3 < NT:
            issue0(ti + 3)
    for l in range(DEPTH):
        if l == 0:
            V(lambda e: e.memset(omlt[:, :], 1.0), [], [omlt])
        else:
            V(lambda e: e.tensor_scalar(out=omlt[:, :], in0=lbt[:, :], scalar1=-1.0, scalar2=1.0, op0=ALU.mult, op1=ALU.add), [lbt], [omlt])
        phase_gla(l)
        phase_hgrn(l)
        phase_mlstm(l)
        phase_conv(l)
        phase_merge(l, 3)
        phase_out(l)
    kb.finish()
    es.close()
    return nc, kb


def make_consts(NT):
    c = np.zeros((128, CW), np.float32)
    c[:, K_ID:K_ID + 128] = np.eye(128, dtype=np.float32)
    s = np.arange(128)[:, None]
    t = np.arange(128)[None, :]
    same = (s // 64) == (t // 64)
    mid = (same & (s <= t)).astype(np.float32)
    first = mid.copy()
    first[16:64, :] = 0
    first[:, 16:64] = 0
    last = mid.copy()
    blk4 = ((s // 4) == (t // 4)) & (s <= t)
    last[64:, :] = 0
    last[:, 64:] = 0
    last[64:, 64:] = blk4[64:, 64:].astype(np.float32)
    c[:, K_MASK:K_MASK + 128] = first
    c[:, K_MASK + 128:K_MASK + 256] = mid
    c[:, K_MASK + 256:K_MASK + 384] = last
    r = np.ones(256, np.float32)
    r[::64] = 0
    c[:, K_R:K_R + 256] = r[None]
    rl = np.ones(128, np.float32)
    rl[0] = 0
    rl[64::4] = 0
    c[:, K_R + 256:K_R + 384] = rl[None]
    for j in range(NSEQ):
        c[64 + 4 * j:64 + 4 * j + 4, K_OH + j] = 1.0
    for h in range(4):
        c[h, K_SEL + h * 128:K_SEL + (h + 1) * 128] = 1.0
    c[:, K_ONE] = 1.0
    return c


def make_par(inp, l):
    p = np.zeros((128, PW), np.float32)
    g = lambda k: np.asarray(inp[k], np.float32)
    p[:, P_NORMW:P_NORMW + 8] = g("norm_w")[l].reshape(8, 128).T
    p[:, P_FNORM:P_FNORM + 8] = g("final_norm").reshape(8, 128).T
    p[0:64, P_NBGK:P_NBGK + 4] = -g("gla_bgk")[l].reshape(4, 64).T
    p[:, P_GNORM] = g("gla_norm")[l]
    p[:, P_HNORM] = g("hg_norm")[l]
    lbl = g("hg_lb_logits")
    p[:, P_LBL:P_LBL + 8] = lbl.reshape(2, 4, 128).transpose(2, 1, 0).reshape(128, 8)
    p[0:4, P_BI] = g("ml_bi")[l]
    p[0:4, P_NBF] = -g("ml_bf")[l]
    p[:, P_MLN:P_MLN + 4] = g("ml_norm")[l].reshape(4, 128).T
    p[:, P_CW:P_CW + 124] = g("conv_w")[l].reshape(31, 4, 128).transpose(2, 1, 0).reshape(128, 124)
    p[:, P_CB:P_CB + 4] = g("conv_b")[l].reshape(4, 128).T
    p[:, P_CG:P_CG + 4] = g("conv_ln_g")[l].reshape(4, 128).T
    p[:, P_CBT:P_CBT + 4] = g("conv_ln_b")[l].reshape(4, 128).T
    p[0:16, P_WGK:P_WGK + 256] = g("gla_wgk2")[l]
    return p


_CACHE = {}


def kernel(**inp):
    x_prompt = np.asarray(inp["x_prompt"], np.float32)
    x_sample = np.asarray(inp["x_sample"], np.float32)
    B, SEQ, _ = x_prompt.shape
    ncores = B
    NT = (SEQ + 64) // 128 + 1
    if NT not in _CACHE:
        _CACHE[NT] = build(NT)
    nc, kb = _CACHE[NT]
    NCOL = NT * 128
    cst = make_consts(NT)
    par = np.stack([make_par(inp, l) for l in range(DEPTH)])
    fnb = np.ascontiguousarray(np.broadcast_to(np.asarray(inp["final_norm"], np.float32)[None, :], (128, D)))
    meta = np.asarray(inp["meta_tokens"], np.float32)
    f = lambda k: np.ascontiguousarray(np.asarray(inp[k], np.float32))
    w_in, w_br, w_o = f("w_in"), f("w_branch"), f("w_out")
    sg, sh, sC, sn, sm, scv = f("state_gla"), f("state_hgrn"), f("state_mlstm_C"), f("state_mlstm_n"), f("state_mlstm_m"), f("state_conv")
    in_maps = []
    for c in range(ncores):
        hin = np.zeros((NCOL, D), np.float32)
        hin[0:16] = meta
        hin[64:64 + SEQ] = x_prompt[c]
        hin[NCOL - 64:] = x_sample[NSEQ * c:NSEQ * (c + 1)].reshape(64, D)
        sl = slice(NSEQ * c, NSEQ * (c + 1))
        in_maps.append({
            "hin": hin, "w_in": w_in, "w_branch": w_br, "w_out": w_o, "par": par, "cst": cst, "fnb": fnb,
            "st_gla": np.ascontiguousarray(sg[:, sl]), "st_hg": np.ascontiguousarray(sh[:, sl]),
            "st_C": np.ascontiguousarray(sC[:, sl]), "st_n": np.ascontiguousarray(sn[:, sl].reshape(DEPTH, NSEQ * 4, 128)),
            "st_m": np.ascontiguousarray(sm[:, sl].reshape(DEPTH, NSEQ * 4)), "st_cv": np.ascontiguousarray(scv[:, sl]),
        })
    res = run_bass_kernel_spmd(nc, in_maps, core_ids=list(range(ncores))).results
    cat = lambda k, ax: np.concatenate([np.asarray(r[k], np.float32) for r in res], axis=ax)
    stk = lambda k: np.stack([np.asarray(r[k], np.float32) for r in res], axis=1)
    y_prompt = np.stack([np.asarray(r["yp"], np.float32) for r in res], axis=0)
    y_sample = cat("ys", 0).reshape(x_sample.shape)
    nS = NSEQ * ncores
    return (y_prompt, y_sample,
            stk("p_gla"), stk("p_hg"), stk("p_C"), stk("p_n"), stk("p_m"), stk("p_cv"),
            cat("s_gla", 1), cat("s_hg", 1), cat("s_C", 1), cat("s_n", 1).reshape(DEPTH, nS, 4, 128),
            cat("s_m", 1), cat("s_cv", 1))

_LINE_COST = {634: 1.06, 671: 1.008, 673: 1.175, 573: 0.537, 574: 0.479, 676: 0.753, 678: 0.126, 680: 1.225, 660: 0.139, 847: 0.452, 851: 0.199, 852: 0.432, 853: 1.101, 856: 0.632, 859: 1.009, 860: 0.899, 865: 0.341, 868: 0.36, 665: 0.236, 771: 0.125, 772: 0.156, 796: 0.672, 788: 0.689, 802: 0.361, 694: 0.064, 699: 0.075, 696: 0.528, 703: 0.662, 708: 0.094, 723: 0.052, 735: 0.102, 744: 0.386, 746: 0.821, 808: 0.489, 813: 0.691, 809: 0.494, 814: 0.751, 642: 0.905, 766: 0.385, 767: 0.216, 782: 0.303, 784: 0.223, 1319: 0.151, 1315: 0.381, 1323: 0.389, 732: 0.352, 720: 0.86, 750: 0.389, 932: 0.323, 933: 0.263, 934: 0.386, 935: 1.009, 937: 0.64, 938: 0.91, 939: 0.899, 941: 0.45, 946: 0.371, 947: 1.166, 1325: 0.435, 1326: 0.727, 1031: 0.452, 1034: 0.452, 1035: 0.399, 1036: 0.274, 1037: 0.57, 1038: 0.318, 1071: 0.525, 1048: 0.566, 1049: 0.141, 1074: 0.451, 1072: 0.328, 1075: 0.35, 1080: 0.348, 1083: 0.366, 738: 0.071, 1097: 0.484, 1098: 0.424, 1099: 0.63, 1100: 0.629, 1103: 0.654, 1104: 0.693, 1106: 1.868, 1108: 0.687, 1110: 0.629, 1111: 0.436, 1115: 0.676, 1116: 0.692, 1117: 0.692, 1107: 0.377, 1054: 0.229, 1060: 0.473, 1061: 0.152, 1066: 0.245, 1009: 0.687, 1010: 0.427, 1013: 0.692, 1014: 0.165, 1245: 4.292, 1168: 0.259, 1169: 0.661, 1198: 0.371, 1200: 0.373, 1212: 0.18, 1222: 0.107, 1225: 0.273, 1226: 0.42, 1228: 0.453, 1202: 0.132, 1203: 0.131, 1235: 0.327, 1237: 0.284, 1238: 0.196, 1265: 0.111, 1272: 0.259, 1283: 0.233, 1284: 1.162, 1288: 0.996, 1290: 0.202, 1294: 1.166, 1296: 0.405, 1298: 1.009, 1299: 1.166, 651: 1.064, 1382: 0.298, 1383: 0.672, 1389: 1.057, 1391: 1.22, 1393: 1.284}
```

```python
import numpy as np
from contextlib import ExitStack
import concourse.bass as bass
import concourse.mybir as mybir
from concourse.bass_utils import run_bass_kernel_spmd

F32 = mybir.dt.float32
BF16 = mybir.dt.bfloat16
AF = mybir.ActivationFunctionType
ALU = mybir.AluOpType
AX = mybir.AxisListType

D = 1024
DEPTH = 2
EPS = 1e-6
D_IN = 11800
C_GQ, C_GK, C_GV, C_GLR, C_GG = 0, 256, 512, 1024, 1040
C_HQ, C_HF, C_HI, C_HG = 1552, 2064, 2576, 3088
C_MQ, C_MK, C_MV, C_MI, C_MF, C_MO, C_MG = 3600, 4112, 4624, 5136, 5140, 5144, 5656
C_CA, C_CB, C_CG = 6168, 6680, 7192
C_GATE = 7704
NSEQ = 16
KD = 16

P_NORMW = 0
P_FNORM = 8
P_NBGK = 16
P_GNORM = 20
P_HNORM = 21
P_LBL = 22
P_BI = 30
P_NBF = 31
P_MLN = 32
P_CW = 36
P_CB = 160
P_CG = 164
P_CBT = 168
P_WGK = 172
PW = 428

K_ID = 0
K_R = 128
K_OH = 512
K_SEL = 528
K_ONE = 1040
CWS = 1041
K_MASK = 1041
CW = 1425


class Res:
    __slots__ = ("w", "r")

    def __init__(self):
        self.w = None
        self.r = {}


class Tl:
    def __init__(self, t, nparts=1):
        self.t = t
        self.parts = [Res() for _ in range(nparts)]

    def __getitem__(self, idx):
        return self.t[idx]

    def p(self, i):
        return self.parts[i]


def _expand(lst):
    out = []
    for x in lst:
        if isinstance(x, Tl):
            out.extend(x.parts)
        elif isinstance(x, (list, tuple)):
            out.extend(_expand(x))
        else:
            out.append(x)
    return out


class KB:
    def __init__(self, nc, es):
        self.nc = nc
        self.eng = {"pe": nc.tensor, "act": nc.scalar, "dve": nc.vector, "pool": nc.gpsimd, "sp": nc.sync}
        self.sem = {}
        for e in ("pe", "act", "dve", "pool"):
            self.sem[e] = es.enter_context(nc.semaphore("s_" + e))
        for q in ("sp", "pool"):
            for i in range(KD):
                self.sem[("d", q, i)] = es.enter_context(nc.semaphore("d%s%d" % (q, i)))
        self.cnt = {e: 0 for e in ("pe", "act", "dve", "pool")}
        self.seen = {e: {} for e in self.eng}
        self.rec = None
        self.slack = 0.0
        self.rbias = 0.0
        self.dma_i = 0
        self.dma_qi = {"sp": 0, "pool": 0}
        self.dma_val = {(q, i): 0 for q in ("sp", "pool") for i in range(KD)}
        self.nins = 0

    def _waits(self, eng, reads, writes):
        need = {}

        def add(ev):
            if ev is None:
                return
            k, v = ev
            if need.get(k, 0) < v:
                need[k] = v

        for r in reads:
            add(r.w)
        for w in writes:
            if not (eng == "pe" and w.w is not None and w.w[0] == "pe"):
                add(w.w)
            for k, v in w.r.items():
                add((k, v))
        e = self.eng[eng]
        seen = self.seen[eng]
        for k, v in need.items():
            if seen.get(k, 0) >= v:
                continue
            e.wait_ge(self.sem[k], v)
            seen[k] = v

    def op(self, eng, fn, reads=(), writes=()):
        if self.rec is not None:
            self.rec.append((0, eng, fn, reads, writes))
            return None
        reads = _expand(reads)
        writes = _expand(writes)
        self._waits(eng, reads, writes)
        ins = fn(self.eng[eng])
        self.cnt[eng] += 1
        ev = (eng, self.cnt[eng])
        ins.then_inc(self.sem[eng], 1)
        self.nins += 1
        for r in reads:
            if r.r.get(eng, 0) < ev[1]:
                r.r[eng] = ev[1]
        for w in writes:
            w.w = ev
            w.r = {}
        return ins

    def dma(self, q, fn, reads=(), writes=()):
        if self.rec is not None:
            self.rec.append((1, q, fn, reads, writes))
            return None
        reads = _expand(reads)
        writes = _expand(writes)
        self._waits(q, reads, writes)
        s = (q, self.dma_qi[q] % KD)
        self.dma_qi[q] += 1
        self.dma_i += 1
        k = ("d",) + s
        e = self.eng[q]
        if self.dma_val[s] > 0 and self.seen[q].get(k, 0) < self.dma_val[s]:
            e.wait_ge(self.sem[k], self.dma_val[s])
            self.seen[q][k] = self.dma_val[s]
        ins = fn(e)
        self.dma_val[s] += 16
        ev = (k, self.dma_val[s])
        ins.then_inc(self.sem[k], 16)
        self.nins += 1
        for r in reads:
            if r.r.get(k, 0) < ev[1]:
                r.r[k] = ev[1]
        for w in writes:
            w.w = ev
            w.r = {}
        return ins

    COST = {"pe": 0.13, "act": 0.6, "dve": 0.6, "pool": 1.0, "sp": 0.0}

    def emit(self, ops):
        assert self.rec is None
        for k, eng, fn, r, w in ops:
            (self.dma if k else self.op)(eng, fn, r, w)

    def emit2(self, a, b):
        ca = sum(self.COST[o[1]] for o in a) + 1e-9
        cb = sum(self.COST[o[1]] for o in b) + 1e-9
        i = j = 0
        xa = xb = 0.0
        while i < len(a) or j < len(b):
            if j >= len(b) or (i < len(a) and xa / ca <= xb / cb):
                o = a[i]; i += 1; xa += self.COST[o[1]]
            else:
                o = b[j]; j += 1; xb += self.COST[o[1]]
            (self.dma if o[0] else self.op)(o[1], o[2], o[3], o[4])

    def pipeline(self, Ps, Rs, Mt=None):
        self.rec = None
        n = len(Ps)
        pos = {"P": [0, 0], "R": [0, 0]}
        segs = {"P": Ps, "R": Rs}
        Mt = Mt or []
        mflat = [(b, o) for (b, ops) in Mt for o in ops]
        mi = [0]

        def head_m():
            if mi[0] >= len(mflat):
                return None
            head("P")
            head("R")
            b, o = mflat[mi[0]]
            if pos["P"][0] < n or pos["R"][0] <= b:
                return None
            return o

        if not hasattr(self, "efree"):
            self.efree = {}
            self.rt = {}

        def head(k):
            sg, ix = pos[k]
            while sg < n and ix >= len(segs[k][sg]):
                sg += 1
                ix = 0
            pos[k] = [sg, ix]
            if sg >= n:
                return None
            if k == "R" and pos["P"][0] <= sg:
                return None
            if k == "P" and sg >= 2 and pos["R"][0] <= sg - 2:
                return None
            return segs[k][sg][ix]

        def est(o):
            kind, eng, fn, r, w = o
            rr, ww = _expand(r), _expand(w)
            t = 0.0
            for x in rr:
                t = max(t, self.rt.get(id(x), (0.0, 0.0))[0])
            for x in ww:
                a = self.rt.get(id(x), (0.0, 0.0))
                t = max(t, a[0], a[1])
            return max(self.efree.get(eng, 0.0), t + self.slack), rr, ww

        while True:
            hp, hr, hm = head("P"), head("R"), head_m()
            if hp is None and hr is None and hm is None:
                if pos["P"][0] >= n and pos["R"][0] >= n and mi[0] >= len(mflat):
                    break
                raise RuntimeError("pipeline deadlock")
            cand = []
            if hr is not None:
                cand.append(("R", hr) + est(hr))
            if hp is not None:
                cand.append(("P", hp) + est(hp))
            if hm is not None:
                cand.append(("M", hm) + est(hm))
            cand.sort(key=lambda c: c[2] - (self.rbias if c[0] == "R" else 0.0))
            k, o, st, rr, ww = cand[0]
            if k == "M":
                mi[0] += 1
            else:
                pos[k][1] += 1
            dur = _cost(o)
            kind, eng = o[0], o[1]
            if kind:
                self.efree[eng] = st + 0.05
                end = st + 1.5
            else:
                end = st + dur
                self.efree[eng] = end
            for x in rr:
                a = self.rt.setdefault(id(x), [0.0, 0.0])
                a[1] = max(a[1], end)
            for x in ww:
                self.rt[id(x)] = [end, 0.0]
            (self.dma if kind else self.op)(o[1], o[2], o[3], o[4])

    def finish(self):
        e = self.eng["sp"]
        for s, v in self.dma_val.items():
            if v > 0:
                e.wait_ge(self.sem[("d",) + s], v)
        for en in ("pe", "act", "dve", "pool"):
            if self.cnt[en] > 0:
                e.wait_ge(self.sem[en], self.cnt[en])


class _Dummy:
    def then_inc(self, *a, **k):
        return self


class _Probe:
    def __init__(self):
        self.out = None

    def __getattr__(self, name):
        def f(*a, **k):
            o = k.get("out", a[0] if a else None)
            if o is None:
                o = k.get("ap", None)
            self.out = o
            return _Dummy()
        return f


def _cost(o):
    kind, eng, fn = o[0], o[1], o[2]
    if kind:
        return 0.05
    try:
        c = _LINE_COST.get(fn.__code__.co_firstlineno)
        if c is not None:
            return c
    except Exception:
        pass
    p = _Probe()
    try:
        fn(p)
        shp = list(p.out.shape)
        n = 1
        for d in shp[1:]:
            n *= int(d)
    except Exception:
        n = 256
    if eng == "pe":
        return 0.03 + n * 0.00047
    if eng == "act":
        return 0.22 + n * 0.00095
    if eng == "dve":
        return 0.12 + n * 0.0011
    return 0.3 + n * 0.0022


class Pool:
    def __init__(self, tiles):
        self.tiles = tiles
        self.i = 0

    def get(self):
        t = self.tiles[self.i % len(self.tiles)]
        self.i += 1
        return t


def build(NT):
    NCOL = NT * 128
    NPR = 64 * (NT * 2 - 2) + 16
    SEQ = NPR - 16
    nc = bass.Bass("TRN2", target_bir_lowering=False)
    es = ExitStack()

    def din(name, shape):
        return nc.dram_tensor(name, shape, F32, kind="ExternalInput").ap()

    def dout(name, shape):
        return nc.dram_tensor(name, shape, F32, kind="ExternalOutput").ap()

    hin = din("hin", [NCOL, D])
    w_in = din("w_in", [DEPTH, D, D_IN])
    w_br = din("w_branch", [DEPTH, 4, 512, D])
    w_out = din("w_out", [DEPTH, D, D])
    par = din("par", [DEPTH, 128, PW])
    cst = din("cst", [128, CW])
    fnb = din("fnb", [128, D])
    st_gla = din("st_gla", [DEPTH, NSEQ, 4, 64, 128])
    st_hg = din("st_hg", [DEPTH, NSEQ, 4, 128, 128])
    st_C = din("st_C", [DEPTH, NSEQ, 4, 128, 128])
    st_n = din("st_n", [DEPTH, NSEQ * 4, 128])
    st_m = din("st_m", [DEPTH, NSEQ * 4])
    st_cv = din("st_cv", [DEPTH, NSEQ, 30, 512])

    yp = dout("yp", [SEQ, D])
    ys = dout("ys", [64, D])
    o_pg = dout("p_gla", [DEPTH, 4, 64, 128])
    o_ph = dout("p_hg", [DEPTH, 4, 128, 128])
    o_pC = dout("p_C", [DEPTH, 4, 128, 128])
    o_pn = dout("p_n", [DEPTH, 4, 128])
    o_pm = dout("p_m", [DEPTH, 4])
    o_pcv = dout("p_cv", [DEPTH, 30, 512])
    o_sg = dout("s_gla", [DEPTH, NSEQ, 4, 64, 128])
    o_sh = dout("s_hg", [DEPTH, NSEQ, 4, 128, 128])
    o_sC = dout("s_C", [DEPTH, NSEQ, 4, 128, 128])
    o_sn = dout("s_n", [DEPTH, NSEQ * 4, 128])
    o_sm = dout("s_m", [DEPTH, NSEQ, 4])
    o_scv = dout("s_cv", [DEPTH, NSEQ, 30, 512])
    hscr = nc.dram_tensor("hscr", [NCOL, D], F32, kind="Internal").ap()
    hscr_res = Res()

    kb = KB(nc, es)
    cnt = [0]

    def sb(shape, dt, nparts=1, name=None):
        cnt[0] += 1
        t = es.enter_context(nc.sbuf_tensor("%s%d" % (name or "t", cnt[0]), list(shape), dt))
        return Tl(t, nparts)

    def ps(shape, dt, name=None):
        cnt[0] += 1
        t = es.enter_context(nc.psum_tensor("%s%d" % (name or "p", cnt[0]), list(shape), dt))
        return Tl(t)

    def pool(shape, dt, bufs, nparts=1, name=None):
        return Pool([sb(shape, dt, nparts, name) for _ in range(bufs)])

    xnT = sb([128, 8, NCOL], BF16, NT, "xnT")
    mixT = sb([128, 8, NCOL], BF16, NT, "mixT")
    zT = sb([128, 4, NCOL], BF16, NT, "zT")
    NSLOT = 5
    wslots = pool([128, 8, 512], BF16, NSLOT, name="w")
    wsm = pool([128, 8, 16], BF16, 2, name="wsm")
    cf = sb([128, CWS], F32, name="cf")
    identb = sb([128, 128], BF16)
    maskb = sb([128, 3, 128], BF16)
    onesb = sb([128, 128], BF16)
    onesD = sb([128, 128], BF16)
    onesC = sb([128, 128], BF16)
    pf = [sb([128, P_WGK], F32, name="pf") for _ in range(DEPTH)]
    wgkb = [sb([16, 256], BF16) for _ in range(DEPTH)]
    lbt = sb([128, 4], F32)
    omlt = sb([128, 4], F32)
    zero4 = sb([128, 4], F32)
    epsc = sb([128, 1], F32)

    pjb = [ps([128, 512], F32, "pj") for _ in range(2)]
    pat = ps([128, 512], F32, "at")
    potb = [ps([128, 512], F32, "ot") for _ in range(2)]
    pkv_t = ps([128, 1024], F32, "kv")
    ptr = ps([128, 1024], BF16, "tr")
    pj = Pool(list(pjb))
    pot = Pool(list(potb))

    def view(ap, parts):
        t = Tl(ap, 1)
        t.parts = parts
        return t

    sm1 = pool([128, 4], F32, 4, name="sm")
    NBM = 256
    gA_t = sb([128, 4, NBM], F32, name="gA")
    gB_t = sb([128, 4, NBM], F32, name="gB")
    gAp, gBp = Pool([gA_t]), Pool([gB_t])
    hflat = [view(t[:, :, :].rearrange("p a b -> p (a b)"), t.parts) for t in (gA_t, gB_t)]
    hst = Pool(hflat)
    ar1 = sb([128, 2240], F32, name="ar1")
    r_eb, r_enb = Res(), Res()
    eb_t = view(ar1[:, 0:1024].rearrange("p (a b) -> p a b", b=NBM), [r_eb])
    enb_t = view(ar1[:, 1024:2048].rearrange("p (a b) -> p a b", b=NBM), [r_enb])
    ebp, enbp = Pool([eb_t]), Pool([enb_t])
    stg = Pool([gA_t, gB_t, eb_t, enb_t])
    hst.tiles += [view(ar1[:, 0:1024], [r_eb]), view(ar1[:, 1024:2048], [r_enb])]
    ucs = view(ar1[:, 0:2176].rearrange("p (a s r) -> p a s r", s=NSEQ, r=34), [r_eb, r_enb])
    fnbt = view(ar1[:, 0:1024], [r_eb, r_enb])
    ar2 = sb([128, 2560], F32, name="ar2")
    ar2b = sb([128, 2560], F32, name="ar2b")
    r2 = [Res() for _ in range(5)]
    r2b = [Res() for _ in range(5)]

    def bfv(i, b, ar=None):
        ar = ar2 if ar is None else ar
        return ar[:, 512 * i:512 * (i + 1)].bitcast(BF16).rearrange("p (a b) -> p a b", b=b)

    qTp = Pool([view(bfv(0, NBM), [r2[0]]), view(bfv(0, NBM, ar2b), [r2b[0]])])
    kTp = Pool([view(bfv(1, NBM), [r2[1]]), view(bfv(1, NBM, ar2b), [r2b[1]])])
    khTp = Pool([view(bfv(2, NBM), [r2[2]]), view(bfv(2, NBM, ar2b), [r2b[2]])])
    vSp = Pool([view(bfv(3, 512), [r2[3]]), view(bfv(3, 512, ar2b), [r2b[3]])])
    sgop = Pool([view(bfv(4, NBM), [r2[4]]), view(bfv(4, NBM, ar2b), [r2b[4]])])
    ucb = view(ar2[:, 0:376].rearrange("p (a b) -> p a b", b=94), [r2[0]])
    ycp = Pool([view(ar2[:, 1144:2168].rearrange("p (a b) -> p a b", b=NBM), r2[2:5])])
    sgp = pool([128, 4, NBM], BF16, 2)
    hst.tiles += [view(ar2[:, 0:1024], r2[0:2]), view(ar2[:, 1024:2048], r2[2:4]),
                  view(ar2b[:, 0:1024], r2b[0:2]), view(ar2b[:, 1024:2048], r2b[2:4])]
    dgA = view(ar2b[:, 576:2560].bitcast(BF16).rearrange("p (a b) -> p a b", b=128), r2b[1:5])
    dgB = view(ar1[:, 0:1984].bitcast(BF16).rearrange("p (a b) -> p a b", b=128), [r_eb, r_enb])
    ubf = view(ar2b[:, 0:572].bitcast(BF16).rearrange("p (a b) -> p a b", b=30 + NBM), r2b[0:2])
    ubf2 = view(ar2[:, 384:956].bitcast(BF16).rearrange("p (a b) -> p a b", b=30 + NBM), r2[0:2])
    wcb = sb([128, 4, 31], BF16)
    EBEp = pool([128, 4, 20], F32, 2)
    gtx_t = sb([128, 1024], BF16, name="gtx")
    khp = pool([128, 512], BF16, 2)
    khmp = pool([128, 512], BF16, 1)
    attp = pool([128, 4, 128], BF16, 2)
    sqp = pool([128, 512], BF16, 2)
    f1p = pool([128, 4, 128], F32, 1)
    f2p = pool([128, 4, 128], F32, 1)
    stgb = pool([128, 4, 256], BF16, 1)
    row4 = pool([4, NBM], F32, 6, name="row4")
    ml_tmp = pool([4, NSEQ], F32, 2)
    glrp = pool([16, NBM], BF16, 1)
    ycb = pool([128, 4, NBM], BF16, 1)
    gtp = Pool([view(ycb.tiles[0][:, :, :].rearrange("p a b -> p (a b)"), ycb.tiles[0].parts),
                view(f1p.tiles[0][:, :, :].rearrange("p a b -> p (a b)").bitcast(BF16), f1p.tiles[0].parts),
                view(f2p.tiles[0][:, :, :].rearrange("p a b -> p (a b)").bitcast(BF16), f2p.tiles[0].parts)])
    r_kv = [Res(), Res()]
    kvA = view(pkv_t[:, 0:512], [r_kv[0]])
    kvB = view(pkv_t[:, 512:1024], [r_kv[1]])
    pkv = view(pkv_t[:, :], r_kv)
    pjm = Pool(pjb + [pat] + potb + [kvA, kvB])

    def psum_plan(kind):
        if kind == "rec1":
            pj.tiles = pjb + [kvB]
            pot.tiles = list(potb)
        elif kind == "rec2":
            pj.tiles = list(pjb)
            pot.tiles = list(potb)
        else:
            pj.tiles = pjb + potb + [kvA, kvB]
            pot.tiles = list(potb)
        pj.i = 0
        pot.i = 0

    hbf = Pool(list(gtp.tiles))
    _rg = [Res(), Res()]
    gtx = Pool([view(gtx_t[:, 0:512], [_rg[0]]), view(gtx_t[:, 512:1024], [_rg[1]])])
    tokp = Pool([view(gtx_t[:, :].bitcast(F32), _rg)])
    cvin = tokp
    S_all = sb([128, 4, 256], F32, name="S")
    Sb_all = sb([128, 4, 256], BF16, name="Sb")
    ml_m0T = sb([4, NSEQ], F32)
    ml_em0 = sb([128, NSEQ * 4], F32)
    ml_n0T = sb([128, NSEQ * 4], F32)
    ml_nTo = sb([128, NSEQ * 4], F32)
    ml_emn = sb([128, 4, NSEQ], F32)
    ml_emnp = sb([128, 4], F32)
    ml_mcar = sb([4, 1], F32)
    ml_msE = sb([4, NSEQ], F32)
    ml_npo = sb([128, 4], F32)

    def V(fn, r, w):
        return kb.op("dve", fn, r, w)

    def A(fn, r, w):
        return kb.op("act", fn, r, w)

    def G(fn, r, w):
        return kb.op("pool", fn, r, w)

    def PE(fn, r, w):
        return kb.op("pe", fn, r, w)

    def DMA(q, fn, r, w):
        return kb.dma(q, fn, r, w)

    def ncdma(e, out, in_):
        with nc.allow_non_contiguous_dma(reason="tiny"):
            return e.dma_start(out=out, in_=in_)

    def rsqrt(out_ap, in_ap, np_, r, w, scale=1.0):
        A(lambda e: e.activation(out=out_ap, in_=in_ap, func=AF.Ln, bias=epsc[0:np_, 0:1], scale=scale), list(r) + [epsc], w)
        A(lambda e: e.activation(out=out_ap, in_=out_ap, func=AF.Exp, scale=-0.5), w, w)

    DMA("sp", lambda e: e.dma_start(out=cf[:, :], in_=cst[:, 0:CWS]), [], [cf])
    for l in range(DEPTH):
        DMA("sp", lambda e, l=l: e.dma_start(out=pf[l][:, :], in_=par[l, :, 0:P_WGK]), [], [pf[l]])
    V(lambda e: e.tensor_copy(identb[:, :], cf[:, K_ID:K_ID + 128]), [cf], [identb])
    tm = tokp.get()
    DMA("sp", lambda e: e.dma_start(out=tm[:, 0:384], in_=cst[:, K_MASK:K_MASK + 384]), [], [tm])
    V(lambda e: e.tensor_copy(maskb[:, :, :], tm[:, 0:384].rearrange("p (a b) -> p a b", b=128)), [tm], [maskb])
    V(lambda e: e.memset(onesb[:, :], 1.0), [], [onesb])
    V(lambda e: e.memset(onesD[:, :], 1.0 / 128), [], [onesD])
    V(lambda e: e.memset(onesC[:, :], 1.0 / 512), [], [onesC])
    V(lambda e: e.memset(zero4[:, :], 0.0), [], [zero4])
    V(lambda e: e.memset(epsc[:, :], EPS), [], [epsc])
    for l in range(DEPTH):
        tw = tokp.get()
        DMA("sp", lambda e, l=l, tw=tw: e.dma_start(out=tw[0:16, 0:256], in_=par[l, 0:16, P_WGK:P_WGK + 256]), [], [tw])
        V(lambda e, l=l, tw=tw: e.tensor_copy(wgkb[l][:, :], tw[0:16, 0:256]), [tw], [wgkb[l]])
    one_c = cf[:, K_ONE:K_ONE + 1]

    def sel(h):
        return cf[0:4, K_SEL + h * 128:K_SEL + (h + 1) * 128]

    def tile_chunks(i):
        if i == NT - 1:
            return [(0, 64, "p", 0)] + [(64 + 4 * j, 4, "s", j) for j in range(NSEQ)]
        if i == 0:
            return [(0, 16, "p", 0), (64, 64, "p", 0)]
        return [(0, 64, "p", 0), (64, 64, "p", 0)]

    def mask_of(i):
        return 2 if i == NT - 1 else (0 if i == 0 else 1)

    blocks = []
    i = 0
    while i < NT - 1:
        n = min(2, NT - 1 - i)
        blocks.append(list(range(i, i + n)))
        i += n
    blocks.append([NT - 1])

    mblocks = []
    i = 0
    while i < NT - 1:
        n = min(4, NT - 1 - i)
        mblocks.append(list(range(i, i + n)))
        i += n
    mblocks.append([NT - 1])

    def rpat(blk, nb):
        if blk[0] == NT - 1:
            return cf[:, K_R + 256:K_R + 384]
        return cf[:, K_R:K_R + nb]

    def load_w(l, col0, n, small=False):
        t = (wsm if small else wslots).get()
        src = w_in[l, :, col0:col0 + n].rearrange("(kc p) n -> p kc n", p=128)
        DMA("pool", lambda e: e.dma_start(out=t[:, :, 0:n], in_=src), [], [t])
        return t

    def load_wb(l, i):
        ts = []
        for hh in range(2):
            t = wslots.get()
            src = w_br[l, i, :, hh * 512:(hh + 1) * 512].rearrange("(kc p) n -> p kc n", p=128)
            DMA("pool", lambda e, t=t, src=src: e.dma_start(out=t[:, 0:4, :], in_=src), [], [t])
            ts.append(t)
        return ts

    def load_wo(l):
        ts = []
        for hh in range(2):
            t = wslots.get()
            src = w_out[l, :, hh * 512:(hh + 1) * 512].rearrange("(kc p) n -> p kc n", p=128)
            DMA("pool", lambda e, t=t, src=src: e.dma_start(out=t[:, :, :], in_=src), [], [t])
            ts.append(t)
        return ts

    def xparts(c0, nb):
        return [xnT.p(t) for t in range(c0 // 128, (c0 + nb + 127) // 128)]

    def proj_fm(w, wc0, M, c0, nb, pst):
        for kc in range(8):
            PE(lambda e, kc=kc: e.matmul(pst[0:M, 0:nb], w[:, kc, wc0:wc0 + M], xnT[:, kc, c0:c0 + nb],
                                        start=(kc == 0), stop=(kc == 7)), [w] + xparts(c0, nb), [pst])

    def proj_tm(w, n, ti, pst):
        for kc in range(8):
            PE(lambda e, kc=kc: e.matmul(pst[:, 0:n], xnT[:, kc, ti * 128:(ti + 1) * 128], w[:, kc, 0:n],
                                        start=(kc == 0), stop=(kc == 7)), [w, xnT.p(ti)], [pst])

    def norm_tile(l, ht, ti, sqpool=None):
        sq = (sqpool or hst).get()
        A(lambda e: e.activation(out=sq[:, :], in_=ht[:, :], func=AF.Square), [ht], [sq])
        s1 = sm1.get()
        V(lambda e: e.reduce_sum(out=s1[:, 0:1], in_=sq[:, :], axis=AX.X), [sq], [s1])
        rsqrt(s1[:, 2:3], s1[:, 0:1], 128, [s1], [s1], scale=1.0 / D)
        xb = hbf.get()
        V(lambda e: e.tensor_scalar_mul(out=xb[:, :], in0=ht[:, :], scalar1=s1[:, 2:3]), [ht, s1], [xb])
        for kc in range(8):
            PE(lambda e, kc=kc: e.transpose(ptr[:, kc * 128:(kc + 1) * 128], xb[:, kc * 128:(kc + 1) * 128], identb[:, :]), [xb, identb], [ptr])
        nw = pf[l][:, P_NORMW:P_NORMW + 8].unsqueeze(2).broadcast_to([128, 8, 128])
        V(lambda e: e.tensor_tensor(out=xnT[:, :, ti * 128:(ti + 1) * 128], in0=ptr[:, :].rearrange("p (a b) -> p a b", b=128), in1=nw, op=ALU.mult),
          [ptr, pf[l]], [xnT.p(ti)])
        return s1

    def rec_tile(cfg, l, ti, bi, c0, qT, kT, khT, vS, EBE, ebi0, post):
        dk, nvc = cfg["dk"], cfg["nvc"]
        dvx = 128 * nvc
        S, Sb = cfg["S"], cfg["Sb"]
        pkv = cfg["kv"]
        tc0 = c0 + bi * 128
        chunks = tile_chunks(ti)
        for h in range(4):
            PE(lambda e, h=h: e.transpose(ptr[:, h * dk:(h + 1) * dk], khT[0:dk, h, tc0:tc0 + 128], identb[0:dk, 0:dk]), [khT, identb], [ptr])
        kh = khp.get()
        A(lambda e: e.activation(out=kh[:, 0:4 * dk], in_=ptr[:, 0:4 * dk], func=AF.Copy), [ptr], [kh])
        for h in range(4):
            PE(lambda e, h=h: e.matmul(pat[:, h * 128:(h + 1) * 128], kT[0:dk, h, tc0:tc0 + 128], qT[0:dk, h, tc0:tc0 + 128], start=True, stop=True),
               [kT, qT], [pat])
        attm = attp.get()
        mk = maskb[:, mask_of(ti):mask_of(ti) + 1, :].broadcast_to([128, 4, 128])
        V(lambda e: e.tensor_tensor(out=attm[:, :, :], in0=pat[:, :].rearrange("p (a b) -> p a b", b=128), in1=mk, op=ALU.mult), [pat, maskb], [attm])
        ots = [pot.get() for _ in range(nvc)]
        for vc in range(nvc):
            for h in range(4):
                lhs = vS[:, bi, h * 128:(h + 1) * 128] if vc == 0 else onesb[:, :]
                PE(lambda e, vc=vc, h=h, lhs=lhs: e.matmul(ots[vc][:, h * 128:(h + 1) * 128], lhs, attm[:, h, :], start=(h == 0), stop=False, skip_group_check=True),
                   [vS, onesb, attm], [ots[vc]])
        nch = len(chunks)
        LA = 2
        pre = {}

        def issue_load(jj):
            t = stg.get()
            cfg["load_state"](l, jj, t)
            pre[jj] = t

        if any(c[2] == "s" for c in chunks):
            for jj in range(min(LA, NSEQ)):
                issue_load(jj)

        def do_chunk(ci, off, L, kind, j):
            last = ci == nch - 1
            ei = ebi0 + ci
            if kind == "p":
                src_b = Sb
            else:
                st = pre.pop(j)
                if j + LA < NSEQ and any(c[2] == "s" for c in chunks):
                    issue_load(j + LA)
                src_b = cfg["stgb"].get()
                A(lambda e, st=st, src_b=src_b: e.activation(out=src_b[0:dk, :, 0:dvx], in_=st[0:dk, :, 0:dvx], func=AF.Copy), [st], [src_b])
            for vc in range(nvc):
                for h in range(4):
                    PE(lambda e, vc=vc, h=h, src_b=src_b: e.matmul(ots[vc][:, h * 128 + off:h * 128 + off + L], src_b[0:dk, h, vc * 128:(vc + 1) * 128],
                                                                qT[0:dk, h, tc0 + off:tc0 + off + L], start=False, stop=last, skip_group_check=True), [src_b, qT], [ots[vc]])
            if kind == "p":
                r0, r1 = off, off + L
                lk = kh
            else:
                r0, r1 = 64, 128
                lk = cfg["khm"].get()
                V(lambda e, lk=lk, j=j: e.tensor_scalar_mul(out=lk[64:128, 0:4 * dk], in0=kh[64:128, 0:4 * dk], scalar1=cf[64:128, K_OH + j:K_OH + j + 1]),
                  [kh, cf], [lk])
            for h in range(4):
                PE(lambda e, h=h, lk=lk: e.matmul(pkv[0:dk, h * dvx:h * dvx + 128], lk[r0:r1, h * dk:(h + 1) * dk], vS[r0:r1, bi, h * 128:(h + 1) * 128], start=True, stop=True),
                   [lk, vS], [pkv])
                if nvc == 2:
                    PE(lambda e, h=h, lk=lk: e.matmul(pkv[0:dk, h * dvx + 128:h * dvx + 256], lk[r0:r1, h * dk:(h + 1) * dk], onesb[r0:r1, :], start=True, stop=True),
                       [lk, onesb], [pkv])
            ebb = EBE[0:dk, :, ei:ei + 1].broadcast_to([dk, 4, dvx])
            kvv = pkv[0:dk, 0:4 * dvx].rearrange("p (a b) -> p a b", b=dvx)
            if kind == "p":
                for h in range(4):
                    V(lambda e, h=h: e.scalar_tensor_tensor(out=S[0:dk, h, 0:dvx], in0=S[0:dk, h, 0:dvx], scalar=EBE[0:dk, h, ei:ei + 1],
                                                            in1=pkv[0:dk, h * dvx:(h + 1) * dvx], op0=ALU.mult, op1=ALU.add), [S, EBE, pkv], [S])
                A(lambda e: e.activation(out=Sb[0:dk, :, 0:dvx], in_=S[0:dk, :, 0:dvx], func=AF.Copy), [S], [Sb])
            else:
                so = st
                for h in range(4):
                    V(lambda e, h=h, so=so, st=st: e.scalar_tensor_tensor(out=so[0:dk, h, 0:dvx], in0=st[0:dk, h, 0:dvx], scalar=EBE[0:dk, h, ei:ei + 1],
                                                                          in1=pkv[0:dk, h * dvx:(h + 1) * dvx], op0=ALU.mult, op1=ALU.add), [st, EBE, pkv], [so])
                cfg["store_state"](l, j, so)

        for ci, (off, L, kind, j) in enumerate(chunks):
            do_chunk(ci, off, L, kind, j)
        return lambda: post(ots, ti, bi, tc0)

    def ebe_fill(EBE, eb, dk, blk, c0):
        base = []
        idx = 0
        for bi, ti in enumerate(blk):
            base.append(idx)
            tcol = bi * 128
            if ti == NT - 1:
                V(lambda e, idx=idx, tcol=tcol: e.tensor_copy(EBE[0:dk, :, idx:idx + 1], eb[0:dk, :, tcol + 63:tcol + 64]), [eb], [EBE])
                V(lambda e, idx=idx, tcol=tcol: e.tensor_copy(EBE[0:dk, :, idx + 1:idx + 17], eb[0:dk, :, tcol + 67:tcol + 128:4]), [eb], [EBE])
                idx += 17
            else:
                cA = 15 if ti == 0 else 63
                V(lambda e, idx=idx, tcol=tcol, cA=cA: e.tensor_copy(EBE[0:dk, :, idx:idx + 1], eb[0:dk, :, tcol + cA:tcol + cA + 1]), [eb], [EBE])
                V(lambda e, idx=idx, tcol=tcol: e.tensor_copy(EBE[0:dk, :, idx + 1:idx + 2], eb[0:dk, :, tcol + 127:tcol + 128]), [eb], [EBE])
                idx += 2
        return base

    def khat(khT, kT, EBE, base, dk, blk):
        for bi, ti in enumerate(blk):
            tcol = bi * 128
            b0 = base[bi]
            for h in range(4):
                if ti == NT - 1:
                    V(lambda e, h=h, b0=b0, tcol=tcol: e.tensor_scalar_mul(out=khT[0:dk, h, tcol:tcol + 64], in0=kT[0:dk, h, tcol:tcol + 64], scalar1=EBE[0:dk, h, b0:b0 + 1]),
                      [kT, EBE], [khT])
                    V(lambda e, h=h, b0=b0, tcol=tcol: e.tensor_tensor(out=khT[0:dk, h, tcol + 64:tcol + 128].rearrange("p (a b) -> p a b", b=4),
                                                                       in0=kT[0:dk, h, tcol + 64:tcol + 128].rearrange("p (a b) -> p a b", b=4),
                                                                       in1=EBE[0:dk, h, b0 + 1:b0 + 17].unsqueeze(2).broadcast_to([dk, 16, 4]), op=ALU.mult), [kT, EBE], [khT])
                elif h == 0:
                    V(lambda e, b0=b0, tcol=tcol: e.tensor_tensor(out=khT[0:dk, :, tcol:tcol + 128].rearrange("p h (a b) -> p h a b", b=64),
                                                                  in0=kT[0:dk, :, tcol:tcol + 128].rearrange("p h (a b) -> p h a b", b=64),
                                                                  in1=EBE[0:dk, :, b0:b0 + 2].unsqueeze(3).broadcast_to([dk, 4, 2, 64]), op=ALU.mult), [kT, EBE], [khT])

    def v_proj(wv, blk, vS):
        for bi, ti in enumerate(blk):
            pst = pj.get()
            proj_tm(wv, 512, ti, pst)
            A(lambda e, bi=bi, pst=pst: e.activation(out=vS[:, bi, :], in_=pst[:, :], func=AF.Copy), [pst], [vS])

    def act_proj(w, c0, nb, dst, func, scale=1.0):
        for h in range(4):
            pst = pj.get()
            proj_fm(w, h * 128, 128, c0, nb, pst)
            A(lambda e, h=h, pst=pst: e.activation(out=dst[:, h, 0:nb], in_=pst[:, 0:nb], func=func, scale=scale), [pst], [dst])

    def post_rms(cfg, l, sg, normcol):
        def post(ots, ti, bi, tc0):
            ot = ots[0]
            sq = sqp.get()
            A(lambda e: e.activation(out=sq[:, :], in_=ot[:, :], func=AF.Square), [ot], [sq])
            PE(lambda e: e.matmul(pat[:, :], onesD[:, :], sq[:, :], start=True, stop=True), [onesD, sq], [pat])
            rs = f1p.get()
            rsqrt(rs[:, :, :].rearrange("p a b -> p (a b)"), pat[:, :], 128, [pat], [rs])
            t1 = f2p.get()
            V(lambda e: e.tensor_tensor(out=t1[:, :, :].rearrange("p a b -> p (a b)"), in0=ot[:, :], in1=rs[:, :, :].rearrange("p a b -> p (a b)"), op=ALU.mult), [ot, rs], [t1])
            V(lambda e: e.scalar_tensor_tensor(out=zT[:, :, ti * 128:(ti + 1) * 128], in0=t1[:, :, :], scalar=normcol, in1=sg[:, :, tc0:tc0 + 128], op0=ALU.mult, op1=ALU.mult),
              [t1, sg, pf[l]], [zT.p(ti)])
        return post

    def phase_gla(l):
        S, Sb = S_all, Sb_all
        V(lambda e: e.memset(S[:, :, :], 0.0), [], [S])
        V(lambda e: e.memset(Sb[:, :, :], 0.0), [], [Sb])
        wqk = load_w(l, C_GQ, 512)
        wlr = load_w(l, C_GLR, 16, small=True)
        wv = load_w(l, C_GV, 512)
        wg = load_w(l, C_GG, 512)

        def load_state(l, j, st):
            DMA("sp", lambda e: e.dma_start(out=st[0:64, :, 0:128], in_=st_gla[l, j].rearrange("h d v -> d h v")), [], [st])

        def store_state(l, j, so):
            DMA("sp", lambda e: e.dma_start(out=o_sg[l, j].rearrange("h d v -> d h v"), in_=so[0:64, :, 0:128]), [so], [])

        cfg = dict(dk=64, nvc=1, S=S, Sb=Sb, load_state=load_state, store_state=store_state, kv=kvA)
        psum_plan("rec1")
        Ps, Rs = [], []

        def body(blk, par):
            Ps.append([])
            Rs.append([])
            kb.rec = Ps[-1]
            c0 = blk[0] * 128
            nb = len(blk) * 128
            pst = pj.get()
            proj_fm(wlr, 0, 16, c0, nb, pst)
            glr = glrp.get()
            A(lambda e: e.activation(out=glr[0:16, 0:nb], in_=pst[0:16, 0:nb], func=AF.Copy), [pst], [glr])
            gA = gAp.get()
            for h in range(4):
                p2 = pj.get()
                PE(lambda e, h=h, p2=p2: e.matmul(p2[0:64, 0:nb], wgkb[l][0:16, h * 64:(h + 1) * 64], glr[0:16, 0:nb], start=True, stop=True), [wgkb[l], glr], [p2])
                A(lambda e, h=h, p2=p2: e.activation(out=gA[0:64, h, 0:nb], in_=p2[0:64, 0:nb], func=AF.Exp, bias=pf[l][0:64, P_NBGK + h:P_NBGK + h + 1], scale=-1.0), [p2, pf[l]], [gA])
            A(lambda e: e.activation(out=gA[0:64, :, 0:nb], in_=gA[0:64, :, 0:nb], func=AF.Ln, bias=one_c[0:64, :], scale=1.0), [gA, cf], [gA])
            gB = gBp.get()
            for h in range(4):
                V(lambda e, h=h: e.tensor_tensor_scan(out=gB[0:64, h, 0:nb], data0=rpat(blk, nb)[0:64, :], data1=gA[0:64, h, 0:nb], initial=0.0, op0=ALU.mult, op1=ALU.add), [gA, cf], [gB])
            eb = ebp.get()
            enb = enbp.get()
            A(lambda e: e.activation(out=eb[0:64, :, 0:nb], in_=gB[0:64, :, 0:nb], func=AF.Exp, scale=-1.0 / 16), [gB], [eb])
            A(lambda e: e.activation(out=enb[0:64, :, 0:nb], in_=gB[0:64, :, 0:nb], func=AF.Exp, scale=1.0 / 16), [gB], [enb])
            qT, kT, khT = qTp.tiles[par], kTp.tiles[par], khTp.tiles[par]
            for h in range(4):
                pq = pj.get()
                proj_fm(wqk, h * 64, 64, c0, nb, pq)
                V(lambda e, h=h, pq=pq: e.scalar_tensor_tensor(out=qT[0:64, h, 0:nb], in0=pq[0:64, 0:nb], scalar=0.125, in1=eb[0:64, h, 0:nb], op0=ALU.mult, op1=ALU.mult), [pq, eb], [qT])
                pk = pj.get()
                proj_fm(wqk, 256 + h * 64, 64, c0, nb, pk)
                V(lambda e, h=h, pk=pk: e.tensor_tensor(out=kT[0:64, h, 0:nb], in0=pk[0:64, 0:nb], in1=enb[0:64, h, 0:nb], op=ALU.mult), [pk, enb], [kT])
            EBE = EBEp.tiles[par]
            base = ebe_fill(EBE, eb, 64, blk, c0)
            khat(khT, kT, EBE, base, 64, blk)
            vS = vSp.tiles[par]
            v_proj(wv, blk, vS)
            sg = sgp.tiles[par]
            act_proj(wg, c0, nb, sg, AF.Silu)
            post = post_rms(cfg, l, sg, pf[l][:, P_GNORM:P_GNORM + 1])
            kb.rec = Rs[-1]
            op_ = 1 - par
            cfg["stgb"] = Pool([stgb.tiles[0], view(ycb.tiles[0][:, :, :], ycb.tiles[0].parts), qTp.tiles[op_], kTp.tiles[op_], khTp.tiles[op_], sgop.tiles[op_]])
            sgo_ = sgp.tiles[op_]
            cfg["khm"] = Pool([khmp.tiles[0], view(sgo_[:, 0:2, :].rearrange("p a b -> p (a b)"), sgo_.parts), view(sgo_[:, 2:4, :].rearrange("p a b -> p (a b)"), sgo_.parts)])
            pend = None
            for bi, ti in enumerate(blk):
                th = rec_tile(cfg, l, ti, bi, 0, qT, kT, khT, vS, EBE, base[bi], post)
                if cfg["nvc"] == 2:
                    th()
                else:
                    if pend is not None:
                        pend()
                    pend = th
            if pend is not None:
                pend()

        for bidx, blk in enumerate(blocks):
            body(blk, bidx % 2)
        merge_overlapped(l, 0, Ps, Rs)
        DMA("sp", lambda e: e.dma_start(out=o_pg[l].rearrange("h d v -> d h v"), in_=S[0:64, :, 0:128]), [S], [])

    def phase_hgrn(l):
        S, Sb = S_all, Sb_all
        V(lambda e: e.memset(S[:, :, :], 0.0), [], [S])
        V(lambda e: e.memset(Sb[:, :, :], 0.0), [], [Sb])
        wq = load_w(l, C_HQ, 512)
        wf = load_w(l, C_HF, 512)
        wv = load_w(l, C_HI, 512)
        wg = load_w(l, C_HG, 512)
        lb = zero4 if l == 0 else lbt
        oml = omlt

        def load_state(l, j, st):
            DMA("sp", lambda e: e.dma_start(out=st[:, :, 0:128], in_=st_hg[l, j].rearrange("h d v -> d h v")), [], [st])

        def store_state(l, j, so):
            DMA("sp", lambda e: e.dma_start(out=o_sh[l, j].rearrange("h d v -> d h v"), in_=so[:, :, 0:128]), [so], [])

        cfg = dict(dk=128, nvc=1, S=S, Sb=Sb, load_state=load_state, store_state=store_state, kv=kvA)
        psum_plan("rec1")
        Ps, Rs = [], []

        def body(blk, par):
            Ps.append([])
            Rs.append([])
            kb.rec = Ps[-1]
            c0 = blk[0] * 128
            nb = len(blk) * 128
            gA, gB = gAp.get(), gBp.get()
            kT = kTp.tiles[par]
            eb, enb = ebp.get(), enbp.get()
            for h in range(4):
                pst = pj.get()
                proj_fm(wf, h * 128, 128, c0, nb, pst)
                A(lambda e, h=h, pst=pst: e.activation(out=gA[:, h, 0:nb], in_=pst[:, 0:nb], func=AF.Sigmoid), [pst], [gA])
                A(lambda e, h=h, pst=pst: e.activation(out=kT[:, h, 0:nb], in_=pst[:, 0:nb], func=AF.Sigmoid, scale=-1.0), [pst], [kT])
                V(lambda e, h=h: e.tensor_scalar(out=gA[:, h, 0:nb], in0=gA[:, h, 0:nb], scalar1=oml[:, h:h + 1], scalar2=lb[:, h:h + 1], op0=ALU.mult, op1=ALU.add), [gA, oml, lb], [gA])
            A(lambda e: e.activation(out=gA[:, :, 0:nb], in_=gA[:, :, 0:nb], func=AF.Ln), [gA], [gA])
            for h in range(4):
                V(lambda e, h=h: e.tensor_tensor_scan(out=gB[:, h, 0:nb], data0=rpat(blk, nb), data1=gA[:, h, 0:nb], initial=0.0, op0=ALU.mult, op1=ALU.add), [gA, cf], [gB])
            A(lambda e: e.activation(out=eb[:, :, 0:nb], in_=gB[:, :, 0:nb], func=AF.Exp), [gB], [eb])
            A(lambda e: e.activation(out=enb[:, :, 0:nb], in_=gB[:, :, 0:nb], func=AF.Exp, scale=-1.0), [gB], [enb])
            for h in range(4):
                V(lambda e, h=h: e.scalar_tensor_tensor(out=kT[:, h, 0:nb], in0=kT[:, h, 0:nb], scalar=oml[:, h:h + 1], in1=enb[:, h, 0:nb], op0=ALU.mult, op1=ALU.mult), [kT, oml, enb], [kT])
            qT, khT = qTp.tiles[par], khTp.tiles[par]
            for h in range(4):
                pq = pj.get()
                proj_fm(wq, h * 128, 128, c0, nb, pq)
                A(lambda e, h=h, pq=pq: e.activation(out=gA[:, h, 0:nb], in_=pq[:, 0:nb], func=AF.Silu), [pq], [gA])
            V(lambda e: e.tensor_tensor(out=qT[:, :, 0:nb], in0=gA[:, :, 0:nb], in1=eb[:, :, 0:nb], op=ALU.mult), [gA, eb], [qT])
            EBE = EBEp.tiles[par]
            base = ebe_fill(EBE, eb, 128, blk, c0)
            khat(khT, kT, EBE, base, 128, blk)
            vS = vSp.tiles[par]
            v_proj(wv, blk, vS)
            sg = sgp.tiles[par]
            act_proj(wg, c0, nb, sg, AF.Silu)
            post = post_rms(cfg, l, sg, pf[l][:, P_HNORM:P_HNORM + 1])
            kb.rec = Rs[-1]
            op_ = 1 - par
            cfg["stgb"] = Pool([stgb.tiles[0], view(ycb.tiles[0][:, :, :], ycb.tiles[0].parts), qTp.tiles[op_], kTp.tiles[op_], khTp.tiles[op_], sgop.tiles[op_]])
            sgo_ = sgp.tiles[op_]
            cfg["khm"] = Pool([khmp.tiles[0], view(sgo_[:, 0:2, :].rearrange("p a b -> p (a b)"), sgo_.parts), view(sgo_[:, 2:4, :].rearrange("p a b -> p (a b)"), sgo_.parts)])
            pend = None
            for bi, ti in enumerate(blk):
                th = rec_tile(cfg, l, ti, bi, 0, qT, kT, khT, vS, EBE, base[bi], post)
                if cfg["nvc"] == 2:
                    th()
                else:
                    if pend is not None:
                        pend()
                    pend = th
            if pend is not None:
                pend()

        for bidx, blk in enumerate(blocks):
            body(blk, bidx % 2)
        merge_overlapped(l, 1, Ps, Rs)
        DMA("sp", lambda e: e.dma_start(out=o_ph[l].rearrange("h d v -> d h v"), in_=S[:, :, 0:128]), [S], [])

    def phase_mlstm(l):
        S, Sb = S_all, Sb_all
        V(lambda e: e.memset(S[:, :, :], 0.0), [], [S])
        V(lambda e: e.memset(Sb[:, :, :], 0.0), [], [Sb])
        wq = load_w(l, C_MQ, 512)
        wk = load_w(l, C_MK, 512)
        wif = load_w(l, C_MI, 8, small=True)
        wv = load_w(l, C_MV, 512)
        wo = load_w(l, C_MO, 512)
        wg = load_w(l, C_MG, 512)
        m0T = ml_m0T
        DMA("sp", lambda e: ncdma(e, m0T[:, :], st_m[l].rearrange("(s h) -> h s", h=4)), [], [m0T])
        em0 = ml_em0
        DMA("sp", lambda e: e.dma_start(out=em0[:, :], in_=st_m[l].partition_broadcast(128)), [], [em0])
        A(lambda e: e.activation(out=em0[:, :], in_=em0[:, :], func=AF.Exp), [em0], [em0])
        n0 = tokp.get()
        DMA("sp", lambda e: e.dma_start(out=n0[0:64, 0:128], in_=st_n[l]), [], [n0])
        PE(lambda e: e.transpose(pat[:, 0:64], n0[0:64, 0:128], cf[0:64, K_ID:K_ID + 64]), [n0, cf], [pat])
        n0T = ml_n0T
        V(lambda e: e.tensor_tensor(out=n0T[:, :], in0=pat[:, 0:64], in1=em0[:, :], op=ALU.mult), [pat, em0], [n0T])
        nTo = ml_nTo
        emn = ml_emn
        emnp = ml_emnp
        mcar = ml_mcar
        V(lambda e: e.memset(mcar[:, :], 0.0), [], [mcar])
        msE = ml_msE
        Cout = {}

        def load_state(l, j, st):
            DMA("sp", lambda e: e.dma_start(out=st[:, :, 0:128], in_=st_C[l, j].rearrange("h d v -> d h v")), [], [st])
            V(lambda e: e.tensor_tensor(out=st[:, :, 0:128], in0=st[:, :, 0:128], in1=em0[:, 4 * j:4 * j + 4].unsqueeze(2).broadcast_to([128, 4, 128]), op=ALU.mult), [st, em0], [st])
            V(lambda e: e.tensor_copy(st[:, :, 128:256], n0T[:, 4 * j:4 * j + 4].unsqueeze(2).broadcast_to([128, 4, 128])), [n0T], [st])

        def store_state(l, j, so):
            V(lambda e: e.tensor_tensor(out=so[:, :, 0:128], in0=so[:, :, 0:128], in1=emn[:, :, j:j + 1].broadcast_to([128, 4, 128]), op=ALU.mult), [so, emn], [so])
            V(lambda e: e.tensor_tensor(out=nTo[:, 4 * j:4 * j + 4], in0=so[:, :, 128], in1=emn[:, :, j], op=ALU.mult), [so, emn], [nTo])
            DMA("sp", lambda e: e.dma_start(out=o_sC[l, j].rearrange("h d v -> d h v"), in_=so[:, :, 0:128]), [so], [])

        cfg = dict(dk=128, nvc=2, S=S, Sb=Sb, load_state=load_state, store_state=store_state, kv=pkv)
        psum_plan("rec2")
        Ps, Rs = [], []

        def body(blk, par):
            Ps.append([])
            Rs.append([])
            kb.rec = Ps[-1]
            c0 = blk[0] * 128
            nb = len(blk) * 128
            islast = blk[0] == NT - 1
            IG, LFn, LF, BP, A2, MR = [row4.get() for _ in range(6)]
            pst = pj.get()
            proj_fm(wif, 0, 4, c0, nb, pst)
            A(lambda e: e.activation(out=IG[0:4, 0:nb], in_=pst[0:4, 0:nb], func=AF.Identity, bias=pf[l][0:4, P_BI:P_BI + 1], scale=1.0), [pst, pf[l]], [IG])
            pst2 = pj.get()
            proj_fm(wif, 4, 4, c0, nb, pst2)
            A(lambda e: e.activation(out=LFn[0:4, 0:nb], in_=pst2[0:4, 0:nb], func=AF.Exp, bias=pf[l][0:4, P_NBF:P_NBF + 1], scale=-1.0), [pst2, pf[l]], [LFn])
            A(lambda e: e.activation(out=LFn[0:4, 0:nb], in_=LFn[0:4, 0:nb], func=AF.Ln, bias=one_c[0:4, :], scale=1.0), [LFn, cf], [LFn])
            V(lambda e: e.tensor_scalar(out=LF[0:4, 0:nb], in0=LFn[0:4, 0:nb], scalar1=-1.0, scalar2=None, op0=ALU.mult), [LFn], [LF])
            V(lambda e: e.tensor_tensor_scan(out=BP[0:4, 0:nb], data0=rpat(blk, nb)[0:4, :], data1=LFn[0:4, 0:nb], initial=0.0, op0=ALU.mult, op1=ALU.add), [LFn, cf], [BP])
            V(lambda e: e.tensor_tensor(out=A2[0:4, 0:nb], in0=IG[0:4, 0:nb], in1=BP[0:4, 0:nb], op=ALU.add), [IG, BP], [A2])
            segs = []
            if blk[0] == 0:
                segs = [(0, 16), (64, nb)]
            elif islast:
                segs = [(0, 64)]
            else:
                segs = [(0, nb)]
            for (a0, a1) in segs:
                V(lambda e, a0=a0, a1=a1: e.tensor_tensor_scan(out=MR[0:4, a0:a1], data0=LF[0:4, a0:a1], data1=IG[0:4, a0:a1], initial=mcar[0:4, 0:1], op0=ALU.add, op1=ALU.max), [LF, IG, mcar], [MR])
                V(lambda e, a1=a1: e.tensor_copy(mcar[0:4, 0:1], MR[0:4, a1 - 1:a1]), [MR], [mcar])
            if islast:
                DMA("sp", lambda e: e.dma_start(out=o_pm[l].rearrange("(h o) -> h o", o=1), in_=mcar[0:4, 0:1]), [mcar], [])
                pq1 = pj.get()
                for h in range(4):
                    PE(lambda e, h=h: e.matmul(pq1[:, h:h + 1], sel(h), mcar[0:4, 0:1], start=True, stop=True), [cf, mcar], [pq1])
                A(lambda e: e.activation(out=emnp[:, :], in_=pq1[:, 0:4], func=AF.Exp, scale=-1.0), [pq1], [emnp])
                cur = m0T
                for p in range(4):
                    tmp = ml_tmp.get()
                    V(lambda e, p=p, cur=cur, tmp=tmp: e.tensor_tensor(out=tmp[0:4, 0:NSEQ], in0=LF[0:4, 64 + p:128:4], in1=cur[0:4, 0:NSEQ], op=ALU.add), [LF, cur], [tmp])
                    V(lambda e, p=p, tmp=tmp: e.tensor_tensor(out=msE[0:4, 0:NSEQ], in0=tmp[0:4, 0:NSEQ], in1=IG[0:4, 64 + p:128:4], op=ALU.max), [tmp, IG], [msE])
                    cur = msE
                DMA("sp", lambda e: ncdma(e, o_sm[l].rearrange("s h -> h s"), msE[0:4, 0:NSEQ]), [msE], [])
                pq2 = pj.get()
                for h in range(4):
                    PE(lambda e, h=h: e.matmul(pq2[:, 64 + h * NSEQ:64 + (h + 1) * NSEQ], sel(h), msE[0:4, 0:NSEQ], start=True, stop=True), [cf, msE], [pq2])
                A(lambda e: e.activation(out=emn[:, :, :].rearrange("p a b -> p (a b)"), in_=pq2[:, 64:64 + 4 * NSEQ], func=AF.Exp, scale=-1.0), [pq2], [emn])
            eb, enb = ebp.get(), enbp.get()
            for h in range(4):
                pb = pj.get()
                PE(lambda e, h=h, pb=pb: e.matmul(pb[:, 0:nb], sel(h), BP[0:4, 0:nb], start=True, stop=True), [cf, BP], [pb])
                A(lambda e, h=h, pb=pb: e.activation(out=eb[:, h, 0:nb], in_=pb[:, 0:nb], func=AF.Exp, scale=-1.0), [pb], [eb])
                pb2 = pj.get()
                PE(lambda e, h=h, pb2=pb2: e.matmul(pb2[:, 0:nb], sel(h), A2[0:4, 0:nb], start=True, stop=True), [cf, A2], [pb2])
                A(lambda e, h=h, pb2=pb2: e.activation(out=enb[:, h, 0:nb], in_=pb2[:, 0:nb], func=AF.Exp), [pb2], [enb])
            qT, kT, khT = qTp.tiles[par], kTp.tiles[par], khTp.tiles[par]
            for h in range(4):
                pq = pj.get()
                proj_fm(wq, h * 128, 128, c0, nb, pq)
                V(lambda e, h=h, pq=pq: e.tensor_tensor(out=qT[:, h, 0:nb], in0=pq[:, 0:nb], in1=eb[:, h, 0:nb], op=ALU.mult), [pq, eb], [qT])
                pk = pj.get()
                proj_fm(wk, h * 128, 128, c0, nb, pk)
                V(lambda e, h=h, pk=pk: e.scalar_tensor_tensor(out=kT[:, h, 0:nb], in0=pk[:, 0:nb], scalar=128.0 ** -0.5, in1=enb[:, h, 0:nb], op0=ALU.mult, op1=ALU.mult), [pk, enb], [kT])
            EBE = EBEp.tiles[par]
            base = ebe_fill(EBE, eb, 128, blk, c0)
            khat(khT, kT, EBE, base, 128, blk)
            vS = vSp.tiles[par]
            v_proj(wv, blk, vS)
            sg, sgo = sgp.tiles[par], sgop.tiles[par]
            act_proj(wg, c0, nb, sg, AF.Silu)
            act_proj(wo, c0, nb, sgo, AF.Sigmoid)

            def post(ots, ti, bi, tc0):
                num, den = ots
                dd = f1p.get()
                ddf = dd[:, :, :].rearrange("p a b -> p (a b)")
                A(lambda e: e.activation(out=ddf, in_=den[:, :], func=AF.Abs), [den], [dd])
                V(lambda e: e.tensor_scalar_max(out=ddf, in0=ddf, scalar1=1.0), [dd], [dd])
                A(lambda e: e.activation(out=ddf, in_=ddf, func=AF.Ln), [dd], [dd])
                A(lambda e: e.activation(out=ddf, in_=ddf, func=AF.Exp, scale=-1.0), [dd], [dd])
                x = f2p.get()
                xf = x[:, :, :].rearrange("p a b -> p (a b)")
                V(lambda e: e.tensor_tensor(out=xf, in0=num[:, :], in1=ddf, op=ALU.mult), [num, dd], [x])
                V(lambda e: e.tensor_tensor(out=x[:, :, :], in0=x[:, :, :], in1=sgo[:, :, tc0:tc0 + 128], op=ALU.mult), [x, sgo], [x])
                xb = sqp.get()
                G(lambda e: e.tensor_copy(xb[:, :], xf), [x], [xb])
                PE(lambda e: e.matmul(pat[:, :], onesD[:, :], xb[:, :], start=True, stop=True), [onesD, xb], [pat])
                V(lambda e: e.tensor_tensor(out=xf, in0=xf, in1=pat[:, :], op=ALU.subtract), [x, pat], [x])
                sq = sqp.get()
                A(lambda e: e.activation(out=sq[:, :], in_=xf, func=AF.Square), [x], [sq])
                PE(lambda e: e.matmul(pat[:, :], onesD[:, :], sq[:, :], start=True, stop=True), [onesD, sq], [pat])
                rs = f1p.get()
                rsf = rs[:, :, :].rearrange("p a b -> p (a b)")
                rsqrt(rsf, pat[:, :], 128, [pat], [rs])
                V(lambda e: e.tensor_tensor(out=xf, in0=xf, in1=rsf, op=ALU.mult), [x, rs], [x])
                V(lambda e: e.tensor_tensor(out=x[:, :, :], in0=x[:, :, :], in1=pf[l][:, P_MLN:P_MLN + 4].unsqueeze(2).broadcast_to([128, 4, 128]), op=ALU.mult), [x, pf[l]], [x])
                V(lambda e: e.tensor_tensor(out=zT[:, :, ti * 128:(ti + 1) * 128], in0=x[:, :, :], in1=sg[:, :, tc0:tc0 + 128], op=ALU.mult), [x, sg], [zT.p(ti)])

            kb.rec = Rs[-1]
            op_ = 1 - par
            cfg["stgb"] = Pool([stgb.tiles[0], view(ycb.tiles[0][:, :, :], ycb.tiles[0].parts), qTp.tiles[op_], kTp.tiles[op_], khTp.tiles[op_], sgop.tiles[op_]])
            sgo_ = sgp.tiles[op_]
            cfg["khm"] = Pool([khmp.tiles[0], view(sgo_[:, 0:2, :].rearrange("p a b -> p (a b)"), sgo_.parts), view(sgo_[:, 2:4, :].rearrange("p a b -> p (a b)"), sgo_.parts)])
            pend = None
            for bi, ti in enumerate(blk):
                th = rec_tile(cfg, l, ti, bi, 0, qT, kT, khT, vS, EBE, base[bi], post)
                if cfg["nvc"] == 2:
                    th()
                else:
                    if pend is not None:
                        pend()
                    pend = th
            if pend is not None:
                pend()

        for bidx, blk in enumerate(blocks):
            body(blk, bidx % 2)
        merge_overlapped(l, 2, Ps, Rs)
        so = stg.get()
        V(lambda e: e.tensor_tensor(out=so[:, :, 0:128], in0=S[:, :, 0:128], in1=emnp[:, :].unsqueeze(2).broadcast_to([128, 4, 128]), op=ALU.mult), [S, emnp], [so])
        DMA("sp", lambda e: e.dma_start(out=o_pC[l].rearrange("h d v -> d h v"), in_=so[:, :, 0:128]), [so], [])
        npo = ml_npo
        V(lambda e: e.tensor_tensor(out=npo[:, :], in0=S[:, :, 128], in1=emnp[:, :], op=ALU.mult), [S, emnp], [npo])
        PE(lambda e: e.transpose(pat[0:4, 0:128], npo[:, 0:4], cf[:, K_ID:K_ID + 128]), [npo, cf], [pat])
        t4 = tokp.get()
        A(lambda e: e.activation(out=t4[0:4, 0:128], in_=pat[0:4, 0:128], func=AF.Copy), [pat], [t4])
        DMA("sp", lambda e: e.dma_start(out=o_pn[l], in_=t4[0:4, 0:128]), [t4], [])
        PE(lambda e: e.transpose(pat[0:64, 128:256], nTo[:, 0:64], cf[:, K_ID:K_ID + 128]), [nTo, cf], [pat])
        t5 = tokp.get()
        A(lambda e: e.activation(out=t5[0:64, 0:128], in_=pat[0:64, 128:256], func=AF.Copy), [pat], [t5])
        DMA("sp", lambda e: e.dma_start(out=o_sn[l], in_=t5[0:64, 0:128]), [t5], [])

    def phase_conv(l):
        psum_plan("stream")
        wa = load_w(l, C_CA, 512)
        wb = load_w(l, C_CB, 512)
        wg = load_w(l, C_CG, 512)
        V(lambda e: e.memset(ubf[:, :, 0:30], 0.0), [], [ubf])
        V(lambda e: e.tensor_copy(wcb[:, :, :], pf[l][:, P_CW:P_CW + 124].rearrange("p (a b) -> p a b", b=31)), [pf[l]], [wcb])
        for g4 in range(4):
            cv = cvin.get()
            DMA("sp", lambda e, cv=cv, g4=g4: e.dma_start(out=cv[0:120, :], in_=st_cv[l, 4 * g4:4 * g4 + 4].rearrange("s r c -> (s r) c")), [], [cv])
            for cc in range(4):
                PE(lambda e, cc=cc, cv=cv: e.transpose(pat[:, cc * 128:cc * 128 + 120], cv[0:120, cc * 128:(cc + 1) * 128], cf[0:120, K_ID:K_ID + 120]), [cv, cf], [pat])
            A(lambda e, g4=g4: e.activation(out=ucs[:, :, 4 * g4:4 * g4 + 4, 0:30], in_=pat[:, :].rearrange("p (a b) -> p a b", b=128)[:, :, 0:120].rearrange("p a (s r) -> p a s r", r=30),
                                            func=AF.Copy), [pat], [ucs])
        DMA("sp", lambda e: e.dma_start(out=o_scv[l, :, 0:26, :], in_=st_cv[l, :, 4:30, :]), [], [])
        ys = f2p.get()

        ubufs = [ubf, ubf2]
        V(lambda e: e.memset(ubf2[:, :, 0:30], 0.0), [], [ubf2])

        def stage1(cc, bidx, blk):
            ub = ubufs[bidx % 2]
            c0 = blk[0] * 128
            nb = len(blk) * 128
            islast = blk[0] == NT - 1
            if blk[0] == 0:
                segs = [(0, 16, 30), (64, nb, 46)]
                nreal = nb - 48
            elif islast:
                segs = [(0, 64, 30)]
                nreal = 64
            else:
                segs = [(0, nb, 30)]
                nreal = nb
            pa = pj.get()
            proj_fm(wa, cc * 128, 128, c0, nb, pa)
            pb = pj.get()
            proj_fm(wb, cc * 128, 128, c0, nb, pb)
            sgm = sgp.get()
            A(lambda e: e.activation(out=sgm[:, 0, 0:nb], in_=pb[:, 0:nb], func=AF.Sigmoid), [pb], [sgm])
            for (a0, a1, d0) in segs:
                V(lambda e, a0=a0, a1=a1, d0=d0: e.tensor_tensor(out=ub[:, cc, d0:d0 + a1 - a0], in0=pa[:, a0:a1], in1=sgm[:, 0, a0:a1], op=ALU.mult), [pa, sgm], [ub])
            if islast:
                V(lambda e: e.tensor_tensor(out=ucb[:, cc, 30:94], in0=pa[:, 0:64], in1=sgm[:, 0, 0:64], op=ALU.mult), [pa, sgm], [ucb])
                V(lambda e: e.tensor_tensor(out=ucs[:, cc, :, 30:34], in0=pa[:, 64:128].rearrange("p (s r) -> p s r", r=4),
                                            in1=sgm[:, 0, 64:128].rearrange("p (s r) -> p s r", r=4), op=ALU.mult), [pa, sgm], [ucs])
            return (cc, blk, ub, nreal, islast)

        def halo(cc, bidx, st):
            ub, nreal, islast = st[2], st[3], st[4]
            if not islast:
                nx = ubufs[(bidx + 1) % 2]
                V(lambda e: e.tensor_copy(nx[:, cc, 0:30], ub[:, cc, nreal:nreal + 30]), [ub], [nx])

        def stage2(st):
            cc, blk, ub, nreal, islast = st
            c0 = blk[0] * 128
            nb = len(blk) * 128
            zparts = [zT.p(t) for t in blk]
            pc = pj.get()
            for jt in range(31):
                PE(lambda e, jt=jt: e.matmul(pc[:, 0:nreal], dg[:, jt, :], ub[:, cc, jt:jt + nreal], start=(jt == 0), stop=(jt == 30)), [dg, ub], [pc])
            bcol = pf[l][:, P_CB + cc:P_CB + cc + 1]
            if blk[0] == 0:
                A(lambda e: e.activation(out=zT[:, cc, c0:c0 + 16], in_=pc[:, 0:16], func=AF.Identity, bias=bcol, scale=1.0), [pc, pf[l]], zparts)
                A(lambda e: e.activation(out=zT[:, cc, c0 + 64:c0 + nb], in_=pc[:, 16:nreal], func=AF.Identity, bias=bcol, scale=1.0), [pc, pf[l]], zparts)
            else:
                A(lambda e: e.activation(out=zT[:, cc, c0:c0 + nreal], in_=pc[:, 0:nreal], func=AF.Identity, bias=bcol, scale=1.0), [pc, pf[l]], zparts)

            def sample_taps():
                o3 = ys[:, cc, 0:64].rearrange("p (s r) -> p s r", r=4)
                for jt in range(31):
                    wcol = pf[l][:, P_CW + cc * 31 + jt:P_CW + cc * 31 + jt + 1]
                    if jt == 0:
                        V(lambda e, wcol=wcol: e.tensor_scalar(out=o3, in0=ucs[:, cc, :, 0:4], scalar1=wcol, scalar2=bcol, op0=ALU.mult, op1=ALU.add), [ucs, pf[l]], [ys])
                    else:
                        V(lambda e, wcol=wcol, jt=jt: e.scalar_tensor_tensor(out=o3, in0=ucs[:, cc, :, jt:jt + 4], scalar=wcol, in1=o3, op0=ALU.mult, op1=ALU.add), [ucs, pf[l], ys], [ys])
                V(lambda e: e.tensor_copy(zT[:, cc, c0 + 64:c0 + 128], ys[:, cc, 0:64]), [ys], zparts)

            return sample_taps if islast else None

        dg = dgA

        def build_dg(cc):
            V(lambda e: e.tensor_tensor(out=dg[:, :, :], in0=identb[:, :].unsqueeze(1).broadcast_to([128, 31, 128]),
                                        in1=wcb[:, cc, :].unsqueeze(2).broadcast_to([128, 31, 128]), op=ALU.mult), [identb, wcb], [dg])

        build_dg(0)
        for cc in range(4):
            taps = None
            pend = None
            for bidx, blk in enumerate(blocks):
                st = stage1(cc, bidx, blk)
                if pend is not None:
                    taps = stage2(pend) or taps
                halo(cc, bidx, st)
                pend = st
            taps = stage2(pend) or taps
            if cc < 3:
                build_dg(cc + 1)
            if taps is not None:
                taps()
        for cc in range(4):
            PE(lambda e, cc=cc: e.transpose(pat[0:64, cc * 128:(cc + 1) * 128], ucb[:, cc, 30:94], cf[:, K_ID:K_ID + 128]), [ucb, cf], [pat])
        tk = tokp.get()
        A(lambda e: e.activation(out=tk[0:64, :], in_=pat[0:64, :], func=AF.Copy), [pat], [tk])
        DMA("sp", lambda e: e.dma_start(out=o_pcv[l, :, :], in_=tk[34:64, :]), [tk], [])
        us = ys
        V(lambda e: e.tensor_copy(us[:, :, 0:64].rearrange("p a (s r) -> p a s r", r=4), ucs[:, :, :, 30:34]), [ucs], [us])
        for cc in range(4):
            PE(lambda e, cc=cc: e.transpose(pat[0:64, cc * 128:(cc + 1) * 128], us[:, cc, 0:64], cf[:, K_ID:K_ID + 128]), [us, cf], [pat])
        tk2 = tokp.get()
        A(lambda e: e.activation(out=tk2[0:64, :], in_=pat[0:64, :], func=AF.Copy), [pat], [tk2])
        DMA("sp", lambda e: e.dma_start(out=o_scv[l, :, 26:30, :], in_=tk2[0:64, :]), [tk2], [])

        def ln_block(blk):
            c0 = blk[0] * 128
            nb = len(blk) * 128
            zparts = [zT.p(t) for t in blk]
            yc = ycp.get()
            for cc in range(4):
                PE(lambda e, cc=cc: e.matmul(pat[:, 0:nb], onesC[:, :], zT[:, cc, c0:c0 + nb], start=(cc == 0), stop=(cc == 3)), [onesC] + zparts, [pat])
            V(lambda e: e.tensor_tensor(out=yc[:, :, 0:nb], in0=zT[:, :, c0:c0 + nb], in1=pat[:, 0:nb].unsqueeze(1).broadcast_to([128, 4, nb]), op=ALU.subtract), zparts + [pat], [yc])
            sg = sgp.get()
            act_proj(wg, c0, nb, sg, AF.Silu)
            yb = ycb.get()
            A(lambda e: e.activation(out=yb[:, :, 0:nb], in_=yc[:, :, 0:nb], func=AF.Square), [yc], [yb])
            for cc in range(4):
                PE(lambda e, cc=cc: e.matmul(pat[:, 0:nb], onesC[:, :], yb[:, cc, 0:nb], start=(cc == 0), stop=(cc == 3)), [onesC, yb], [pat])
            rs = f2p.get()
            rsf = rs[:, :, :].rearrange("p a b -> p (a b)")
            rsqrt(rsf[:, 0:nb], pat[:, 0:nb], 128, [pat], [rs])
            V(lambda e: e.tensor_tensor(out=yc[:, :, 0:nb], in0=yc[:, :, 0:nb], in1=rsf[:, 0:nb].unsqueeze(1).broadcast_to([128, 4, nb]), op=ALU.mult), [yc, rs], [yc])
            for cc in range(4):
                V(lambda e, cc=cc: e.tensor_scalar(out=yc[:, cc, 0:nb], in0=yc[:, cc, 0:nb], scalar1=pf[l][:, P_CG + cc:P_CG + cc + 1], scalar2=pf[l][:, P_CBT + cc:P_CBT + cc + 1], op0=ALU.mult, op1=ALU.add),
                  [yc, pf[l]], [yc])
            A(lambda e: e.activation(out=yc[:, :, 0:nb], in_=yc[:, :, 0:nb], func=AF.Silu), [yc], [yc])
            V(lambda e: e.tensor_tensor(out=zT[:, :, c0:c0 + nb], in0=yc[:, :, 0:nb], in1=sg[:, :, 0:nb], op=ALU.mult), [yc, sg], zparts)

        for blk in blocks:
            ln_block(blk)

    def merge_blocks(i, wts, blks, pgp, gtl, W=512):
        wg0, wg1, wb = wts

        def one(blk, oc):
            c0 = blk[0] * 128
            nb = len(blk) * 128
            wgt = wg0 if oc < 4 else wg1
            pg = pgp.get()
            proj_fm(wgt, (oc % 4) * 128, 128, c0, nb, pg)
            gt = gtl.get()
            A(lambda e: e.activation(out=gt[:, 0:nb], in_=pg[:, 0:nb], func=AF.Sigmoid), [pg], [gt])
            py = pgp.get()
            wbt = wb[oc // 4]
            for kc in range(4):
                PE(lambda e, kc=kc: e.matmul(py[:, 0:nb], wbt[:, kc, (oc % 4) * 128:(oc % 4 + 1) * 128], zT[:, kc, c0:c0 + nb], start=(kc == 0), stop=(kc == 3)),
                   [wbt] + [zT.p(t) for t in blk], [py])
            mparts = [mixT.p(t) for t in blk]
            if i == 0:
                V(lambda e: e.tensor_tensor(out=mixT[:, oc, c0:c0 + nb], in0=py[:, 0:nb], in1=gt[:, 0:nb], op=ALU.mult), [py, gt], mparts)
            else:
                V(lambda e: e.tensor_tensor(out=gt[:, W:W + nb], in0=py[:, 0:nb], in1=gt[:, 0:nb], op=ALU.mult), [py, gt], [gt])
                G(lambda e: e.tensor_tensor(out=mixT[:, oc, c0:c0 + nb], in0=mixT[:, oc, c0:c0 + nb], in1=gt[:, W:W + nb], op=ALU.add), [gt] + mparts, mparts)

        for blk in blks:
            for oc in range(8):
                one(blk, oc)

    def load_merge_w(l, i):
        return (load_w(l, C_GATE + i * 1024, 512), load_w(l, C_GATE + i * 1024 + 512, 512), load_wb(l, i))

    def phase_merge(l, i):
        wts = load_merge_w(l, i)
        psum_plan("stream")
        merge_blocks(i, wts, mblocks, pjm, gtp)

    def merge_overlapped(l, i, Ps, Rs):
        Mt = []
        kb.rec = []
        wts = load_merge_w(l, i)
        mpool = Pool(list(pj.tiles))
        for bi_, blk in enumerate(blocks[:-1]):
            if bi_ > 0:
                kb.rec = []
            merge_blocks(i, wts, [blk], mpool, gtx, 256)
            Mt.append((bi_, kb.rec))
        kb.pipeline(Ps, Rs, Mt)
        psum_plan("stream")
        merge_blocks(i, wts, blocks[-1:], pjm, gtp)

    def phase_out(l):
        psum_plan("stream")
        wo = load_wo(l)
        if l == DEPTH - 1:
            hst.tiles = [t for t in hst.tiles if t.parts != [r_eb]]
            DMA("sp", lambda e: e.dma_start(out=fnbt[:, :], in_=fnb[:, :]), [], [fnbt])
        tl = list(hst.tiles)
        hA, hB = Pool(tl[:4]), Pool(tl[4:])
        LA = 3
        hts = {}

        def issue(tj):
            t = hA.get()
            if l == 0:
                DMA("sp", lambda e: e.dma_start(out=t[:, :], in_=hin[tj * 128:(tj + 1) * 128, :]), [], [t])
            else:
                DMA("sp", lambda e: e.dma_start(out=t[:, :], in_=hscr[tj * 128:(tj + 1) * 128, :]), [hscr_res], [t])
            hts[tj] = t

        for tj in range(min(LA, NT)):
            issue(tj)
        for ti in range(NT):
            ht = hts.pop(ti)
            for hh in range(2):
                po = pj.get()
                for kc in range(8):
                    PE(lambda e, kc=kc, po=po: e.matmul(po[:, :], mixT[:, kc, ti * 128:(ti + 1) * 128], wo[hh][:, kc, :], start=(kc == 0), stop=(kc == 7)), [mixT.p(ti), wo[hh]], [po])
                V(lambda e, po=po: e.tensor_tensor(out=ht[:, hh * 512:(hh + 1) * 512], in0=ht[:, hh * 512:(hh + 1) * 512], in1=po[:, :], op=ALU.add), [ht, po], [ht])
            if l == 0:
                DMA("sp", lambda e: e.dma_start(out=hscr[ti * 128:(ti + 1) * 128, :], in_=ht[:, :]), [ht], [hscr_res])
                norm_tile(1, ht, ti, hB)
            else:
                sq = hB.get()
                A(lambda e: e.activation(out=sq[:, :], in_=ht[:, :], func=AF.Square), [ht], [sq])
                s1 = sm1.get()
                V(lambda e: e.reduce_sum(out=s1[:, 0:1], in_=sq[:, :], axis=AX.X), [sq], [s1])
                rsqrt(s1[:, 2:3], s1[:, 0:1], 128, [s1], [s1], scale=1.0 / D)
                V(lambda e: e.scalar_tensor_tensor(out=sq[:, :], in0=ht[:, :], scalar=s1[:, 2:3], in1=fnbt[:, :], op0=ALU.mult, op1=ALU.mult), [ht, s1, fnbt], [sq])
                if ti == 0:
                    DMA("sp", lambda e: e.dma_start(out=yp[0:64, :], in_=sq[64:128, :]), [sq], [])
                elif ti == NT - 1:
                    DMA("sp", lambda e: e.dma_start(out=yp[SEQ - 64:SEQ, :], in_=sq[0:64, :]), [sq], [])
                    DMA("sp", lambda e: e.dma_start(out=ys[:, :], in_=sq[64:128, :]), [sq], [])
                else:
                    DMA("sp", lambda e: e.dma_start(out=yp[ti * 128 - 64:ti * 128 + 64, :], in_=sq[:, :]), [sq], [])
            if ti + LA < NT:
                issue(ti + LA)

    lbl = pf[1][:, P_LBL:P_LBL + 8].rearrange("p (h l) -> p h l", l=2)
    V(lambda e: e.tensor_tensor(out=lbt[:, :], in0=lbl[:, :, 1], in1=lbl[:, :, 0], op=ALU.subtract), [pf[1]], [lbt])
    A(lambda e: e.activation(out=lbt[:, :], in_=lbt[:, :], func=AF.Sigmoid), [lbt], [lbt])

    tl0 = list(hst.tiles)
    hA0, hB0 = Pool(tl0[:4]), Pool(tl0[4:])
    hts0 = {}

    def issue0(tj):
        t = hA0.get()
        DMA("sp", lambda e: e.dma_start(out=t[:, :], in_=hin[tj * 128:(tj + 1) * 128, :]), [], [t])
        hts0[tj] = t

    for tj in range(min(3, NT)):
        issue0(tj)
    for ti in range(NT):
        norm_tile(0, hts0.pop(ti), ti, hB0)
        if ti + 3 < NT:
            issue0(ti + 3)
    for l in range(DEPTH):
        if l == 0:
            V(lambda e: e.memset(omlt[:, :], 1.0), [], [omlt])
        else:
            V(lambda e: e.tensor_scalar(out=omlt[:, :], in0=lbt[:, :], scalar1=-1.0, scalar2=1.0, op0=ALU.mult, op1=ALU.add), [lbt], [omlt])
        phase_gla(l)
        phase_hgrn(l)
        phase_mlstm(l)
        phase_conv(l)
        phase_merge(l, 3)
        phase_out(l)
    kb.finish()
    es.close()
    return nc, kb


def make_consts(NT):
    c = np.zeros((128, CW), np.float32)
    c[:, K_ID:K_ID + 128] = np.eye(128, dtype=np.float32)
    s = np.arange(128)[:, None]
    t = np.arange(128)[None, :]
    same = (s // 64) == (t // 64)
    mid = (same & (s <= t)).astype(np.float32)
    first = mid.copy()
    first[16:64, :] = 0
    first[:, 16:64] = 0
    last = mid.copy()
    blk4 = ((s // 4) == (t // 4)) & (s <= t)
    last[64:, :] = 0
    last[:, 64:] = 0
    last[64:, 64:] = blk4[64:, 64:].astype(np.float32)
    c[:, K_MASK:K_MASK + 128] = first
    c[:, K_MASK + 128:K_MASK + 256] = mid
    c[:, K_MASK + 256:K_MASK + 384] = last
    r = np.ones(256, np.float32)
    r[::64] = 0
    c[:, K_R:K_R + 256] = r[None]
    rl = np.ones(128, np.float32)
    rl[0] = 0
    rl[64::4] = 0
    c[:, K_R + 256:K_R + 384] = rl[None]
    for j in range(NSEQ):
        c[64 + 4 * j:64 + 4 * j + 4, K_OH + j] = 1.0
    for h in range(4):
        c[h, K_SEL + h * 128:K_SEL + (h + 1) * 128] = 1.0
    c[:, K_ONE] = 1.0
    return c


def make_par(inp, l):
    p = np.zeros((128, PW), np.float32)
    g = lambda k: np.asarray(inp[k], np.float32)
    p[:, P_NORMW:P_NORMW + 8] = g("norm_w")[l].reshape(8, 128).T
    p[:, P_FNORM:P_FNORM + 8] = g("final_norm").reshape(8, 128).T
    p[0:64, P_NBGK:P_NBGK + 4] = -g("gla_bgk")[l].reshape(4, 64).T
    p[:, P_GNORM] = g("gla_norm")[l]
    p[:, P_HNORM] = g("hg_norm")[l]
    lbl = g("hg_lb_logits")
    p[:, P_LBL:P_LBL + 8] = lbl.reshape(2, 4, 128).transpose(2, 1, 0).reshape(128, 8)
    p[0:4, P_BI] = g("ml_bi")[l]
    p[0:4, P_NBF] = -g("ml_bf")[l]
    p[:, P_MLN:P_MLN + 4] = g("ml_norm")[l].reshape(4, 128).T
    p[:, P_CW:P_CW + 124] = g("conv_w")[l].reshape(31, 4, 128).transpose(2, 1, 0).reshape(128, 124)
    p[:, P_CB:P_CB + 4] = g("conv_b")[l].reshape(4, 128).T
    p[:, P_CG:P_CG + 4] = g("conv_ln_g")[l].reshape(4, 128).T
    p[:, P_CBT:P_CBT + 4] = g("conv_ln_b")[l].reshape(4, 128).T
    p[0:16, P_WGK:P_WGK + 256] = g("gla_wgk2")[l]
    return p


_CACHE = {}


def kernel(**inp):
    x_prompt = np.asarray(inp["x_prompt"], np.float32)
    x_sample = np.asarray(inp["x_sample"], np.float32)
    B, SEQ, _ = x_prompt.shape
    ncores = B
    NT = (SEQ + 64) // 128 + 1
    if NT not in _CACHE:
        _CACHE[NT] = build(NT)
    nc, kb = _CACHE[NT]
    NCOL = NT * 128
    cst = make_consts(NT)
    par = np.stack([make_par(inp, l) for l in range(DEPTH)])
    fnb = np.ascontiguousarray(np.broadcast_to(np.asarray(inp["final_norm"], np.float32)[None, :], (128, D)))
    meta = np.asarray(inp["meta_tokens"], np.float32)
    f = lambda k: np.ascontiguousarray(np.asarray(inp[k], np.float32))
    w_in, w_br, w_o = f("w_in"), f("w_branch"), f("w_out")
    sg, sh, sC, sn, sm, scv = f("state_gla"), f("state_hgrn"), f("state_mlstm_C"), f("state_mlstm_n"), f("state_mlstm_m"), f("state_conv")
    in_maps = []
    for c in range(ncores):
        hin = np.zeros((NCOL, D), np.float32)
        hin[0:16] = meta
        hin[64:64 + SEQ] = x_prompt[c]
        hin[NCOL - 64:] = x_sample[NSEQ * c:NSEQ * (c + 1)].reshape(64, D)
        sl = slice(NSEQ * c, NSEQ * (c + 1))
        in_maps.append({
            "hin": hin, "w_in": w_in, "w_branch": w_br, "w_out": w_o, "par": par, "cst": cst, "fnb": fnb,
            "st_gla": np.ascontiguousarray(sg[:, sl]), "st_hg": np.ascontiguousarray(sh[:, sl]),
            "st_C": np.ascontiguousarray(sC[:, sl]), "st_n": np.ascontiguousarray(sn[:, sl].reshape(DEPTH, NSEQ * 4, 128)),
            "st_m": np.ascontiguousarray(sm[:, sl].reshape(DEPTH, NSEQ * 4)), "st_cv": np.ascontiguousarray(scv[:, sl]),
        })
    res = run_bass_kernel_spmd(nc, in_maps, core_ids=list(range(ncores))).results
    cat = lambda k, ax: np.concatenate([np.asarray(r[k], np.float32) for r in res], axis=ax)
    stk = lambda k: np.stack([np.asarray(r[k], np.float32) for r in res], axis=1)
    y_prompt = np.stack([np.asarray(r["yp"], np.float32) for r in res], axis=0)
    y_sample = cat("ys", 0).reshape(x_sample.shape)
    nS = NSEQ * ncores
    return (y_prompt, y_sample,
            stk("p_gla"), stk("p_hg"), stk("p_C"), stk("p_n"), stk("p_m"), stk("p_cv"),
            cat("s_gla", 1), cat("s_hg", 1), cat("s_C", 1), cat("s_n", 1).reshape(DEPTH, nS, 4, 128),
            cat("s_m", 1), cat("s_cv", 1))

_LINE_COST = {634: 1.065, 671: 1.008, 673: 1.175, 573: 0.538, 574: 0.479, 676: 0.753, 678: 0.126, 680: 1.225, 660: 0.143, 860: 0.454, 864: 0.211, 865: 0.433, 866: 1.1, 869: 0.632, 872: 1.009, 873: 0.899, 878: 0.34, 881: 0.361, 665: 0.241, 784: 0.124, 785: 0.169, 809: 0.67, 801: 0.676, 815: 0.362, 694: 0.067, 699: 0.077, 696: 0.531, 703: 0.662, 708: 0.095, 736: 0.053, 748: 0.104, 757: 0.387, 759: 0.823, 821: 0.49, 826: 0.691, 822: 0.493, 827: 0.752, 642: 0.908, 779: 0.165, 780: 0.693, 795: 0.295, 797: 0.215, 1332: 0.153, 1328: 0.382, 1336: 0.389, 745: 0.342, 733: 0.859, 763: 0.389, 945: 0.328, 946: 0.262, 947: 0.383, 948: 1.009, 950: 0.636, 951: 0.909, 952: 0.899, 954: 0.45, 959: 0.373, 960: 1.166, 1338: 0.436, 1339: 0.728, 1044: 0.453, 1047: 0.454, 1048: 0.399, 1049: 0.274, 1050: 0.57, 1051: 0.318, 1084: 0.531, 1061: 0.566, 1062: 0.141, 1087: 0.453, 1085: 0.324, 1088: 0.348, 1093: 0.348, 1096: 0.363, 751: 0.072, 1110: 0.485, 1111: 0.425, 1112: 0.629, 1113: 0.63, 1116: 0.692, 1117: 0.693, 1119: 1.873, 1121: 0.675, 1123: 0.629, 1124: 0.418, 1128: 0.69, 1129: 0.693, 1130: 0.692, 1120: 0.372, 1067: 0.229, 1073: 0.473, 1074: 0.174, 1079: 0.254, 1022: 0.687, 1023: 0.427, 1026: 0.692, 1027: 0.165, 1258: 4.293, 1181: 0.259, 1182: 0.662, 1211: 0.372, 1213: 0.373, 1225: 0.18, 1235: 0.108, 1238: 0.274, 1239: 0.421, 1241: 0.455, 1215: 0.132, 1216: 0.131, 1248: 0.328, 1250: 0.284, 1251: 0.196, 1278: 0.112, 1285: 0.258, 1296: 0.231, 1297: 1.162, 1301: 0.996, 1303: 0.21, 1307: 1.166, 1309: 0.405, 1311: 1.009, 1312: 1.166, 651: 1.067, 1395: 0.294, 1396: 0.673, 1402: 1.057, 1404: 1.22, 1406: 1.284}
```

```python
import numpy as np
from contextlib import ExitStack
import concourse.bass as bass
import concourse.mybir as mybir
from concourse.bass_utils import run_bass_kernel_spmd

F32 = mybir.dt.float32
BF16 = mybir.dt.bfloat16
AF = mybir.ActivationFunctionType
ALU = mybir.AluOpType
AX = mybir.AxisListType

D = 1024
DEPTH = 2
EPS = 1e-6
D_IN = 11800
C_GQ, C_GK, C_GV, C_GLR, C_GG = 0, 256, 512, 1024, 1040
C_HQ, C_HF, C_HI, C_HG = 1552, 2064, 2576, 3088
C_MQ, C_MK, C_MV, C_MI, C_MF, C_MO, C_MG = 3600, 4112, 4624, 5136, 5140, 5144, 5656
C_CA, C_CB, C_CG = 6168, 6680, 7192
C_GATE = 7704
NSEQ = 16
KD = 16

P_NORMW = 0
P_FNORM = 8
P_NBGK = 16
P_GNORM = 20
P_HNORM = 21
P_LBL = 22
P_BI = 30
P_NBF = 31
P_MLN = 32
P_CW = 36
P_CB = 160
P_CG = 164
P_CBT = 168
P_WGK = 172
PW = 428

K_ID = 0
K_R = 128
K_OH = 512
K_SEL = 528
K_ONE = 1040
CWS = 1041
K_MASK = 1041
CW = 1425


class Res:
    __slots__ = ("w", "r")

    def __init__(self):
        self.w = None
        self.r = {}


class Tl:
    def __init__(self, t, nparts=1):
        self.t = t
        self.parts = [Res() for _ in range(nparts)]

    def __getitem__(self, idx):
        return self.t[idx]

    def p(self, i):
        return self.parts[i]


def _expand(lst):
    out = []
    for x in lst:
        if isinstance(x, Tl):
            out.extend(x.parts)
        elif isinstance(x, (list, tuple)):
            out.extend(_expand(x))
        else:
            out.append(x)
    return out


class KB:
    def __init__(self, nc, es):
        self.nc = nc
        self.eng = {"pe": nc.tensor, "act": nc.scalar, "dve": nc.vector, "pool": nc.gpsimd, "sp": nc.sync}
        self.sem = {}
        for e in ("pe", "act", "dve", "pool"):
            self.sem[e] = es.enter_context(nc.semaphore("s_" + e))
        for q in ("sp", "pool"):
            for i in range(KD):
                self.sem[("d", q, i)] = es.enter_context(nc.semaphore("d%s%d" % (q, i)))
        self.cnt = {e: 0 for e in ("pe", "act", "dve", "pool")}
        self.seen = {e: {} for e in self.eng}
        self.rec = None
        self.slack = 0.0
        self.rbias = 0.0
        self.dma_i = 0
        self.dma_qi = {"sp": 0, "pool": 0}
        self.dma_val = {(q, i): 0 for q in ("sp", "pool") for i in range(KD)}
        self.nins = 0

    def _waits(self, eng, reads, writes):
        need = {}

        def add(ev):
            if ev is None:
                return
            k, v = ev
            if need.get(k, 0) < v:
                need[k] = v

        for r in reads:
            add(r.w)
        for w in writes:
            if not (eng == "pe" and w.w is not None and w.w[0] == "pe"):
                add(w.w)
            for k, v in w.r.items():
                add((k, v))
        e = self.eng[eng]
        seen = self.seen[eng]
        for k, v in need.items():
            if seen.get(k, 0) >= v:
                continue
            e.wait_ge(self.sem[k], v)
            seen[k] = v

    def op(self, eng, fn, reads=(), writes=()):
        if self.rec is not None:
            self.rec.append((0, eng, fn, reads, writes))
            return None
        reads = _expand(reads)
        writes = _expand(writes)
        self._waits(eng, reads, writes)
        ins = fn(self.eng[eng])
        self.cnt[eng] += 1
        ev = (eng, self.cnt[eng])
        ins.then_inc(self.sem[eng], 1)
        self.nins += 1
        for r in reads:
            if r.r.get(eng, 0) < ev[1]:
                r.r[eng] = ev[1]
        for w in writes:
            w.w = ev
            w.r = {}
        return ins

    def dma(self, q, fn, reads=(), writes=()):
        if self.rec is not None:
            self.rec.append((1, q, fn, reads, writes))
            return None
        reads = _expand(reads)
        writes = _expand(writes)
        self._waits(q, reads, writes)
        s = (q, self.dma_qi[q] % KD)
        self.dma_qi[q] += 1
        self.dma_i += 1
        k = ("d",) + s
        e = self.eng[q]
        if self.dma_val[s] > 0 and self.seen[q].get(k, 0) < self.dma_val[s]:
            e.wait_ge(self.sem[k], self.dma_val[s])
            self.seen[q][k] = self.dma_val[s]
        ins = fn(e)
        self.dma_val[s] += 16
        ev = (k, self.dma_val[s])
        ins.then_inc(self.sem[k], 16)
        self.nins += 1
        for r in reads:
            if r.r.get(k, 0) < ev[1]:
                r.r[k] = ev[1]
        for w in writes:
            w.w = ev
            w.r = {}
        return ins

    COST = {"pe": 0.13, "act": 0.6, "dve": 0.6, "pool": 1.0, "sp": 0.0}

    def emit(self, ops):
        assert self.rec is None
        for k, eng, fn, r, w in ops:
            (self.dma if k else self.op)(eng, fn, r, w)

    def emit2(self, a, b):
        ca = sum(self.COST[o[1]] for o in a) + 1e-9
        cb = sum(self.COST[o[1]] for o in b) + 1e-9
        i = j = 0
        xa = xb = 0.0
        while i < len(a) or j < len(b):
            if j >= len(b) or (i < len(a) and xa / ca <= xb / cb):
                o = a[i]; i += 1; xa += self.COST[o[1]]
            else:
                o = b[j]; j += 1; xb += self.COST[o[1]]
            (self.dma if o[0] else self.op)(o[1], o[2], o[3], o[4])

    def pipeline(self, Ps, Rs, Mt=None):
        self.rec = None
        n = len(Ps)
        pos = {"P": [0, 0], "R": [0, 0]}
        segs = {"P": Ps, "R": Rs}
        Mt = Mt or []
        mflat = [(b, o) for (b, ops) in Mt for o in ops]
        mi = [0]

        def head_m():
            if mi[0] >= len(mflat):
                return None
            head("P")
            head("R")
            b, o = mflat[mi[0]]
            if pos["P"][0] < n or pos["R"][0] <= b:
                return None
            return o

        if not hasattr(self, "efree"):
            self.efree = {}
            self.rt = {}

        def head(k):
            sg, ix = pos[k]
            while sg < n and ix >= len(segs[k][sg]):
                sg += 1
                ix = 0
            pos[k] = [sg, ix]
            if sg >= n:
                return None
            if k == "R" and pos["P"][0] <= sg:
                return None
            if k == "P" and sg >= 2 and pos["R"][0] <= sg - 2:
                return None
            return segs[k][sg][ix]

        def est(o):
            kind, eng, fn, r, w = o
            rr, ww = _expand(r), _expand(w)
            t = 0.0
            for x in rr:
                t = max(t, self.rt.get(id(x), (0.0, 0.0))[0])
            for x in ww:
                a = self.rt.get(id(x), (0.0, 0.0))
                t = max(t, a[0], a[1])
            return max(self.efree.get(eng, 0.0), t + self.slack), rr, ww

        while True:
            hp, hr, hm = head("P"), head("R"), head_m()
            if hp is None and hr is None and hm is None:
                if pos["P"][0] >= n and pos["R"][0] >= n and mi[0] >= len(mflat):
                    break
                raise RuntimeError("pipeline deadlock")
            cand = []
            if hr is not None:
                cand.append(("R", hr) + est(hr))
            if hp is not None:
                cand.append(("P", hp) + est(hp))
            if hm is not None:
                cand.append(("M", hm) + est(hm))
            cand.sort(key=lambda c: c[2] - (self.rbias if c[0] == "R" else 0.0))
            k, o, st, rr, ww = cand[0]
            if k == "M":
                mi[0] += 1
            else:
                pos[k][1] += 1
            dur = _cost(o)
            kind, eng = o[0], o[1]
            if kind:
                self.efree[eng] = st + 0.05
                end = st + 1.5
            else:
                end = st + dur
                self.efree[eng] = end
            for x in rr:
                a = self.rt.setdefault(id(x), [0.0, 0.0])
                a[1] = max(a[1], end)
            for x in ww:
                self.rt[id(x)] = [end, 0.0]
            (self.dma if kind else self.op)(o[1], o[2], o[3], o[4])

    def finish(self):
        e = self.eng["sp"]
        for s, v in self.dma_val.items():
            if v > 0:
                e.wait_ge(self.sem[("d",) + s], v)
        for en in ("pe", "act", "dve", "pool"):
            if self.cnt[en] > 0:
                e.wait_ge(self.sem[en], self.cnt[en])


class _Dummy:
    def then_inc(self, *a, **k):
        return self


class _Probe:
    def __init__(self):
        self.out = None

    def __getattr__(self, name):
        def f(*a, **k):
            o = k.get("out", a[0] if a else None)
            if o is None:
                o = k.get("ap", None)
            self.out = o
            return _Dummy()
        return f


def _cost(o):
    kind, eng, fn = o[0], o[1], o[2]
    if kind:
        return 0.05
    try:
        c = _LINE_COST.get(fn.__code__.co_firstlineno)
        if c is not None:
            return c
    except Exception:
        pass
    p = _Probe()
    try:
        fn(p)
        shp = list(p.out.shape)
        n = 1
        for d in shp[1:]:
            n *= int(d)
    except Exception:
        n = 256
    if eng == "pe":
        return 0.03 + n * 0.00047
    if eng == "act":
        return 0.22 + n * 0.00095
    if eng == "dve":
        return 0.12 + n * 0.0011
    return 0.3 + n * 0.0022


class Pool:
    def __init__(self, tiles):
        self.tiles = tiles
        self.i = 0

    def get(self):
        t = self.tiles[self.i % len(self.tiles)]
        self.i += 1
        return t


def build(NT):
    NCOL = NT * 128
    NPR = 64 * (NT * 2 - 2) + 16
    SEQ = NPR - 16
    nc = bass.Bass("TRN2", target_bir_lowering=False)
    es = ExitStack()

    def din(name, shape):
        return nc.dram_tensor(name, shape, F32, kind="ExternalInput").ap()

    def dout(name, shape):
        return nc.dram_tensor(name, shape, F32, kind="ExternalOutput").ap()

    hin = din("hin", [NCOL, D])
    w_in = din("w_in", [DEPTH, D, D_IN])
    w_br = din("w_branch", [DEPTH, 4, 512, D])
    w_out = din("w_out", [DEPTH, D, D])
    par = din("par", [DEPTH, 128, PW])
    cst = din("cst", [128, CW])
    fnb = din("fnb", [128, D])
    st_gla = din("st_gla", [DEPTH, NSEQ, 4, 64, 128])
    st_hg = din("st_hg", [DEPTH, NSEQ, 4, 128, 128])
    st_C = din("st_C", [DEPTH, NSEQ, 4, 128, 128])
    st_n = din("st_n", [DEPTH, NSEQ * 4, 128])
    st_m = din("st_m", [DEPTH, NSEQ * 4])
    st_cv = din("st_cv", [DEPTH, NSEQ, 30, 512])

    yp = dout("yp", [SEQ, D])
    ys = dout("ys", [64, D])
    o_pg = dout("p_gla", [DEPTH, 4, 64, 128])
    o_ph = dout("p_hg", [DEPTH, 4, 128, 128])
    o_pC = dout("p_C", [DEPTH, 4, 128, 128])
    o_pn = dout("p_n", [DEPTH, 4, 128])
    o_pm = dout("p_m", [DEPTH, 4])
    o_pcv = dout("p_cv", [DEPTH, 30, 512])
    o_sg = dout("s_gla", [DEPTH, NSEQ, 4, 64, 128])
    o_sh = dout("s_hg", [DEPTH, NSEQ, 4, 128, 128])
    o_sC = dout("s_C", [DEPTH, NSEQ, 4, 128, 128])
    o_sn = dout("s_n", [DEPTH, NSEQ * 4, 128])
    o_sm = dout("s_m", [DEPTH, NSEQ, 4])
    o_scv = dout("s_cv", [DEPTH, NSEQ, 30, 512])
    hscr = nc.dram_tensor("hscr", [NCOL, D], F32, kind="Internal").ap()
    hscr_res = Res()

    kb = KB(nc, es)
    cnt = [0]

    def sb(shape, dt, nparts=1, name=None):
        cnt[0] += 1
        t = es.enter_context(nc.sbuf_tensor("%s%d" % (name or "t", cnt[0]), list(shape), dt))
        return Tl(t, nparts)

    def ps(shape, dt, name=None):
        cnt[0] += 1
        t = es.enter_context(nc.psum_tensor("%s%d" % (name or "p", cnt[0]), list(shape), dt))
        return Tl(t)

    def pool(shape, dt, bufs, nparts=1, name=None):
        return Pool([sb(shape, dt, nparts, name) for _ in range(bufs)])

    xnT = sb([128, 8, NCOL], BF16, NT, "xnT")
    mixT = sb([128, 8, NCOL], BF16, NT, "mixT")
    zT = sb([128, 4, NCOL], BF16, NT, "zT")
    NSLOT = 5
    wslots = pool([128, 8, 512], BF16, NSLOT, name="w")
    wsm = pool([128, 8, 16], BF16, 2, name="wsm")
    cf = sb([128, CWS], F32, name="cf")
    identb = sb([128, 128], BF16)
    maskb = sb([128, 3, 128], BF16)
    onesb = sb([128, 128], BF16)
    onesD = sb([128, 128], BF16)
    onesC = sb([128, 128], BF16)
    pf = [sb([128, P_WGK], F32, name="pf") for _ in range(DEPTH)]
    wgkb = [sb([16, 256], BF16) for _ in range(DEPTH)]
    lbt = sb([128, 4], F32)
    omlt = sb([128, 4], F32)
    zero4 = sb([128, 4], F32)
    epsc = sb([128, 1], F32)

    pjb = [ps([128, 512], F32, "pj") for _ in range(2)]
    pat = ps([128, 512], F32, "at")
    potb = [ps([128, 512], F32, "ot") for _ in range(2)]
    pkv_t = ps([128, 1024], F32, "kv")
    ptr = ps([128, 1024], BF16, "tr")
    pj = Pool(list(pjb))
    pot = Pool(list(potb))

    def view(ap, parts):
        t = Tl(ap, 1)
        t.parts = parts
        return t

    sm1 = pool([128, 4], F32, 4, name="sm")
    NBM = 256
    gA_t = sb([128, 4, NBM], F32, name="gA")
    gB_t = sb([128, 4, NBM], F32, name="gB")
    gAp, gBp = Pool([gA_t]), Pool([gB_t])
    hflat = [view(t[:, :, :].rearrange("p a b -> p (a b)"), t.parts) for t in (gA_t, gB_t)]
    hst = Pool(hflat)
    ar1 = sb([128, 2240], F32, name="ar1")
    r_eb, r_enb = Res(), Res()
    eb_t = view(ar1[:, 0:1024].rearrange("p (a b) -> p a b", b=NBM), [r_eb])
    enb_t = view(ar1[:, 1024:2048].rearrange("p (a b) -> p a b", b=NBM), [r_enb])
    ebp, enbp = Pool([eb_t]), Pool([enb_t])
    stg = Pool([gA_t, gB_t, eb_t, enb_t])
    hst.tiles += [view(ar1[:, 0:1024], [r_eb]), view(ar1[:, 1024:2048], [r_enb])]
    ucs = view(ar1[:, 0:2176].rearrange("p (a s r) -> p a s r", s=NSEQ, r=34), [r_eb, r_enb])
    fnbt = view(ar1[:, 0:1024], [r_eb, r_enb])
    ar2 = sb([128, 2560], F32, name="ar2")
    ar2b = sb([128, 2560], F32, name="ar2b")
    r2 = [Res() for _ in range(5)]
    r2b = [Res() for _ in range(5)]

    def bfv(i, b, ar=None):
        ar = ar2 if ar is None else ar
        return ar[:, 512 * i:512 * (i + 1)].bitcast(BF16).rearrange("p (a b) -> p a b", b=b)

    qTp = Pool([view(bfv(0, NBM), [r2[0]]), view(bfv(0, NBM, ar2b), [r2b[0]])])
    kTp = Pool([view(bfv(1, NBM), [r2[1]]), view(bfv(1, NBM, ar2b), [r2b[1]])])
    khTp = Pool([view(bfv(2, NBM), [r2[2]]), view(bfv(2, NBM, ar2b), [r2b[2]])])
    vSp = Pool([view(bfv(3, 512), [r2[3]]), view(bfv(3, 512, ar2b), [r2b[3]])])
    sgop = Pool([view(bfv(4, NBM), [r2[4]]), view(bfv(4, NBM, ar2b), [r2b[4]])])
    ucb = view(ar2[:, 0:376].rearrange("p (a b) -> p a b", b=94), [r2[0]])
    ycp = Pool([view(ar2[:, 1144:2168].rearrange("p (a b) -> p a b", b=NBM), r2[2:5])])
    sgp = pool([128, 4, NBM], BF16, 2)
    hst.tiles += [view(ar2[:, 0:1024], r2[0:2]), view(ar2[:, 1024:2048], r2[2:4]),
                  view(ar2b[:, 0:1024], r2b[0:2]), view(ar2b[:, 1024:2048], r2b[2:4])]
    dgA = view(ar2b[:, 576:2560].bitcast(BF16).rearrange("p (a b) -> p a b", b=128), r2b[1:5])
    dgB = view(ar1[:, 0:1984].bitcast(BF16).rearrange("p (a b) -> p a b", b=128), [r_eb, r_enb])
    ubf = view(ar2b[:, 0:572].bitcast(BF16).rearrange("p (a b) -> p a b", b=30 + NBM), r2b[0:2])
    ubf2 = view(ar2[:, 384:956].bitcast(BF16).rearrange("p (a b) -> p a b", b=30 + NBM), r2[0:2])
    wcb = sb([128, 4, 31], BF16)
    EBEp = pool([128, 4, 20], F32, 2)
    gtx_t = sb([128, 1024], BF16, name="gtx")
    khp = pool([128, 512], BF16, 2)
    khmp = pool([128, 512], BF16, 1)
    attp = pool([128, 4, 128], BF16, 2)
    sqp = pool([128, 512], BF16, 2)
    f1p = pool([128, 4, 128], F32, 1)
    f2p = pool([128, 4, 128], F32, 1)
    stgb = pool([128, 4, 256], BF16, 1)
    row4 = pool([4, NBM], F32, 6, name="row4")
    ml_tmp = pool([4, NSEQ], F32, 2)
    glrp = pool([16, NBM], BF16, 1)
    ycb = pool([128, 4, NBM], BF16, 1)
    gtp = Pool([view(ycb.tiles[0][:, :, :].rearrange("p a b -> p (a b)"), ycb.tiles[0].parts),
                view(f1p.tiles[0][:, :, :].rearrange("p a b -> p (a b)").bitcast(BF16), f1p.tiles[0].parts),
                view(f2p.tiles[0][:, :, :].rearrange("p a b -> p (a b)").bitcast(BF16), f2p.tiles[0].parts)])
    r_kv = [Res(), Res()]
    kvA = view(pkv_t[:, 0:512], [r_kv[0]])
    kvB = view(pkv_t[:, 512:1024], [r_kv[1]])
    pkv = view(pkv_t[:, :], r_kv)
    pjm = Pool(pjb + [pat] + potb + [kvA, kvB])

    def psum_plan(kind):
        if kind == "rec1":
            pj.tiles = pjb + [kvB]
            pot.tiles = list(potb)
        elif kind == "rec2":
            pj.tiles = list(pjb)
            pot.tiles = list(potb)
        else:
            pj.tiles = pjb + potb + [kvA, kvB]
            pot.tiles = list(potb)
        pj.i = 0
        pot.i = 0

    hbf = Pool(list(gtp.tiles))
    _rg = [Res(), Res()]
    gtx = Pool([view(gtx_t[:, 0:512], [_rg[0]]), view(gtx_t[:, 512:1024], [_rg[1]])])
    tokp = Pool([view(gtx_t[:, :].bitcast(F32), _rg)])
    cvin = tokp
    S_all = sb([128, 4, 256], F32, name="S")
    Sb_all = sb([128, 4, 256], BF16, name="Sb")
    ml_m0T = sb([4, NSEQ], F32)
    ml_em0 = sb([128, NSEQ * 4], F32)
    ml_n0T = sb([128, NSEQ * 4], F32)
    ml_nTo = sb([128, NSEQ * 4], F32)
    ml_emn = sb([128, 4, NSEQ], F32)
    ml_emnp = sb([128, 4], F32)
    ml_mcar = sb([4, 1], F32)
    ml_msE = sb([4, NSEQ], F32)
    ml_npo = sb([128, 4], F32)

    def V(fn, r, w):
        return kb.op("dve", fn, r, w)

    def A(fn, r, w):
        return kb.op("act", fn, r, w)

    def G(fn, r, w):
        return kb.op("pool", fn, r, w)

    def PE(fn, r, w):
        return kb.op("pe", fn, r, w)

    def DMA(q, fn, r, w):
        return kb.dma(q, fn, r, w)

    def ncdma(e, out, in_):
        with nc.allow_non_contiguous_dma(reason="tiny"):
            return e.dma_start(out=out, in_=in_)

    def rsqrt(out_ap, in_ap, np_, r, w, scale=1.0):
        A(lambda e: e.activation(out=out_ap, in_=in_ap, func=AF.Ln, bias=epsc[0:np_, 0:1], scale=scale), list(r) + [epsc], w)
        A(lambda e: e.activation(out=out_ap, in_=out_ap, func=AF.Exp, scale=-0.5), w, w)

    DMA("sp", lambda e: e.dma_start(out=cf[:, :], in_=cst[:, 0:CWS]), [], [cf])
    for l in range(DEPTH):
        DMA("sp", lambda e, l=l: e.dma_start(out=pf[l][:, :], in_=par[l, :, 0:P_WGK]), [], [pf[l]])
    V(lambda e: e.tensor_copy(identb[:, :], cf[:, K_ID:K_ID + 128]), [cf], [identb])
    tm = tokp.get()
    DMA("sp", lambda e: e.dma_start(out=tm[:, 0:384], in_=cst[:, K_MASK:K_MASK + 384]), [], [tm])
    V(lambda e: e.tensor_copy(maskb[:, :, :], tm[:, 0:384].rearrange("p (a b) -> p a b", b=128)), [tm], [maskb])
    V(lambda e: e.memset(onesb[:, :], 1.0), [], [onesb])
    V(lambda e: e.memset(onesD[:, :], 1.0 / 128), [], [onesD])
    V(lambda e: e.memset(onesC[:, :], 1.0 / 512), [], [onesC])
    V(lambda e: e.memset(zero4[:, :], 0.0), [], [zero4])
    V(lambda e: e.memset(epsc[:, :], EPS), [], [epsc])
    for l in range(DEPTH):
        tw = tokp.get()
        DMA("sp", lambda e, l=l, tw=tw: e.dma_start(out=tw[0:16, 0:256], in_=par[l, 0:16, P_WGK:P_WGK + 256]), [], [tw])
        V(lambda e, l=l, tw=tw: e.tensor_copy(wgkb[l][:, :], tw[0:16, 0:256]), [tw], [wgkb[l]])
    one_c = cf[:, K_ONE:K_ONE + 1]

    def sel(h):
        return cf[0:4, K_SEL + h * 128:K_SEL + (h + 1) * 128]

    def tile_chunks(i):
        if i == NT - 1:
            return [(0, 64, "p", 0)] + [(64 + 4 * j, 4, "s", j) for j in range(NSEQ)]
        if i == 0:
            return [(0, 16, "p", 0), (64, 64, "p", 0)]
        return [(0, 64, "p", 0), (64, 64, "p", 0)]

    def mask_of(i):
        return 2 if i == NT - 1 else (0 if i == 0 else 1)

    blocks = []
    i = 0
    while i < NT - 1:
        n = min(2, NT - 1 - i)
        blocks.append(list(range(i, i + n)))
        i += n
    blocks.append([NT - 1])

    mblocks = []
    i = 0
    while i < NT - 1:
        n = min(4, NT - 1 - i)
        mblocks.append(list(range(i, i + n)))
        i += n
    mblocks.append([NT - 1])

    def rpat(blk, nb):
        if blk[0] == NT - 1:
            return cf[:, K_R + 256:K_R + 384]
        return cf[:, K_R:K_R + nb]

    def load_w(l, col0, n, small=False):
        t = (wsm if small else wslots).get()
        src = w_in[l, :, col0:col0 + n].rearrange("(kc p) n -> p kc n", p=128)
        DMA("pool", lambda e: e.dma_start(out=t[:, :, 0:n], in_=src), [], [t])
        return t

    def load_wb(l, i):
        ts = []
        for hh in range(2):
            t = wslots.get()
            src = w_br[l, i, :, hh * 512:(hh + 1) * 512].rearrange("(kc p) n -> p kc n", p=128)
            DMA("pool", lambda e, t=t, src=src: e.dma_start(out=t[:, 0:4, :], in_=src), [], [t])
            ts.append(t)
        return ts

    def load_wo(l):
        ts = []
        for hh in range(2):
            t = wslots.get()
            src = w_out[l, :, hh * 512:(hh + 1) * 512].rearrange("(kc p) n -> p kc n", p=128)
            DMA("pool", lambda e, t=t, src=src: e.dma_start(out=t[:, :, :], in_=src), [], [t])
            ts.append(t)
        return ts

    def xparts(c0, nb):
        return [xnT.p(t) for t in range(c0 // 128, (c0 + nb + 127) // 128)]

    def proj_fm(w, wc0, M, c0, nb, pst):
        for kc in range(8):
            PE(lambda e, kc=kc: e.matmul(pst[0:M, 0:nb], w[:, kc, wc0:wc0 + M], xnT[:, kc, c0:c0 + nb],
                                        start=(kc == 0), stop=(kc == 7)), [w] + xparts(c0, nb), [pst])

    def proj_tm(w, n, ti, pst):
        for kc in range(8):
            PE(lambda e, kc=kc: e.matmul(pst[:, 0:n], xnT[:, kc, ti * 128:(ti + 1) * 128], w[:, kc, 0:n],
                                        start=(kc == 0), stop=(kc == 7)), [w, xnT.p(ti)], [pst])

    def norm_tile(l, ht, ti, sqpool=None):
        sq = (sqpool or hst).get()
        A(lambda e: e.activation(out=sq[:, :], in_=ht[:, :], func=AF.Square), [ht], [sq])
        s1 = sm1.get()
        V(lambda e: e.reduce_sum(out=s1[:, 0:1], in_=sq[:, :], axis=AX.X), [sq], [s1])
        rsqrt(s1[:, 2:3], s1[:, 0:1], 128, [s1], [s1], scale=1.0 / D)
        xb = hbf.get()
        V(lambda e: e.tensor_scalar_mul(out=xb[:, :], in0=ht[:, :], scalar1=s1[:, 2:3]), [ht, s1], [xb])
        for kc in range(8):
            PE(lambda e, kc=kc: e.transpose(ptr[:, kc * 128:(kc + 1) * 128], xb[:, kc * 128:(kc + 1) * 128], identb[:, :]), [xb, identb], [ptr])
        nw = pf[l][:, P_NORMW:P_NORMW + 8].unsqueeze(2).broadcast_to([128, 8, 128])
        V(lambda e: e.tensor_tensor(out=xnT[:, :, ti * 128:(ti + 1) * 128], in0=ptr[:, :].rearrange("p (a b) -> p a b", b=128), in1=nw, op=ALU.mult),
          [ptr, pf[l]], [xnT.p(ti)])
        return s1

    def rec_tile(cfg, l, ti, bi, c0, qT, kT, khT, vS, EBE, ebi0, post):
        dk, nvc = cfg["dk"], cfg["nvc"]
        dvx = 128 * nvc
        S, Sb = cfg["S"], cfg["Sb"]
        pkv = cfg["kv"]
        tc0 = c0 + bi * 128
        chunks = tile_chunks(ti)
        for h in range(4):
            PE(lambda e, h=h: e.transpose(ptr[:, h * dk:(h + 1) * dk], khT[0:dk, h, tc0:tc0 + 128], identb[0:dk, 0:dk]), [khT, identb], [ptr])
        kh = khp.get()
        A(lambda e: e.activation(out=kh[:, 0:4 * dk], in_=ptr[:, 0:4 * dk], func=AF.Copy), [ptr], [kh])
        for h in range(4):
            PE(lambda e, h=h: e.matmul(pat[:, h * 128:(h + 1) * 128], kT[0:dk, h, tc0:tc0 + 128], qT[0:dk, h, tc0:tc0 + 128], start=True, stop=True),
               [kT, qT], [pat])
        attm = attp.get()
        mk = maskb[:, mask_of(ti):mask_of(ti) + 1, :].broadcast_to([128, 4, 128])
        V(lambda e: e.tensor_tensor(out=attm[:, :, :], in0=pat[:, :].rearrange("p (a b) -> p a b", b=128), in1=mk, op=ALU.mult), [pat, maskb], [attm])
        ots = [pot.get() for _ in range(nvc)]
        for vc in range(nvc):
            for h in range(4):
                lhs = vS[:, bi, h * 128:(h + 1) * 128] if vc == 0 else onesb[:, :]
                PE(lambda e, vc=vc, h=h, lhs=lhs: e.matmul(ots[vc][:, h * 128:(h + 1) * 128], lhs, attm[:, h, :], start=(h == 0), stop=False, skip_group_check=True),
                   [vS, onesb, attm], [ots[vc]])
        nch = len(chunks)
        LA = 3
        pre = {}

        def issue_load(jj):
            t = stg.get()
            cfg["load_state"](l, jj, t)
            pre[jj] = t

        if any(c[2] == "s" for c in chunks):
            for jj in range(min(LA, NSEQ)):
                issue_load(jj)

        def do_chunk(ci, off, L, kind, j):
            last = ci == nch - 1
            ei = ebi0 + ci
            if kind == "p":
                src_b = Sb
            else:
                st = pre.pop(j)
                if j + LA < NSEQ and any(c[2] == "s" for c in chunks):
                    issue_load(j + LA)
                src_b = cfg["stgb"].get()
                A(lambda e, st=st, src_b=src_b: e.activation(out=src_b[0:dk, :, 0:dvx], in_=st[0:dk, :, 0:dvx], func=AF.Copy), [st], [src_b])
            for vc in range(nvc):
                for h in range(4):
                    PE(lambda e, vc=vc, h=h, src_b=src_b: e.matmul(ots[vc][:, h * 128 + off:h * 128 + off + L], src_b[0:dk, h, vc * 128:(vc + 1) * 128],
                                                                qT[0:dk, h, tc0 + off:tc0 + off + L], start=False, stop=last, skip_group_check=True), [src_b, qT], [ots[vc]])
            if kind == "p":
                r0, r1 = off, off + L
                lk = kh
            else:
                r0, r1 = 64, 128
                lk = cfg["khm"].get()
                V(lambda e, lk=lk, j=j: e.tensor_scalar_mul(out=lk[64:128, 0:4 * dk], in0=kh[64:128, 0:4 * dk], scalar1=cf[64:128, K_OH + j:K_OH + j + 1]),
                  [kh, cf], [lk])
            for h in range(4):
                PE(lambda e, h=h, lk=lk: e.matmul(pkv[0:dk, h * dvx:h * dvx + 128], lk[r0:r1, h * dk:(h + 1) * dk], vS[r0:r1, bi, h * 128:(h + 1) * 128], start=True, stop=True),
                   [lk, vS], [pkv])
                if nvc == 2:
                    PE(lambda e, h=h, lk=lk: e.matmul(pkv[0:dk, h * dvx + 128:h * dvx + 256], lk[r0:r1, h * dk:(h + 1) * dk], onesb[r0:r1, :], start=True, stop=True),
                       [lk, onesb], [pkv])
            ebb = EBE[0:dk, :, ei:ei + 1].broadcast_to([dk, 4, dvx])
            kvv = pkv[0:dk, 0:4 * dvx].rearrange("p (a b) -> p a b", b=dvx)
            if kind == "p":
                for h in range(4):
                    V(lambda e, h=h: e.scalar_tensor_tensor(out=S[0:dk, h, 0:dvx], in0=S[0:dk, h, 0:dvx], scalar=EBE[0:dk, h, ei:ei + 1],
                                                            in1=pkv[0:dk, h * dvx:(h + 1) * dvx], op0=ALU.mult, op1=ALU.add), [S, EBE, pkv], [S])
                A(lambda e: e.activation(out=Sb[0:dk, :, 0:dvx], in_=S[0:dk, :, 0:dvx], func=AF.Copy), [S], [Sb])
            else:
                so = st
                for h in range(4):
                    V(lambda e, h=h, so=so, st=st: e.scalar_tensor_tensor(out=so[0:dk, h, 0:dvx], in0=st[0:dk, h, 0:dvx], scalar=EBE[0:dk, h, ei:ei + 1],
                                                                          in1=pkv[0:dk, h * dvx:(h + 1) * dvx], op0=ALU.mult, op1=ALU.add), [st, EBE, pkv], [so])
                cfg["store_state"](l, j, so)

        for ci, (off, L, kind, j) in enumerate(chunks):
            do_chunk(ci, off, L, kind, j)
        return lambda: post(ots, ti, bi, tc0)

    def ebe_fill(EBE, eb, dk, blk, c0):
        base = []
        idx = 0
        for bi, ti in enumerate(blk):
            base.append(idx)
            tcol = bi * 128
            if ti == NT - 1:
                V(lambda e, idx=idx, tcol=tcol: e.tensor_copy(EBE[0:dk, :, idx:idx + 1], eb[0:dk, :, tcol + 63:tcol + 64]), [eb], [EBE])
                V(lambda e, idx=idx, tcol=tcol: e.tensor_copy(EBE[0:dk, :, idx + 1:idx + 17], eb[0:dk, :, tcol + 67:tcol + 128:4]), [eb], [EBE])
                idx += 17
            else:
                cA = 15 if ti == 0 else 63
                V(lambda e, idx=idx, tcol=tcol, cA=cA: e.tensor_copy(EBE[0:dk, :, idx:idx + 1], eb[0:dk, :, tcol + cA:tcol + cA + 1]), [eb], [EBE])
                V(lambda e, idx=idx, tcol=tcol: e.tensor_copy(EBE[0:dk, :, idx + 1:idx + 2], eb[0:dk, :, tcol + 127:tcol + 128]), [eb], [EBE])
                idx += 2
        return base

    def khat(khT, kT, EBE, base, dk, blk):
        for bi, ti in enumerate(blk):
            tcol = bi * 128
            b0 = base[bi]
            for h in range(4):
                if ti == NT - 1:
                    V(lambda e, h=h, b0=b0, tcol=tcol: e.tensor_scalar_mul(out=khT[0:dk, h, tcol:tcol + 64], in0=kT[0:dk, h, tcol:tcol + 64], scalar1=EBE[0:dk, h, b0:b0 + 1]),
                      [kT, EBE], [khT])
                    V(lambda e, h=h, b0=b0, tcol=tcol: e.tensor_tensor(out=khT[0:dk, h, tcol + 64:tcol + 128].rearrange("p (a b) -> p a b", b=4),
                                                                       in0=kT[0:dk, h, tcol + 64:tcol + 128].rearrange("p (a b) -> p a b", b=4),
                                                                       in1=EBE[0:dk, h, b0 + 1:b0 + 17].unsqueeze(2).broadcast_to([dk, 16, 4]), op=ALU.mult), [kT, EBE], [khT])
                elif h == 0:
                    V(lambda e, b0=b0, tcol=tcol: e.tensor_tensor(out=khT[0:dk, :, tcol:tcol + 128].rearrange("p h (a b) -> p h a b", b=64),
                                                                  in0=kT[0:dk, :, tcol:tcol + 128].rearrange("p h (a b) -> p h a b", b=64),
                                                                  in1=EBE[0:dk, :, b0:b0 + 2].unsqueeze(3).broadcast_to([dk, 4, 2, 64]), op=ALU.mult), [kT, EBE], [khT])

    def v_proj(wv, blk, vS):
        for bi, ti in enumerate(blk):
            pst = pj.get()
            proj_tm(wv, 512, ti, pst)
            A(lambda e, bi=bi, pst=pst: e.activation(out=vS[:, bi, :], in_=pst[:, :], func=AF.Copy), [pst], [vS])

    def act_proj(w, c0, nb, dst, func, scale=1.0):
        for h in range(4):
            pst = pj.get()
            proj_fm(w, h * 128, 128, c0, nb, pst)
            A(lambda e, h=h, pst=pst: e.activation(out=dst[:, h, 0:nb], in_=pst[:, 0:nb], func=func, scale=scale), [pst], [dst])

    def post_rms(cfg, l, sg, normcol):
        def post(ots, ti, bi, tc0):
            ot = ots[0]
            sq = sqp.get()
            A(lambda e: e.activation(out=sq[:, :], in_=ot[:, :], func=AF.Square), [ot], [sq])
            PE(lambda e: e.matmul(pat[:, :], onesD[:, :], sq[:, :], start=True, stop=True), [onesD, sq], [pat])
            rs = f1p.get()
            rsqrt(rs[:, :, :].rearrange("p a b -> p (a b)"), pat[:, :], 128, [pat], [rs])
            t1 = f2p.get()
            V(lambda e: e.tensor_tensor(out=t1[:, :, :].rearrange("p a b -> p (a b)"), in0=ot[:, :], in1=rs[:, :, :].rearrange("p a b -> p (a b)"), op=ALU.mult), [ot, rs], [t1])
            V(lambda e: e.scalar_tensor_tensor(out=zT[:, :, ti * 128:(ti + 1) * 128], in0=t1[:, :, :], scalar=normcol, in1=sg[:, :, tc0:tc0 + 128], op0=ALU.mult, op1=ALU.mult),
              [t1, sg, pf[l]], [zT.p(ti)])
        return post

    def phase_gla(l):
        S, Sb = S_all, Sb_all
        V(lambda e: e.memset(S[:, :, :], 0.0), [], [S])
        V(lambda e: e.memset(Sb[:, :, :], 0.0), [], [Sb])
        wqk = load_w(l, C_GQ, 512)
        wlr = load_w(l, C_GLR, 16, small=True)
        wv = load_w(l, C_GV, 512)
        wg = load_w(l, C_GG, 512)

        def load_state(l, j, st):
            DMA("sp", lambda e: e.dma_start(out=st[0:64, :, 0:128], in_=st_gla[l, j].rearrange("h d v -> d h v")), [], [st])

        def store_state(l, j, so):
            DMA("sp", lambda e: e.dma_start(out=o_sg[l, j].rearrange("h d v -> d h v"), in_=so[0:64, :, 0:128]), [so], [])

        cfg = dict(dk=64, nvc=1, S=S, Sb=Sb, load_state=load_state, store_state=store_state, kv=kvA)
        psum_plan("rec1")
        Ps, Rs = [], []

        def body(blk, par):
            Ps.append([])
            Rs.append([])
            kb.rec = Ps[-1]
            c0 = blk[0] * 128
            nb = len(blk) * 128
            pst = pj.get()
            proj_fm(wlr, 0, 16, c0, nb, pst)
            glr = glrp.get()
            A(lambda e: e.activation(out=glr[0:16, 0:nb], in_=pst[0:16, 0:nb], func=AF.Copy), [pst], [glr])
            gA = gAp.get()
            for h in range(4):
                p2 = pj.get()
                PE(lambda e, h=h, p2=p2: e.matmul(p2[0:64, 0:nb], wgkb[l][0:16, h * 64:(h + 1) * 64], glr[0:16, 0:nb], start=True, stop=True), [wgkb[l], glr], [p2])
                A(lambda e, h=h, p2=p2: e.activation(out=gA[0:64, h, 0:nb], in_=p2[0:64, 0:nb], func=AF.Exp, bias=pf[l][0:64, P_NBGK + h:P_NBGK + h + 1], scale=-1.0), [p2, pf[l]], [gA])
            A(lambda e: e.activation(out=gA[0:64, :, 0:nb], in_=gA[0:64, :, 0:nb], func=AF.Ln, bias=one_c[0:64, :], scale=1.0), [gA, cf], [gA])
            gB = gBp.get()
            for h in range(4):
                V(lambda e, h=h: e.tensor_tensor_scan(out=gB[0:64, h, 0:nb], data0=rpat(blk, nb)[0:64, :], data1=gA[0:64, h, 0:nb], initial=0.0, op0=ALU.mult, op1=ALU.add), [gA, cf], [gB])
            eb = ebp.get()
            enb = enbp.get()
            A(lambda e: e.activation(out=eb[0:64, :, 0:nb], in_=gB[0:64, :, 0:nb], func=AF.Exp, scale=-1.0 / 16), [gB], [eb])
            A(lambda e: e.activation(out=enb[0:64, :, 0:nb], in_=gB[0:64, :, 0:nb], func=AF.Exp, scale=1.0 / 16), [gB], [enb])
            qT, kT, khT = qTp.tiles[par], kTp.tiles[par], khTp.tiles[par]
            for h in range(4):
                pq = pj.get()
                proj_fm(wqk, h * 64, 64, c0, nb, pq)
                V(lambda e, h=h, pq=pq: e.scalar_tensor_tensor(out=qT[0:64, h, 0:nb], in0=pq[0:64, 0:nb], scalar=0.125, in1=eb[0:64, h, 0:nb], op0=ALU.mult, op1=ALU.mult), [pq, eb], [qT])
                pk = pj.get()
                proj_fm(wqk, 256 + h * 64, 64, c0, nb, pk)
                V(lambda e, h=h, pk=pk: e.tensor_tensor(out=kT[0:64, h, 0:nb], in0=pk[0:64, 0:nb], in1=enb[0:64, h, 0:nb], op=ALU.mult), [pk, enb], [kT])
            EBE = EBEp.tiles[par]
            base = ebe_fill(EBE, eb, 64, blk, c0)
            khat(khT, kT, EBE, base, 64, blk)
            vS = vSp.tiles[par]
            v_proj(wv, blk, vS)
            sg = sgp.tiles[par]
            act_proj(wg, c0, nb, sg, AF.Silu)
            post = post_rms(cfg, l, sg, pf[l][:, P_GNORM:P_GNORM + 1])
            kb.rec = Rs[-1]
            op_ = 1 - par
            cfg["stgb"] = Pool([stgb.tiles[0], view(ycb.tiles[0][:, :, :], ycb.tiles[0].parts), qTp.tiles[op_], kTp.tiles[op_], khTp.tiles[op_], sgop.tiles[op_]])
            sgo_ = sgp.tiles[op_]
            cfg["khm"] = Pool([khmp.tiles[0], view(sgo_[:, 0:2, :].rearrange("p a b -> p (a b)"), sgo_.parts), view(sgo_[:, 2:4, :].rearrange("p a b -> p (a b)"), sgo_.parts)])
            pend = None
            for bi, ti in enumerate(blk):
                th = rec_tile(cfg, l, ti, bi, 0, qT, kT, khT, vS, EBE, base[bi], post)
                if cfg["nvc"] == 2:
                    th()
                else:
                    if pend is not None:
                        pend()
                    pend = th
            if pend is not None:
                pend()

        for bidx, blk in enumerate(blocks):
            body(blk, bidx % 2)
        merge_overlapped(l, 0, Ps, Rs)
        DMA("sp", lambda e: e.dma_start(out=o_pg[l].rearrange("h d v -> d h v"), in_=S[0:64, :, 0:128]), [S], [])

    def phase_hgrn(l):
        S, Sb = S_all, Sb_all
        V(lambda e: e.memset(S[:, :, :], 0.0), [], [S])
        V(lambda e: e.memset(Sb[:, :, :], 0.0), [], [Sb])
        wq = load_w(l, C_HQ, 512)
        wf = load_w(l, C_HF, 512)
        wv = load_w(l, C_HI, 512)
        wg = load_w(l, C_HG, 512)
        lb = zero4 if l == 0 else lbt
        oml = omlt

        def load_state(l, j, st):
            DMA("sp", lambda e: e.dma_start(out=st[:, :, 0:128], in_=st_hg[l, j].rearrange("h d v -> d h v")), [], [st])

        def store_state(l, j, so):
            DMA("sp", lambda e: e.dma_start(out=o_sh[l, j].rearrange("h d v -> d h v"), in_=so[:, :, 0:128]), [so], [])

        cfg = dict(dk=128, nvc=1, S=S, Sb=Sb, load_state=load_state, store_state=store_state, kv=kvA)
        psum_plan("rec1")
        Ps, Rs = [], []

        def body(blk, par):
            Ps.append([])
            Rs.append([])
            kb.rec = Ps[-1]
            c0 = blk[0] * 128
            nb = len(blk) * 128
            gA, gB = gAp.get(), gBp.get()
            kT = kTp.tiles[par]
            eb, enb = ebp.get(), enbp.get()
            for h in range(4):
                pst = pj.get()
                proj_fm(wf, h * 128, 128, c0, nb, pst)
                A(lambda e, h=h, pst=pst: e.activation(out=gA[:, h, 0:nb], in_=pst[:, 0:nb], func=AF.Sigmoid), [pst], [gA])
                A(lambda e, h=h, pst=pst: e.activation(out=kT[:, h, 0:nb], in_=pst[:, 0:nb], func=AF.Sigmoid, scale=-1.0), [pst], [kT])
                V(lambda e, h=h: e.tensor_scalar(out=gA[:, h, 0:nb], in0=gA[:, h, 0:nb], scalar1=oml[:, h:h + 1], scalar2=lb[:, h:h + 1], op0=ALU.mult, op1=ALU.add), [gA, oml, lb], [gA])
            A(lambda e: e.activation(out=gA[:, :, 0:nb], in_=gA[:, :, 0:nb], func=AF.Ln), [gA], [gA])
            for h in range(4):
                V(lambda e, h=h: e.tensor_tensor_scan(out=gB[:, h, 0:nb], data0=rpat(blk, nb), data1=gA[:, h, 0:nb], initial=0.0, op0=ALU.mult, op1=ALU.add), [gA, cf], [gB])
            A(lambda e: e.activation(out=eb[:, :, 0:nb], in_=gB[:, :, 0:nb], func=AF.Exp), [gB], [eb])
            A(lambda e: e.activation(out=enb[:, :, 0:nb], in_=gB[:, :, 0:nb], func=AF.Exp, scale=-1.0), [gB], [enb])
            for h in range(4):
                V(lambda e, h=h: e.scalar_tensor_tensor(out=kT[:, h, 0:nb], in0=kT[:, h, 0:nb], scalar=oml[:, h:h + 1], in1=enb[:, h, 0:nb], op0=ALU.mult, op1=ALU.mult), [kT, oml, enb], [kT])
            qT, khT = qTp.tiles[par], khTp.tiles[par]
            for h in range(4):
                pq = pj.get()
                proj_fm(wq, h * 128, 128, c0, nb, pq)
                A(lambda e, h=h, pq=pq: e.activation(out=gA[:, h, 0:nb], in_=pq[:, 0:nb], func=AF.Silu), [pq], [gA])
            V(lambda e: e.tensor_tensor(out=qT[:, :, 0:nb], in0=gA[:, :, 0:nb], in1=eb[:, :, 0:nb], op=ALU.mult), [gA, eb], [qT])
            EBE = EBEp.tiles[par]
            base = ebe_fill(EBE, eb, 128, blk, c0)
            khat(khT, kT, EBE, base, 128, blk)
            vS = vSp.tiles[par]
            v_proj(wv, blk, vS)
            sg = sgp.tiles[par]
            act_proj(wg, c0, nb, sg, AF.Silu)
            post = post_rms(cfg, l, sg, pf[l][:, P_HNORM:P_HNORM + 1])
            kb.rec = Rs[-1]
            op_ = 1 - par
            cfg["stgb"] = Pool([stgb.tiles[0], view(ycb.tiles[0][:, :, :], ycb.tiles[0].parts), qTp.tiles[op_], kTp.tiles[op_], khTp.tiles[op_], sgop.tiles[op_]])
            sgo_ = sgp.tiles[op_]
            cfg["khm"] = Pool([khmp.tiles[0], view(sgo_[:, 0:2, :].rearrange("p a b -> p (a b)"), sgo_.parts), view(sgo_[:, 2:4, :].rearrange("p a b -> p (a b)"), sgo_.parts)])
            pend = None
            for bi, ti in enumerate(blk):
                th = rec_tile(cfg, l, ti, bi, 0, qT, kT, khT, vS, EBE, base[bi], post)
                if cfg["nvc"] == 2:
                    th()
                else:
                    if pend is not None:
                        pend()
                    pend = th
            if pend is not None:
                pend()

        for bidx, blk in enumerate(blocks):
            body(blk, bidx % 2)
        merge_overlapped(l, 1, Ps, Rs)
        DMA("sp", lambda e: e.dma_start(out=o_ph[l].rearrange("h d v -> d h v"), in_=S[:, :, 0:128]), [S], [])

    def phase_mlstm(l):
        S, Sb = S_all, Sb_all
        V(lambda e: e.memset(S[:, :, :], 0.0), [], [S])
        V(lambda e: e.memset(Sb[:, :, :], 0.0), [], [Sb])
        wq = load_w(l, C_MQ, 512)
        wk = load_w(l, C_MK, 512)
        wif = load_w(l, C_MI, 8, small=True)
        wv = load_w(l, C_MV, 512)
        wo = load_w(l, C_MO, 512)
        wg = load_w(l, C_MG, 512)
        m0T = ml_m0T
        DMA("sp", lambda e: ncdma(e, m0T[:, :], st_m[l].rearrange("(s h) -> h s", h=4)), [], [m0T])
        em0 = ml_em0
        DMA("sp", lambda e: e.dma_start(out=em0[:, :], in_=st_m[l].partition_broadcast(128)), [], [em0])
        A(lambda e: e.activation(out=em0[:, :], in_=em0[:, :], func=AF.Exp), [em0], [em0])
        n0 = tokp.get()
        DMA("sp", lambda e: e.dma_start(out=n0[0:64, 0:128], in_=st_n[l]), [], [n0])
        PE(lambda e: e.transpose(pat[:, 0:64], n0[0:64, 0:128], cf[0:64, K_ID:K_ID + 64]), [n0, cf], [pat])
        n0T = ml_n0T
        V(lambda e: e.tensor_tensor(out=n0T[:, :], in0=pat[:, 0:64], in1=em0[:, :], op=ALU.mult), [pat, em0], [n0T])
        nTo = ml_nTo
        emn = ml_emn
        emnp = ml_emnp
        mcar = ml_mcar
        V(lambda e: e.memset(mcar[:, :], 0.0), [], [mcar])
        msE = ml_msE
        Cout = {}

        def load_state(l, j, st):
            DMA("sp", lambda e: e.dma_start(out=st[:, :, 0:128], in_=st_C[l, j].rearrange("h d v -> d h v")), [], [st])
            V(lambda e: e.tensor_tensor(out=st[:, :, 0:128], in0=st[:, :, 0:128], in1=em0[:, 4 * j:4 * j + 4].unsqueeze(2).broadcast_to([128, 4, 128]), op=ALU.mult), [st, em0], [st])
            V(lambda e: e.tensor_copy(st[:, :, 128:256], n0T[:, 4 * j:4 * j + 4].unsqueeze(2).broadcast_to([128, 4, 128])), [n0T], [st])

        def store_state(l, j, so):
            V(lambda e: e.tensor_tensor(out=so[:, :, 0:128], in0=so[:, :, 0:128], in1=emn[:, :, j:j + 1].broadcast_to([128, 4, 128]), op=ALU.mult), [so, emn], [so])
            V(lambda e: e.tensor_tensor(out=nTo[:, 4 * j:4 * j + 4], in0=so[:, :, 128], in1=emn[:, :, j], op=ALU.mult), [so, emn], [nTo])
            DMA("sp", lambda e: e.dma_start(out=o_sC[l, j].rearrange("h d v -> d h v"), in_=so[:, :, 0:128]), [so], [])

        cfg = dict(dk=128, nvc=2, S=S, Sb=Sb, load_state=load_state, store_state=store_state, kv=pkv)
        psum_plan("rec2")
        Ps, Rs = [], []

        def body(blk, par):
            Ps.append([])
            Rs.append([])
            kb.rec = Ps[-1]
            c0 = blk[0] * 128
            nb = len(blk) * 128
            islast = blk[0] == NT - 1
            IG, LFn, LF, BP, A2, MR = [row4.get() for _ in range(6)]
            pst = pj.get()
            proj_fm(wif, 0, 4, c0, nb, pst)
            A(lambda e: e.activation(out=IG[0:4, 0:nb], in_=pst[0:4, 0:nb], func=AF.Identity, bias=pf[l][0:4, P_BI:P_BI + 1], scale=1.0), [pst, pf[l]], [IG])
            pst2 = pj.get()
            proj_fm(wif, 4, 4, c0, nb, pst2)
            A(lambda e: e.activation(out=LFn[0:4, 0:nb], in_=pst2[0:4, 0:nb], func=AF.Exp, bias=pf[l][0:4, P_NBF:P_NBF + 1], scale=-1.0), [pst2, pf[l]], [LFn])
            A(lambda e: e.activation(out=LFn[0:4, 0:nb], in_=LFn[0:4, 0:nb], func=AF.Ln, bias=one_c[0:4, :], scale=1.0), [LFn, cf], [LFn])
            V(lambda e: e.tensor_scalar(out=LF[0:4, 0:nb], in0=LFn[0:4, 0:nb], scalar1=-1.0, scalar2=None, op0=ALU.mult), [LFn], [LF])
            V(lambda e: e.tensor_tensor_scan(out=BP[0:4, 0:nb], data0=rpat(blk, nb)[0:4, :], data1=LFn[0:4, 0:nb], initial=0.0, op0=ALU.mult, op1=ALU.add), [LFn, cf], [BP])
            V(lambda e: e.tensor_tensor(out=A2[0:4, 0:nb], in0=IG[0:4, 0:nb], in1=BP[0:4, 0:nb], op=ALU.add), [IG, BP], [A2])
            segs = []
            if blk[0] == 0:
                segs = [(0, 16), (64, nb)]
            elif islast:
                segs = [(0, 64)]
            else:
                segs = [(0, nb)]
            for (a0, a1) in segs:
                V(lambda e, a0=a0, a1=a1: e.tensor_tensor_scan(out=MR[0:4, a0:a1], data0=LF[0:4, a0:a1], data1=IG[0:4, a0:a1], initial=mcar[0:4, 0:1], op0=ALU.add, op1=ALU.max), [LF, IG, mcar], [MR])
                V(lambda e, a1=a1: e.tensor_copy(mcar[0:4, 0:1], MR[0:4, a1 - 1:a1]), [MR], [mcar])
            if islast:
                DMA("sp", lambda e: e.dma_start(out=o_pm[l].rearrange("(h o) -> h o", o=1), in_=mcar[0:4, 0:1]), [mcar], [])
                pq1 = pj.get()
                for h in range(4):
                    PE(lambda e, h=h: e.matmul(pq1[:, h:h + 1], sel(h), mcar[0:4, 0:1], start=True, stop=True), [cf, mcar], [pq1])
                A(lambda e: e.activation(out=emnp[:, :], in_=pq1[:, 0:4], func=AF.Exp, scale=-1.0), [pq1], [emnp])
                cur = m0T
                for p in range(4):
                    tmp = ml_tmp.get()
                    V(lambda e, p=p, cur=cur, tmp=tmp: e.tensor_tensor(out=tmp[0:4, 0:NSEQ], in0=LF[0:4, 64 + p:128:4], in1=cur[0:4, 0:NSEQ], op=ALU.add), [LF, cur], [tmp])
                    V(lambda e, p=p, tmp=tmp: e.tensor_tensor(out=msE[0:4, 0:NSEQ], in0=tmp[0:4, 0:NSEQ], in1=IG[0:4, 64 + p:128:4], op=ALU.max), [tmp, IG], [msE])
                    cur = msE
                DMA("sp", lambda e: ncdma(e, o_sm[l].rearrange("s h -> h s"), msE[0:4, 0:NSEQ]), [msE], [])
                pq2 = pj.get()
                for h in range(4):
                    PE(lambda e, h=h: e.matmul(pq2[:, 64 + h * NSEQ:64 + (h + 1) * NSEQ], sel(h), msE[0:4, 0:NSEQ], start=True, stop=True), [cf, msE], [pq2])
                A(lambda e: e.activation(out=emn[:, :, :].rearrange("p a b -> p (a b)"), in_=pq2[:, 64:64 + 4 * NSEQ], func=AF.Exp, scale=-1.0), [pq2], [emn])
            eb, enb = ebp.get(), enbp.get()
            for h in range(4):
                pb = pj.get()
                PE(lambda e, h=h, pb=pb: e.matmul(pb[:, 0:nb], sel(h), BP[0:4, 0:nb], start=True, stop=True), [cf, BP], [pb])
                A(lambda e, h=h, pb=pb: e.activation(out=eb[:, h, 0:nb], in_=pb[:, 0:nb], func=AF.Exp, scale=-1.0), [pb], [eb])
                pb2 = pj.get()
                PE(lambda e, h=h, pb2=pb2: e.matmul(pb2[:, 0:nb], sel(h), A2[0:4, 0:nb], start=True, stop=True), [cf, A2], [pb2])
                A(lambda e, h=h, pb2=pb2: e.activation(out=enb[:, h, 0:nb], in_=pb2[:, 0:nb], func=AF.Exp), [pb2], [enb])
            qT, kT, khT = qTp.tiles[par], kTp.tiles[par], khTp.tiles[par]
            for h in range(4):
                pq = pj.get()
                proj_fm(wq, h * 128, 128, c0, nb, pq)
                V(lambda e, h=h, pq=pq: e.tensor_tensor(out=qT[:, h, 0:nb], in0=pq[:, 0:nb], in1=eb[:, h, 0:nb], op=ALU.mult), [pq, eb], [qT])
                pk = pj.get()
                proj_fm(wk, h * 128, 128, c0, nb, pk)
                V(lambda e, h=h, pk=pk: e.scalar_tensor_tensor(out=kT[:, h, 0:nb], in0=pk[:, 0:nb], scalar=128.0 ** -0.5, in1=enb[:, h, 0:nb], op0=ALU.mult, op1=ALU.mult), [pk, enb], [kT])
            EBE = EBEp.tiles[par]
            base = ebe_fill(EBE, eb, 128, blk, c0)
            khat(khT, kT, EBE, base, 128, blk)
            vS = vSp.tiles[par]
            v_proj(wv, blk, vS)
            sg, sgo = sgp.tiles[par], sgop.tiles[par]
            act_proj(wg, c0, nb, sg, AF.Silu)
            act_proj(wo, c0, nb, sgo, AF.Sigmoid)

            def post(ots, ti, bi, tc0):
                num, den = ots
                dd = f1p.get()
                ddf = dd[:, :, :].rearrange("p a b -> p (a b)")
                A(lambda e: e.activation(out=ddf, in_=den[:, :], func=AF.Abs), [den], [dd])
                V(lambda e: e.tensor_scalar_max(out=ddf, in0=ddf, scalar1=1.0), [dd], [dd])
                A(lambda e: e.activation(out=ddf, in_=ddf, func=AF.Ln), [dd], [dd])
                A(lambda e: e.activation(out=ddf, in_=ddf, func=AF.Exp, scale=-1.0), [dd], [dd])
                x = f2p.get()
                xf = x[:, :, :].rearrange("p a b -> p (a b)")
                V(lambda e: e.tensor_tensor(out=xf, in0=num[:, :], in1=ddf, op=ALU.mult), [num, dd], [x])
                V(lambda e: e.tensor_tensor(out=x[:, :, :], in0=x[:, :, :], in1=sgo[:, :, tc0:tc0 + 128], op=ALU.mult), [x, sgo], [x])
                xb = sqp.get()
                G(lambda e: e.tensor_copy(xb[:, :], xf), [x], [xb])
                PE(lambda e: e.matmul(pat[:, :], onesD[:, :], xb[:, :], start=True, stop=True), [onesD, xb], [pat])
                V(lambda e: e.tensor_tensor(out=xf, in0=xf, in1=pat[:, :], op=ALU.subtract), [x, pat], [x])
                sq = sqp.get()
                A(lambda e: e.activation(out=sq[:, :], in_=xf, func=AF.Square), [x], [sq])
                PE(lambda e: e.matmul(pat[:, :], onesD[:, :], sq[:, :], start=True, stop=True), [onesD, sq], [pat])
                rs = f1p.get()
                rsf = rs[:, :, :].rearrange("p a b -> p (a b)")
                rsqrt(rsf, pat[:, :], 128, [pat], [rs])
                V(lambda e: e.tensor_tensor(out=xf, in0=xf, in1=rsf, op=ALU.mult), [x, rs], [x])
                V(lambda e: e.tensor_tensor(out=x[:, :, :], in0=x[:, :, :], in1=pf[l][:, P_MLN:P_MLN + 4].unsqueeze(2).broadcast_to([128, 4, 128]), op=ALU.mult), [x, pf[l]], [x])
                V(lambda e: e.tensor_tensor(out=zT[:, :, ti * 128:(ti + 1) * 128], in0=x[:, :, :], in1=sg[:, :, tc0:tc0 + 128], op=ALU.mult), [x, sg], [zT.p(ti)])

            kb.rec = Rs[-1]
            op_ = 1 - par
            cfg["stgb"] = Pool([stgb.tiles[0], view(ycb.tiles[0][:, :, :], ycb.tiles[0].parts), qTp.tiles[op_], kTp.tiles[op_], khTp.tiles[op_], sgop.tiles[op_]])
            sgo_ = sgp.tiles[op_]
            cfg["khm"] = Pool([khmp.tiles[0], view(sgo_[:, 0:2, :].rearrange("p a b -> p (a b)"), sgo_.parts), view(sgo_[:, 2:4, :].rearrange("p a b -> p (a b)"), sgo_.parts)])
            pend = None
            for bi, ti in enumerate(blk):
                th = rec_tile(cfg, l, ti, bi, 0, qT, kT, khT, vS, EBE, base[bi], post)
                if cfg["nvc"] == 2:
                    th()
                else:
                    if pend is not None:
                        pend()
                    pend = th
            if pend is not None:
                pend()

        for bidx, blk in enumerate(blocks):
            body(blk, bidx % 2)
        merge_overlapped(l, 2, Ps, Rs)
        so = stg.get()
        V(lambda e: e.tensor_tensor(out=so[:, :, 0:128], in0=S[:, :, 0:128], in1=emnp[:, :].unsqueeze(2).broadcast_to([128, 4, 128]), op=ALU.mult), [S, emnp], [so])
        DMA("sp", lambda e: e.dma_start(out=o_pC[l].rearrange("h d v -> d h v"), in_=so[:, :, 0:128]), [so], [])
        npo = ml_npo
        V(lambda e: e.tensor_tensor(out=npo[:, :], in0=S[:, :, 128], in1=emnp[:, :], op=ALU.mult), [S, emnp], [npo])
        PE(lambda e: e.transpose(pat[0:4, 0:128], npo[:, 0:4], cf[:, K_ID:K_ID + 128]), [npo, cf], [pat])
        t4 = tokp.get()
        A(lambda e: e.activation(out=t4[0:4, 0:128], in_=pat[0:4, 0:128], func=AF.Copy), [pat], [t4])
        DMA("sp", lambda e: e.dma_start(out=o_pn[l], in_=t4[0:4, 0:128]), [t4], [])
        PE(lambda e: e.transpose(pat[0:64, 128:256], nTo[:, 0:64], cf[:, K_ID:K_ID + 128]), [nTo, cf], [pat])
        t5 = tokp.get()
        A(lambda e: e.activation(out=t5[0:64, 0:128], in_=pat[0:64, 128:256], func=AF.Copy), [pat], [t5])
        DMA("sp", lambda e: e.dma_start(out=o_sn[l], in_=t5[0:64, 0:128]), [t5], [])

    def phase_conv(l):
        psum_plan("stream")
        wa = load_w(l, C_CA, 512)
        wb = load_w(l, C_CB, 512)
        wg = load_w(l, C_CG, 512)
        V(lambda e: e.memset(ubf[:, :, 0:30], 0.0), [], [ubf])
        V(lambda e: e.tensor_copy(wcb[:, :, :], pf[l][:, P_CW:P_CW + 124].rearrange("p (a b) -> p a b", b=31)), [pf[l]], [wcb])
        for g4 in range(4):
            cv = cvin.get()
            DMA("sp", lambda e, cv=cv, g4=g4: e.dma_start(out=cv[0:120, :], in_=st_cv[l, 4 * g4:4 * g4 + 4].rearrange("s r c -> (s r) c")), [], [cv])
            for cc in range(4):
                PE(lambda e, cc=cc, cv=cv: e.transpose(pat[:, cc * 128:cc * 128 + 120], cv[0:120, cc * 128:(cc + 1) * 128], cf[0:120, K_ID:K_ID + 120]), [cv, cf], [pat])
            A(lambda e, g4=g4: e.activation(out=ucs[:, :, 4 * g4:4 * g4 + 4, 0:30], in_=pat[:, :].rearrange("p (a b) -> p a b", b=128)[:, :, 0:120].rearrange("p a (s r) -> p a s r", r=30),
                                            func=AF.Copy), [pat], [ucs])
        DMA("sp", lambda e: e.dma_start(out=o_scv[l, :, 0:26, :], in_=st_cv[l, :, 4:30, :]), [], [])
        ys = f2p.get()

        ubufs = [ubf, ubf2]
        V(lambda e: e.memset(ubf2[:, :, 0:30], 0.0), [], [ubf2])

        def stage1(cc, bidx, blk):
            ub = ubufs[bidx % 2]
            c0 = blk[0] * 128
            nb = len(blk) * 128
            islast = blk[0] == NT - 1
            if blk[0] == 0:
                segs = [(0, 16, 30), (64, nb, 46)]
                nreal = nb - 48
            elif islast:
                segs = [(0, 64, 30)]
                nreal = 64
            else:
                segs = [(0, nb, 30)]
                nreal = nb
            pa = pj.get()
            proj_fm(wa, cc * 128, 128, c0, nb, pa)
            pb = pj.get()
            proj_fm(wb, cc * 128, 128, c0, nb, pb)
            sgm = sgp.get()
            A(lambda e: e.activation(out=sgm[:, 0, 0:nb], in_=pb[:, 0:nb], func=AF.Sigmoid), [pb], [sgm])
            for (a0, a1, d0) in segs:
                V(lambda e, a0=a0, a1=a1, d0=d0: e.tensor_tensor(out=ub[:, cc, d0:d0 + a1 - a0], in0=pa[:, a0:a1], in1=sgm[:, 0, a0:a1], op=ALU.mult), [pa, sgm], [ub])
            if islast:
                V(lambda e: e.tensor_tensor(out=ucb[:, cc, 30:94], in0=pa[:, 0:64], in1=sgm[:, 0, 0:64], op=ALU.mult), [pa, sgm], [ucb])
                V(lambda e: e.tensor_tensor(out=ucs[:, cc, :, 30:34], in0=pa[:, 64:128].rearrange("p (s r) -> p s r", r=4),
                                            in1=sgm[:, 0, 64:128].rearrange("p (s r) -> p s r", r=4), op=ALU.mult), [pa, sgm], [ucs])
            return (cc, blk, ub, nreal, islast)

        def halo(cc, bidx, st):
            ub, nreal, islast = st[2], st[3], st[4]
            if not islast:
                nx = ubufs[(bidx + 1) % 2]
                V(lambda e: e.tensor_copy(nx[:, cc, 0:30], ub[:, cc, nreal:nreal + 30]), [ub], [nx])

        def stage2(st):
            cc, blk, ub, nreal, islast = st
            c0 = blk[0] * 128
            nb = len(blk) * 128
            zparts = [zT.p(t) for t in blk]
            pc = pj.get()
            for jt in range(31):
                PE(lambda e, jt=jt: e.matmul(pc[:, 0:nreal], dg[:, jt, :], ub[:, cc, jt:jt + nreal], start=(jt == 0), stop=(jt == 30)), [dg, ub], [pc])
            bcol = pf[l][:, P_CB + cc:P_CB + cc + 1]
            if blk[0] == 0:
                A(lambda e: e.activation(out=zT[:, cc, c0:c0 + 16], in_=pc[:, 0:16], func=AF.Identity, bias=bcol, scale=1.0), [pc, pf[l]], zparts)
                A(lambda e: e.activation(out=zT[:, cc, c0 + 64:c0 + nb], in_=pc[:, 16:nreal], func=AF.Identity, bias=bcol, scale=1.0), [pc, pf[l]], zparts)
            else:
                A(lambda e: e.activation(out=zT[:, cc, c0:c0 + nreal], in_=pc[:, 0:nreal], func=AF.Identity, bias=bcol, scale=1.0), [pc, pf[l]], zparts)

            def sample_taps():
                o3 = ys[:, cc, 0:64].rearrange("p (s r) -> p s r", r=4)
                for jt in range(31):
                    wcol = pf[l][:, P_CW + cc * 31 + jt:P_CW + cc * 31 + jt + 1]
                    if jt == 0:
                        V(lambda e, wcol=wcol: e.tensor_scalar(out=o3, in0=ucs[:, cc, :, 0:4], scalar1=wcol, scalar2=bcol, op0=ALU.mult, op1=ALU.add), [ucs, pf[l]], [ys])
                    else:
                        V(lambda e, wcol=wcol, jt=jt: e.scalar_tensor_tensor(out=o3, in0=ucs[:, cc, :, jt:jt + 4], scalar=wcol, in1=o3, op0=ALU.mult, op1=ALU.add), [ucs, pf[l], ys], [ys])
                V(lambda e: e.tensor_copy(zT[:, cc, c0 + 64:c0 + 128], ys[:, cc, 0:64]), [ys], zparts)

            return sample_taps if islast else None

        dg = dgA

        def build_dg(cc):
            V(lambda e: e.tensor_tensor(out=dg[:, :, :], in0=identb[:, :].unsqueeze(1).broadcast_to([128, 31, 128]),
                                        in1=wcb[:, cc, :].unsqueeze(2).broadcast_to([128, 31, 128]), op=ALU.mult), [identb, wcb], [dg])

        build_dg(0)
        for cc in range(4):
            taps = None
            pend = None
            for bidx, blk in enumerate(blocks):
                st = stage1(cc, bidx, blk)
                if pend is not None:
                    taps = stage2(pend) or taps
                halo(cc, bidx, st)
                pend = st
            taps = stage2(pend) or taps
            if cc < 3:
                build_dg(cc + 1)
            if taps is not None:
                taps()
        for cc in range(4):
            PE(lambda e, cc=cc: e.transpose(pat[0:64, cc * 128:(cc + 1) * 128], ucb[:, cc, 30:94], cf[:, K_ID:K_ID + 128]), [ucb, cf], [pat])
        tk = tokp.get()
        A(lambda e: e.activation(out=tk[0:64, :], in_=pat[0:64, :], func=AF.Copy), [pat], [tk])
        DMA("sp", lambda e: e.dma_start(out=o_pcv[l, :, :], in_=tk[34:64, :]), [tk], [])
        us = ys
        V(lambda e: e.tensor_copy(us[:, :, 0:64].rearrange("p a (s r) -> p a s r", r=4), ucs[:, :, :, 30:34]), [ucs], [us])
        for cc in range(4):
            PE(lambda e, cc=cc: e.transpose(pat[0:64, cc * 128:(cc + 1) * 128], us[:, cc, 0:64], cf[:, K_ID:K_ID + 128]), [us, cf], [pat])
        tk2 = tokp.get()
        A(lambda e: e.activation(out=tk2[0:64, :], in_=pat[0:64, :], func=AF.Copy), [pat], [tk2])
        DMA("sp", lambda e: e.dma_start(out=o_scv[l, :, 26:30, :], in_=tk2[0:64, :]), [tk2], [])

        def ln_block(blk):
            c0 = blk[0] * 128
            nb = len(blk) * 128
            zparts = [zT.p(t) for t in blk]
            yc = ycp.get()
            for cc in range(4):
                PE(lambda e, cc=cc: e.matmul(pat[:, 0:nb], onesC[:, :], zT[:, cc, c0:c0 + nb], start=(cc == 0), stop=(cc == 3)), [onesC] + zparts, [pat])
            V(lambda e: e.tensor_tensor(out=yc[:, :, 0:nb], in0=zT[:, :, c0:c0 + nb], in1=pat[:, 0:nb].unsqueeze(1).broadcast_to([128, 4, nb]), op=ALU.subtract), zparts + [pat], [yc])
            sg = sgp.get()
            act_proj(wg, c0, nb, sg, AF.Silu)
            yb = ycb.get()
            A(lambda e: e.activation(out=yb[:, :, 0:nb], in_=yc[:, :, 0:nb], func=AF.Square), [yc], [yb])
            for cc in range(4):
                PE(lambda e, cc=cc: e.matmul(pat[:, 0:nb], onesC[:, :], yb[:, cc, 0:nb], start=(cc == 0), stop=(cc == 3)), [onesC, yb], [pat])
            rs = f2p.get()
            rsf = rs[:, :, :].rearrange("p a b -> p (a b)")
            rsqrt(rsf[:, 0:nb], pat[:, 0:nb], 128, [pat], [rs])
            V(lambda e: e.tensor_tensor(out=yc[:, :, 0:nb], in0=yc[:, :, 0:nb], in1=rsf[:, 0:nb].unsqueeze(1).broadcast_to([128, 4, nb]), op=ALU.mult), [yc, rs], [yc])
            for cc in range(4):
                V(lambda e, cc=cc: e.tensor_scalar(out=yc[:, cc, 0:nb], in0=yc[:, cc, 0:nb], scalar1=pf[l][:, P_CG + cc:P_CG + cc + 1], scalar2=pf[l][:, P_CBT + cc:P_CBT + cc + 1], op0=ALU.mult, op1=ALU.add),
                  [yc, pf[l]], [yc])
            A(lambda e: e.activation(out=yc[:, :, 0:nb], in_=yc[:, :, 0:nb], func=AF.Silu), [yc], [yc])
            V(lambda e: e.tensor_tensor(out=zT[:, :, c0:c0 + nb], in0=yc[:, :, 0:nb], in1=sg[:, :, 0:nb], op=ALU.mult), [yc, sg], zparts)

        for blk in blocks:
            ln_block(blk)

    def merge_blocks(i, wts, blks, pgp, gtl, W=512):
        wg0, wg1, wb = wts

        def one(blk, oc):
            c0 = blk[0] * 128
            nb = len(blk) * 128
            wgt = wg0 if oc < 4 else wg1
            pg = pgp.get()
            proj_fm(wgt, (oc % 4) * 128, 128, c0, nb, pg)
            gt = gtl.get()
            A(lambda e: e.activation(out=gt[:, 0:nb], in_=pg[:, 0:nb], func=AF.Sigmoid), [pg], [gt])
            py = pgp.get()
            wbt = wb[oc // 4]
            for kc in range(4):
                PE(lambda e, kc=kc: e.matmul(py[:, 0:nb], wbt[:, kc, (oc % 4) * 128:(oc % 4 + 1) * 128], zT[:, kc, c0:c0 + nb], start=(kc == 0), stop=(kc == 3)),
                   [wbt] + [zT.p(t) for t in blk], [py])
            mparts = [mixT.p(t) for t in blk]
            if i == 0:
                V(lambda e: e.tensor_tensor(out=mixT[:, oc, c0:c0 + nb], in0=py[:, 0:nb], in1=gt[:, 0:nb], op=ALU.mult), [py, gt], mparts)
            else:
                V(lambda e: e.tensor_tensor(out=gt[:, W:W + nb], in0=py[:, 0:nb], in1=gt[:, 0:nb], op=ALU.mult), [py, gt], [gt])
                G(lambda e: e.tensor_tensor(out=mixT[:, oc, c0:c0 + nb], in0=mixT[:, oc, c0:c0 + nb], in1=gt[:, W:W + nb], op=ALU.add), [gt] + mparts, mparts)

        for blk in blks:
            for oc in range(8):
                one(blk, oc)

    def load_merge_w(l, i):
        return (load_w(l, C_GATE + i * 1024, 512), load_w(l, C_GATE + i * 1024 + 512, 512), load_wb(l, i))

    def phase_merge(l, i):
        wts = load_merge_w(l, i)
        psum_plan("stream")
        merge_blocks(i, wts, mblocks, pjm, gtp)

    def merge_overlapped(l, i, Ps, Rs):
        Mt = []
        kb.rec = []
        wts = load_merge_w(l, i)
        mpool = Pool(list(pj.tiles))
        for bi_, blk in enumerate(blocks[:-1]):
            if bi_ > 0:
                kb.rec = []
            merge_blocks(i, wts, [blk], mpool, gtx, 256)
            Mt.append((bi_, kb.rec))
        kb.pipeline(Ps, Rs, Mt)
        psum_plan("stream")
        merge_blocks(i, wts, blocks[-1:], pjm, gtp)

    def phase_out(l):
        psum_plan("stream")
        wo = load_wo(l)
        if l == DEPTH - 1:
            hst.tiles = [t for t in hst.tiles if t.parts != [r_eb]]
            DMA("sp", lambda e: e.dma_start(out=fnbt[:, :], in_=fnb[:, :]), [], [fnbt])
        tl = list(hst.tiles)
        hA, hB = Pool(tl[:4]), Pool(tl[4:])
        LA = 3
        hts = {}

        def issue(tj):
            t = hA.get()
            if l == 0:
                DMA("sp", lambda e: e.dma_start(out=t[:, :], in_=hin[tj * 128:(tj + 1) * 128, :]), [], [t])
            else:
                DMA("sp", lambda e: e.dma_start(out=t[:, :], in_=hscr[tj * 128:(tj + 1) * 128, :]), [hscr_res], [t])
            hts[tj] = t

        for tj in range(min(LA, NT)):
            issue(tj)
        for ti in range(NT):
            ht = hts.pop(ti)
            for hh in range(2):
                po = pj.get()
                for kc in range(8):
                    PE(lambda e, kc=kc, po=po: e.matmul(po[:, :], mixT[:, kc, ti * 128:(ti + 1) * 128], wo[hh][:, kc, :], start=(kc == 0), stop=(kc == 7)), [mixT.p(ti), wo[hh]], [po])
                V(lambda e, po=po: e.tensor_tensor(out=ht[:, hh * 512:(hh + 1) * 512], in0=ht[:, hh * 512:(hh + 1) * 512], in1=po[:, :], op=ALU.add), [ht, po], [ht])
            if l == 0:
                DMA("sp", lambda e: e.dma_start(out=hscr[ti * 128:(ti + 1) * 128, :], in_=ht[:, :]), [ht], [hscr_res])
                norm_tile(1, ht, ti, hB)
            else:
                sq = hB.get()
                A(lambda e: e.activation(out=sq[:, :], in_=ht[:, :], func=AF.Square), [ht], [sq])
                s1 = sm1.get()
                V(lambda e: e.reduce_sum(out=s1[:, 0:1], in_=sq[:, :], axis=AX.X), [sq], [s1])
                rsqrt(s1[:, 2:3], s1[:, 0:1], 128, [s1], [s1], scale=1.0 / D)
                V(lambda e: e.scalar_tensor_tensor(out=sq[:, :], in0=ht[:, :], scalar=s1[:, 2:3], in1=fnbt[:, :], op0=ALU.mult, op1=ALU.mult), [ht, s1, fnbt], [sq])
                if ti == 0:
                    DMA("sp", lambda e: e.dma_start(out=yp[0:64, :], in_=sq[64:128, :]), [sq], [])
                elif ti == NT - 1:
                    DMA("sp", lambda e: e.dma_start(out=yp[SEQ - 64:SEQ, :], in_=sq[0:64, :]), [sq], [])
                    DMA("sp", lambda e: e.dma_start(out=ys[:, :], in_=sq[64:128, :]), [sq], [])
                else:
                    DMA("sp", lambda e: e.dma_start(out=yp[ti * 128 - 64:ti * 128 + 64, :], in_=sq[:, :]), [sq], [])
            if ti + LA < NT:
                issue(ti + LA)

    lbl = pf[1][:, P_LBL:P_LBL + 8].rearrange("p (h l) -> p h l", l=2)
    V(lambda e: e.tensor_tensor(out=lbt[:, :], in0=lbl[:, :, 1], in1=lbl[:, :, 0], op=ALU.subtract), [pf[1]], [lbt])
    A(lambda e: e.activation(out=lbt[:, :], in_=lbt[:, :], func=AF.Sigmoid), [lbt], [lbt])

    tl0 = list(hst.tiles)
    hA0, hB0 = Pool(tl0[:4]), Pool(tl0[4:])
    hts0 = {}

    def issue0(tj):
        t = hA0.get()
        DMA("sp", lambda e: e.dma_start(out=t[:, :], in_=hin[tj * 128:(tj + 1) * 128, :]), [], [t])
        hts0[tj] = t

    for tj in range(min(3, NT)):
        issue0(tj)
    for ti in range(NT):
        norm_tile(0, hts0.pop(ti), ti, hB0)
        if ti + 3 < NT:
            issue0(ti + 3)
    for l in range(DEPTH):
        if l == 0:
            V(lambda e: e.memset(omlt[:, :], 1.0), [], [omlt])
        else:
            V(lambda e: e.tensor_scalar(out=omlt[:, :], in0=lbt[:, :], scalar1=-1.0, scalar2=1.0, op0=ALU.mult, op1=ALU.add), [lbt], [omlt])
        phase_gla(l)
        phase_hgrn(l)
        phase_mlstm(l)
        phase_conv(l)
        phase_merge(l, 3)
        phase_out(l)
    kb.finish()
    es.close()
    return nc, kb


def make_consts(NT):
    c = np.zeros((128, CW), np.float32)
    c[:, K_ID:K_ID + 128] = np.eye(128, dtype=np.float32)
    s = np.arange(128)[:, None]
    t = np.arange(128)[None, :]
    same = (s // 64) == (t // 64)
    mid = (same & (s <= t)).astype(np.float32)
    first = mid.copy()
    first[16:64, :] = 0
    first[:, 16:64] = 0
    last = mid.copy()
    blk4 = ((s // 4) == (t // 4)) & (s <= t)
    last[64:, :] = 0
    last[:, 64:] = 0
    last[64:, 64:] = blk4[64:, 64:].astype(np.float32)
    c[:, K_MASK:K_MASK + 128] = first
    c[:, K_MASK + 128:K_MASK + 256] = mid
    c[:, K_MASK + 256:K_MASK + 384] = last
    r = np.ones(256, np.float32)
    r[::64] = 0
    c[:, K_R:K_R + 256] = r[None]
    rl = np.ones(128, np.float32)
    rl[0] = 0
    rl[64::4] = 0
    c[:, K_R + 256:K_R + 384] = rl[None]
    for j in range(NSEQ):
        c[64 + 4 * j:64 + 4 * j + 4, K_OH + j] = 1.0
    for h in range(4):
        c[h, K_SEL + h * 128:K_SEL + (h + 1) * 128] = 1.0
    c[:, K_ONE] = 1.0
    return c


def make_par(inp, l):
    p = np.zeros((128, PW), np.float32)
    g = lambda k: np.asarray(inp[k], np.float32)
    p[:, P_NORMW:P_NORMW + 8] = g("norm_w")[l].reshape(8, 128).T
    p[:, P_FNORM:P_FNORM + 8] = g("final_norm").reshape(8, 128).T
    p[0:64, P_NBGK:P_NBGK + 4] = -g("gla_bgk")[l].reshape(4, 64).T
    p[:, P_GNORM] = g("gla_norm")[l]
    p[:, P_HNORM] = g("hg_norm")[l]
    lbl = g("hg_lb_logits")
    p[:, P_LBL:P_LBL + 8] = lbl.reshape(2, 4, 128).transpose(2, 1, 0).reshape(128, 8)
    p[0:4, P_BI] = g("ml_bi")[l]
    p[0:4, P_NBF] = -g("ml_bf")[l]
    p[:, P_MLN:P_MLN + 4] = g("ml_norm")[l].reshape(4, 128).T
    p[:, P_CW:P_CW + 124] = g("conv_w")[l].reshape(31, 4, 128).transpose(2, 1, 0).reshape(128, 124)
    p[:, P_CB:P_CB + 4] = g("conv_b")[l].reshape(4, 128).T
    p[:, P_CG:P_CG + 4] = g("conv_ln_g")[l].reshape(4, 128).T
    p[:, P_CBT:P_CBT + 4] = g("conv_ln_b")[l].reshape(4, 128).T
    p[0:16, P_WGK:P_WGK + 256] = g("gla_wgk2")[l]
    return p


_CACHE = {}


def kernel(**inp):
    x_prompt = np.asarray(inp["x_prompt"], np.float32)
    x_sample = np.asarray(inp["x_sample"], np.float32)
    B, SEQ, _ = x_prompt.shape
    ncores = B
    NT = (SEQ + 64) // 128 + 1
    if NT not in _CACHE:
        _CACHE[NT] = build(NT)
    nc, kb = _CACHE[NT]
    NCOL = NT * 128
    cst = make_consts(NT)
    par = np.stack([make_par(inp, l) for l in range(DEPTH)])
    fnb = np.ascontiguousarray(np.broadcast_to(np.asarray(inp["final_norm"], np.float32)[None, :], (128, D)))
    meta = np.asarray(inp["meta_tokens"], np.float32)
    f = lambda k: np.ascontiguousarray(np.asarray(inp[k], np.float32))
    w_in, w_br, w_o = f("w_in"), f("w_branch"), f("w_out")
    sg, sh, sC, sn, sm, scv = f("state_gla"), f("state_hgrn"), f("state_mlstm_C"), f("state_mlstm_n"), f("state_mlstm_m"), f("state_conv")
    in_maps = []
    for c in range(ncores):
        hin = np.zeros((NCOL, D), np.float32)
        hin[0:16] = meta
        hin[64:64 + SEQ] = x_prompt[c]
        hin[NCOL - 64:] = x_sample[NSEQ * c:NSEQ * (c + 1)].reshape(64, D)
        sl = slice(NSEQ * c, NSEQ * (c + 1))
        in_maps.append({
            "hin": hin, "w_in": w_in, "w_branch": w_br, "w_out": w_o, "par": par, "cst": cst, "fnb": fnb,
            "st_gla": np.ascontiguousarray(sg[:, sl]), "st_hg": np.ascontiguousarray(sh[:, sl]),
            "st_C": np.ascontiguousarray(sC[:, sl]), "st_n": np.ascontiguousarray(sn[:, sl].reshape(DEPTH, NSEQ * 4, 128)),
            "st_m": np.ascontiguousarray(sm[:, sl].reshape(DEPTH, NSEQ * 4)), "st_cv": np.ascontiguousarray(scv[:, sl]),
        })
    res = run_bass_kernel_spmd(nc, in_maps, core_ids=list(range(ncores))).results
    cat = lambda k, ax: np.concatenate([np.asarray(r[k], np.float32) for r in res], axis=ax)
    stk = lambda k: np.stack([np.asarray(r[k], np.float32) for r in res], axis=1)
    y_prompt = np.stack([np.asarray(r["yp"], np.float32) for r in res], axis=0)
    y_sample = cat("ys", 0).reshape(x_sample.shape)
    nS = NSEQ * ncores
    return (y_prompt, y_sample,
            stk("p_gla"), stk("p_hg"), stk("p_C"), stk("p_n"), stk("p_m"), stk("p_cv"),
            cat("s_gla", 1), cat("s_hg", 1), cat("s_C", 1), cat("s_n", 1).reshape(DEPTH, nS, 4, 128),
            cat("s_m", 1), cat("s_cv", 1))

_LINE_COST = {634: 1.065, 671: 1.008, 673: 1.175, 573: 0.538, 574: 0.479, 676: 0.753, 678: 0.126, 680: 1.225, 660: 0.143, 860: 0.454, 864: 0.211, 865: 0.433, 866: 1.1, 869: 0.632, 872: 1.009, 873: 0.899, 878: 0.34, 881: 0.361, 665: 0.241, 784: 0.124, 785: 0.169, 809: 0.67, 801: 0.676, 815: 0.362, 694: 0.067, 699: 0.077, 696: 0.531, 703: 0.662, 708: 0.095, 736: 0.053, 748: 0.104, 757: 0.387, 759: 0.823, 821: 0.49, 826: 0.691, 822: 0.493, 827: 0.752, 642: 0.908, 779: 0.165, 780: 0.693, 795: 0.295, 797: 0.215, 1332: 0.153, 1328: 0.382, 1336: 0.389, 745: 0.342, 733: 0.859, 763: 0.389, 945: 0.328, 946: 0.262, 947: 0.383, 948: 1.009, 950: 0.636, 951: 0.909, 952: 0.899, 954: 0.45, 959: 0.373, 960: 1.166, 1338: 0.436, 1339: 0.728, 1044: 0.453, 1047: 0.454, 1048: 0.399, 1049: 0.274, 1050: 0.57, 1051: 0.318, 1084: 0.531, 1061: 0.566, 1062: 0.141, 1087: 0.453, 1085: 0.324, 1088: 0.348, 1093: 0.348, 1096: 0.363, 751: 0.072, 1110: 0.485, 1111: 0.425, 1112: 0.629, 1113: 0.63, 1116: 0.692, 1117: 0.693, 1119: 1.873, 1121: 0.675, 1123: 0.629, 1124: 0.418, 1128: 0.69, 1129: 0.693, 1130: 0.692, 1120: 0.372, 1067: 0.229, 1073: 0.473, 1074: 0.174, 1079: 0.254, 1022: 0.687, 1023: 0.427, 1026: 0.692, 1027: 0.165, 1258: 4.293, 1181: 0.259, 1182: 0.662, 1211: 0.372, 1213: 0.373, 1225: 0.18, 1235: 0.108, 1238: 0.274, 1239: 0.421, 1241: 0.455, 1215: 0.132, 1216: 0.131, 1248: 0.328, 1250: 0.284, 1251: 0.196, 1278: 0.112, 1285: 0.258, 1296: 0.231, 1297: 1.162, 1301: 0.996, 1303: 0.21, 1307: 1.166, 1309: 0.405, 1311: 1.009, 1312: 1.166, 651: 1.067, 1395: 0.294, 1396: 0.673, 1402: 1.057, 1404: 1.22, 1406: 1.284}
```
